# Optimizing a Trainium2 kernel written in Bass

```python
import math
import jax, jax.numpy as jnp
from jax import lax
import numpy as np

D_MODEL = 1024
BATCH = 4
SEQ = 8192
DEPTH = 4
DEC_BATCH = 16
DEC_SEQ = 32
PAST_LEN = 2048

CHUNK = 64
N_A_LAYERS = DEPTH // 2
N_B_LAYERS = DEPTH - N_A_LAYERS
M_HEADS = 8
M_DV = D_MODEL // M_HEADS
M_DK = M_DV // 2
A_SPLITS = (M_HEADS * M_DK, 2 * M_HEADS * M_DK, 2 * M_HEADS * M_DK + M_HEADS * M_DV,
            2 * M_HEADS * M_DK + 2 * M_HEADS * M_DV, 2 * M_HEADS * M_DK + 2 * M_HEADS * M_DV + M_HEADS)
A_PROJ = 2 * M_HEADS * M_DK + 2 * M_HEADS * M_DV + 2 * M_HEADS
WINDOW = 128
WIN_CHUNKS = WINDOW // CHUNK
N_Q_HEADS = 16
N_KV_HEADS = 2
HEAD_DIM = 64
GROUP = N_Q_HEADS // N_KV_HEADS
D_ATT = N_Q_HEADS * HEAD_DIM
D_FF = ((8 * D_MODEL // 3 + 255) // 256) * 256
DEEPNORM_ALPHA = (2 * DEPTH) ** 0.25
DEEPNORM_BETA = (8 * DEPTH) ** -0.25
LN_EPS = 1e-5
HEAD_NORM_EPS = 1e-6

kernel_name = "yoco_mlstm_swa_sink_alibi_deepnorm_step"

F32 = jnp.float32


def layer_norm(x, g, b):
    xf = x.astype(F32)
    mu = jnp.mean(xf, -1, keepdims=True)
    var = jnp.mean(jnp.square(xf - mu), -1, keepdims=True)
    return ((xf - mu) * lax.rsqrt(var + LN_EPS) * g.astype(F32) + b.astype(F32)).astype(x.dtype)


def swiglu(x, w_gu, w_down):
    g, u = jnp.split(x @ w_gu, 2, axis=-1)
    return (jax.nn.silu(g) * u) @ w_down


def mlstm_chunk(q, k, v, ig, lf, C0, n0, m0):
    L = q.shape[2]
    b = jnp.cumsum(lf, axis=-1)
    causal = jnp.tril(jnp.ones((L, L), bool))
    log_d = jnp.where(causal, b[..., :, None] - b[..., None, :] + ig[..., None, :], -jnp.inf)
    inter = b + m0[..., None]
    m = jnp.maximum(inter, jnp.max(log_d, -1))
    d = jnp.exp(log_d - m[..., None])
    w_inter = jnp.exp(inter - m)
    qk = jnp.einsum('bhtd,bhsd->bhts', q, k) * d
    num = w_inter[..., None] * jnp.einsum('bhtd,bhde->bhte', q, C0) + jnp.einsum('bhts,bhse->bhte', qk, v)
    den = w_inter * jnp.einsum('bhtd,bhd->bht', q, n0) + jnp.sum(qk, -1)
    h = num / jnp.maximum(jnp.abs(den), jnp.exp(-m))[..., None]
    m_new = m[..., -1]
    w_end = jnp.exp(b[..., -1:] - b + ig - m_new[..., None])
    decay = jnp.exp(inter[..., -1] - m_new)
    C_new = decay[..., None, None] * C0 + jnp.einsum('bhs,bhsd,bhse->bhde', w_end, k, v)
    n_new = decay[..., None] * n0 + jnp.einsum('bhs,bhsd->bhd', w_end, k)
    return h, C_new, n_new, m_new


def mlstm_mixer(x, w_in, b_gate, g_norm, w_out, C0, n0, m0):
    B, T, _ = x.shape
    H = M_HEADS
    q, k, v, o, ig, fg = jnp.split(x @ w_in, A_SPLITS, axis=-1)
    heads = lambda a, dd: a.reshape(B, T, H, dd).transpose(0, 2, 1, 3).astype(F32)
    q = heads(q, M_DK)
    k = heads(k, M_DK) * (M_DK ** -0.5)
    v = heads(v, M_DV)
    gates = jnp.concatenate([ig, fg], -1).astype(F32) + b_gate.astype(F32)
    ig = gates[..., :H].transpose(0, 2, 1)
    lf = jax.nn.log_sigmoid(gates[..., H:]).transpose(0, 2, 1)
    L = min(CHUNK, T)
    nc = T // L

    def to_chunks(a):
        return jnp.moveaxis(a.reshape(a.shape[:2] + (nc, L) + a.shape[3:]), 2, 0)

    def step(carry, xs):
        C, n, m = carry
        h, C, n, m = mlstm_chunk(*xs, C, n, m)
        return (C, n, m), h

    (C, n, m), h = lax.scan(step, (C0.astype(F32), n0.astype(F32), m0.astype(F32)),
                            (to_chunks(q), to_chunks(k), to_chunks(v), to_chunks(ig), to_chunks(lf)))
    h = jnp.moveaxis(h, 0, 2).reshape(B, H, T, M_DV).transpose(0, 2, 1, 3)
    h = h * lax.rsqrt(jnp.mean(h * h, -1, keepdims=True) + HEAD_NORM_EPS)
    h = h.reshape(B, T, H * M_DV) * g_norm.astype(F32) * jax.nn.sigmoid(o.astype(F32))
    return h.astype(x.dtype) @ w_out, C, n, m


def alibi_slopes():
    return jnp.exp2(-8.0 * jnp.arange(1, N_Q_HEADS + 1, dtype=F32) / N_Q_HEADS)


def sink_attention(q, k, v, sinks, bias):
    s = jnp.einsum('bcqhgd,bckhd->bchgqk', q, k).astype(F32) * (HEAD_DIM ** -0.5) + bias
    sk = sinks.astype(F32).reshape(N_KV_HEADS, GROUP)[:, :, None, None]
    mx = jnp.maximum(jnp.max(s, -1, keepdims=True), sk)
    p = jnp.exp(s - mx)
    p = p / (jnp.sum(p, -1, keepdims=True) + jnp.exp(sk - mx))
    return jnp.einsum('bchgqk,bckhd->bcqhgd', p.astype(v.dtype), v)


def swa_prompt(x, k, v, w_q, sinks, w_out, slopes):
    B, T, _ = x.shape
    nc = T // CHUNK
    wk = WINDOW + CHUNK
    q = (x @ w_q).reshape(B, nc, CHUNK, N_KV_HEADS, GROUP, HEAD_DIM)
    pad = ((0, 0), (WINDOW, 0), (0, 0), (0, 0))
    kc = jnp.pad(k, pad).reshape(B, nc + WIN_CHUNKS, CHUNK, N_KV_HEADS, HEAD_DIM)
    vc = jnp.pad(v, pad).reshape(B, nc + WIN_CHUNKS, CHUNK, N_KV_HEADS, HEAD_DIM)
    kw = jnp.concatenate([kc[:, j:j + nc] for j in range(WIN_CHUNKS + 1)], axis=2)
    vw = jnp.concatenate([vc[:, j:j + nc] for j in range(WIN_CHUNKS + 1)], axis=2)
    dist = jnp.arange(CHUNK)[:, None] - jnp.arange(wk)[None, :] + WINDOW
    alibi = -slopes.reshape(N_KV_HEADS, GROUP, 1, 1) * jnp.abs(dist).astype(F32)
    key_pos = jnp.arange(nc)[:, None] * CHUNK - WINDOW + jnp.arange(wk)[None, :]
    bias = jnp.where((key_pos >= 0)[:, None, None, None, :], alibi[None], -jnp.inf)[None]
    o = sink_attention(q, kw, vw, sinks, bias)
    return o.reshape(B, T, D_ATT) @ w_out


def swa_sample(x, k_cache, v_cache, k, v, w_q, sinks, w_out, slopes):
    B, T, _ = x.shape
    W = k_cache.shape[1]
    q = (x @ w_q).reshape(B, 1, T, N_KV_HEADS, GROUP, HEAD_DIM)
    kw = jnp.concatenate([k_cache.astype(k.dtype), k], axis=1)[:, None]
    vw = jnp.concatenate([v_cache.astype(v.dtype), v], axis=1)[:, None]
    dist = jnp.arange(T)[:, None] - jnp.arange(W + T)[None, :] + W
    bias = (-slopes.reshape(N_KV_HEADS, GROUP, 1, 1) * jnp.abs(dist).astype(F32))[None, None]
    o = sink_attention(q, kw, vw, sinks, bias)
    return o.reshape(B, T, D_ATT) @ w_out


def trunk(x, C0s, n0s, m0s, k_cache, v_cache, w_in_a, b_gate_a, g_norm_a, w_out_a, w_kv,
          w_q_b, sinks_b, w_out_b, w_gu, w_down, ln_g, ln_b):
    slopes = alibi_slopes()
    Cs, ns, ms = [], [], []
    k_sh = v_sh = None
    for layer in range(DEPTH):
        if layer < N_A_LAYERS:
            mix, C, n, m = mlstm_mixer(x, w_in_a[layer], b_gate_a[layer], g_norm_a[layer], w_out_a[layer],
                                       C0s[layer], n0s[layer], m0s[layer])
            Cs.append(C)
            ns.append(n)
            ms.append(m)
        else:
            if layer == N_A_LAYERS:
                B, T, _ = x.shape
                kv = (x @ w_kv).reshape(B, T, 2, N_KV_HEADS, HEAD_DIM)
                k_sh, v_sh = kv[:, :, 0], kv[:, :, 1]
            j = layer - N_A_LAYERS
            if k_cache is None:
                mix = swa_prompt(x, k_sh, v_sh, w_q_b[j], sinks_b[j], w_out_b[j], slopes)
            else:
                mix = swa_sample(x, k_cache, v_cache, k_sh, v_sh, w_q_b[j], sinks_b[j], w_out_b[j], slopes)
        x = layer_norm(DEEPNORM_ALPHA * x + mix, ln_g[layer, 0], ln_b[layer, 0])
        x = layer_norm(DEEPNORM_ALPHA * x + swiglu(x, w_gu[layer], w_down[layer]), ln_g[layer, 1], ln_b[layer, 1])
    return x, jnp.stack(Cs), jnp.stack(ns), jnp.stack(ms), k_sh, v_sh


def setup_inputs(seed: int = 0) -> dict:
    key = jax.random.key(seed)
    ks = jax.random.split(key, 24)
    nrm = lambda k, shape, scale: jax.random.normal(k, shape, F32) * scale
    win_rows = min(WINDOW, PAST_LEN)
    b_i = nrm(ks[9], (N_A_LAYERS, M_HEADS), 0.1)
    b_f = jnp.linspace(3.0, 6.0, M_HEADS, dtype=F32)[None, :] + nrm(ks[10], (N_A_LAYERS, M_HEADS), 0.1)
    return {
        "x_prompt": nrm(ks[0], (BATCH, SEQ, D_MODEL), 1.0),
        "x_sample": nrm(ks[1], (DEC_BATCH, DEC_SEQ, D_MODEL), 1.0),
        "state_C": nrm(ks[2], (N_A_LAYERS, DEC_BATCH, M_HEADS, M_DK, M_DV), 0.1),
        "state_n": nrm(ks[3], (N_A_LAYERS, DEC_BATCH, M_HEADS, M_DK), 0.3),
        "state_m": nrm(ks[4], (N_A_LAYERS, DEC_BATCH, M_HEADS), 1.0),
        "cache_k": nrm(ks[5], (DEC_BATCH, win_rows, N_KV_HEADS, HEAD_DIM), 1.0),
        "cache_v": nrm(ks[6], (DEC_BATCH, win_rows, N_KV_HEADS, HEAD_DIM), 1.0),
        "w_in_a": nrm(ks[7], (N_A_LAYERS, D_MODEL, A_PROJ), D_MODEL ** -0.5),
        "b_gate_a": jnp.concatenate([b_i, b_f], axis=-1),
        "g_norm_a": 1.0 + nrm(ks[11], (N_A_LAYERS, M_HEADS * M_DV), 0.02),
        "w_out_a": nrm(ks[12], (N_A_LAYERS, M_HEADS * M_DV, D_MODEL), DEEPNORM_BETA * (M_HEADS * M_DV) ** -0.5),
        "w_kv": nrm(ks[13], (D_MODEL, 2 * N_KV_HEADS * HEAD_DIM), D_MODEL ** -0.5),
        "w_q_b": nrm(ks[14], (N_B_LAYERS, D_MODEL, D_ATT), D_MODEL ** -0.5),
        "sinks_b": nrm(ks[15], (N_B_LAYERS, N_Q_HEADS), 0.5),
        "w_out_b": nrm(ks[16], (N_B_LAYERS, D_ATT, D_MODEL), DEEPNORM_BETA * D_ATT ** -0.5),
        "w_gu": nrm(ks[17], (DEPTH, D_MODEL, 2 * D_FF), D_MODEL ** -0.5),
        "w_down": nrm(ks[18], (DEPTH, D_FF, D_MODEL), DEEPNORM_BETA * D_FF ** -0.5),
        "ln_g": 1.0 + nrm(ks[19], (DEPTH, 2, D_MODEL), 0.02),
        "ln_b": nrm(ks[20], (DEPTH, 2, D_MODEL), 0.02),
    }


def reference(x_prompt, x_sample, state_C, state_n, state_m, cache_k, cache_v, w_in_a, b_gate_a, g_norm_a,
              w_out_a, w_kv, w_q_b, sinks_b, w_out_b, w_gu, w_down, ln_g, ln_b):
    B = x_prompt.shape[0]
    C0 = jnp.zeros((N_A_LAYERS, B, M_HEADS, M_DK, M_DV), F32)
    n0 = jnp.zeros((N_A_LAYERS, B, M_HEADS, M_DK), F32)
    m0 = jnp.zeros((N_A_LAYERS, B, M_HEADS), F32)
    y_prompt, p_C, p_n, p_m, p_k, p_v = trunk(x_prompt, C0, n0, m0, None, None, w_in_a, b_gate_a, g_norm_a,
                                              w_out_a, w_kv, w_q_b, sinks_b, w_out_b, w_gu, w_down, ln_g, ln_b)
    y_sample, s_C, s_n, s_m, s_k, s_v = trunk(x_sample, state_C, state_n, state_m, cache_k, cache_v, w_in_a,
                                              b_gate_a, g_norm_a, w_out_a, w_kv, w_q_b, sinks_b, w_out_b,
                                              w_gu, w_down, ln_g, ln_b)
    rows = min(WINDOW, x_prompt.shape[1])
    return (y_prompt, y_sample, p_C, p_n, p_m, p_k[:, -rows:], p_v[:, -rows:], s_C, s_n, s_m, s_k, s_v)
```

```python
import math
from contextlib import ExitStack
import numpy as np
import concourse.bass as bass
import concourse.mybir as mybir
from concourse.bass_utils import run_bass_kernel_spmd

F32 = mybir.dt.float32
BF16 = mybir.dt.bfloat16
AF = mybir.ActivationFunctionType
ALU = mybir.AluOpType
AX = mybir.AxisListType

PE, ACT, DVE, POOL, SP = "pe", "act", "dve", "pool", "sp"
ENGS = (PE, ACT, DVE, POOL, SP)
DMA_SEMS = {SP: 12, POOL: 10, ACT: 6}

D = 1024
KC = 8
DFF = 2816
NF = 22
ALPHA = 8.0 ** 0.25
LNENG = "dve"
import os as _os
YQ = _os.environ.get("K_YQ", "act")
BCAST = bool(int(_os.environ.get("K_BC", "1")))
UVENG = _os.environ.get("K_UVENG", "pool")
LN_EPS = 1e-5
HN_EPS = 1e-6
NEG = -30000.0


class Buf:
    __slots__ = ("w", "r", "name", "excl")

    def __init__(self, name="", excl=False):
        self.w = {}
        self.r = {}
        self.name = name
        self.excl = excl


class Ins:
    __slots__ = ("fn", "waits", "dma")

    def __init__(self, fn):
        self.fn = fn
        self.waits = []
        self.dma = None


class Prog:
    def __init__(self):
        self.streams = {e: [] for e in ENGS}
        self.waited = {e: {} for e in ENGS}
        self.needed = {e: set() for e in ENGS}
        self.dma_next = {q: 0 for q in DMA_SEMS}
        self.dma_val = {q: [0] * n for q, n in DMA_SEMS.items()}
        self.final_tokens = []

    def _deps(self, me, reads, writes):
        deps = {}
        for b in reads:
            for k, v in b.w.items():
                if deps.get(k, -1) < v:
                    deps[k] = v
            if b.excl:
                for k, v in b.r.items():
                    if k != me and deps.get(k, -1) < v:
                        deps[k] = v
        for b in writes:
            for k, v in b.w.items():
                if (k != me or me != PE) and deps.get(k, -1) < v:
                    deps[k] = v
            for k, v in b.r.items():
                if (k != me or me != PE) and deps.get(k, -1) < v:
                    deps[k] = v
        return deps

    def _commit(self, eng, ins, deps):
        wd = self.waited[eng]
        for k, v in deps.items():
            if wd.get(k, -1) < v:
                wd[k] = v
                ins.waits.append((k, v))
                if not isinstance(k, tuple):
                    self.needed[k].add(v)

    def op(self, eng, fn, reads=(), writes=()):
        ins = Ins(fn)
        n = len(self.streams[eng])
        self._commit(eng, ins, self._deps(eng, reads, writes))
        self.streams[eng].append(ins)
        for b in reads:
            b.r[eng] = n
        for b in writes:
            b.w[eng] = n
        return ins

    def dma(self, q, fn, reads=(), writes=(), final=False):
        ins = Ins(fn)
        si = self.dma_next[q]
        self.dma_next[q] = (si + 1) % DMA_SEMS[q]
        key = ("dma", q, si)
        deps = self._deps(key, reads, writes)
        pv = self.dma_val[q][si]
        if pv > 0 and deps.get(key, -1) < pv:
            deps[key] = pv
        self._commit(q, ins, deps)
        val = pv + 16
        self.dma_val[q][si] = val
        ins.dma = (key, val)
        self.streams[q].append(ins)
        for b in reads:
            b.r[key] = val
        for b in writes:
            b.w[key] = val
        if final:
            self.final_tokens.append((key, val))
        return ins

    def emit(self, block, sems, dma_sems):
        rank = {e: {s: i + 1 for i, s in enumerate(sorted(self.needed[e]))} for e in ENGS}
        handles = {PE: "tensor", ACT: "scalar", DVE: "vector", POOL: "gpsimd", SP: "sync"}
        prog = self

        def make(e):
            def body(eng):
                rk = rank[e]
                for n, ins in enumerate(prog.streams[e]):
                    for k, v in ins.waits:
                        if isinstance(k, tuple):
                            eng.wait_ge(dma_sems[k], v)
                        else:
                            eng.wait_ge(sems[k], rank[k][v])
                    inst = ins.fn(eng)
                    if ins.dma is not None:
                        inst.then_inc(dma_sems[ins.dma[0]], 16)
                    elif n in rk:
                        inst.then_inc(sems[e], 1)
                if e == SP:
                    for k, v in prog.final_tokens:
                        eng.wait_ge(dma_sems[k], v)
            return body
        for e in ENGS:
            getattr(block, handles[e])(make(e))


def _consts():
    c = {}
    c["ident"] = np.eye(128, dtype=np.float32)
    s = np.arange(128)
    c["tri"] = (s[:, None] <= s[None, :]).astype(np.float32)
    sel = np.zeros((8, 128), np.float32)
    for k in range(8):
        sel[k, (k % 2) * 64:(k % 2) * 64 + 64] = 1.0
    c["sel"] = sel
    pm = np.zeros((8, 4), np.float32)
    for k in range(8):
        pm[k, k // 2] = 1.0
    c["pairmask"] = pm
    c["ones8"] = np.ones((8, 128), np.float32)
    valid = (s < 32).astype(np.float32)
    c["valid2"] = np.stack([valid, (valid - 1.0) * 30000.0], 1).astype(np.float32)
    slopes = np.exp2(-8.0 * np.arange(1, 17, dtype=np.float64) / 16)
    bt = np.zeros((8, 128, 512), np.float32)
    k = np.arange(128)[:, None]
    q = np.arange(128)[None, :]
    for kb in range(2):
        kpos = k - 128 if kb == 0 else k
        dist = np.abs(q - kpos).astype(np.float64)
        cq = q // 64
        ck = (k // 64) - 2 if kb == 0 else (k // 64)
        vis = (ck >= cq - 2) & (ck <= cq)
        for kv in range(2):
            for p in range(2):
                for hh in range(4):
                    head = kv * 8 + 2 * hh + p
                    t = np.where(vis, -slopes[head] * dist, NEG)
                    bt[kb * 4 + kv * 2 + p][:, hh * 128:(hh + 1) * 128] = t.astype(np.float32)
    c["biasT"] = bt
    return c


CONST_SHAPES = {"ident": [128, 128], "tri": [128, 128], "sel": [8, 128], "pairmask": [8, 4],
                "ones8": [8, 128], "valid2": [128, 2], "biasT": [8, 128, 512]}


def build(NT_P, n_samp=2, NPRE=0):
    nc = bass.Bass("TRN2", target_bir_lowering=False)
    P = Prog()

    def din(name, shape, dt=F32):
        return nc.dram_tensor(name, list(shape), dt, kind="ExternalInput").ap()

    def dout(name, shape, dt=F32):
        return nc.dram_tensor(name, list(shape), dt, kind="ExternalOutput").ap()

    def dscr(name, shape, dt=BF16):
        return nc.dram_tensor(name, list(shape), dt, kind="Internal").ap()

    NTOK = NT_P * 512
    NOUT = (NT_P - NPRE) * 512
    keep_in = None
    xin = din("xin", [NTOK, D])
    xs_in = din("xs", [n_samp, 32, D])
    stC_in = din("stC", [2, n_samp, 4, 128, 128])
    stn_in = din("stn", [2, n_samp, 4, 128])
    stm_in = din("stm", [2, n_samp, 8])
    ck_in = din("ck", [n_samp, 128, 128])
    cv_in = din("cv", [n_samp, 128, 128])
    w_in_a = din("w_in_a", [2, D, 3088])
    w_out_a = din("w_out_a", [2, D, D])
    w_kv = din("w_kv", [D, 256])
    w_q_b = din("w_q_b", [2, D, D])
    w_out_b = din("w_out_b", [2, D, D])
    w_gu = din("w_gu", [4, D, 2 * DFF])
    w_down = din("w_down", [4, DFF, D])
    bgate_in = din("bgate", [2, 128, 16])
    gnT_in = din("gnT", [2, 128, 8])
    sinks_in = din("sinksrep", [2, 128, 16])
    lng_in = din("lng", [4, 2, 128, 8])
    lnb_in = din("lnb", [4, 2, 128, 8])
    cin = {k: din("c_" + k, shp) for k, shp in CONST_SHAPES.items()}

    y_out = dout("y", [NOUT, D])
    keep_in = din("keep", [128, 1])
    ys_out = dout("ys", [n_samp, 32, D])
    pC_out = dout("pC", [2, 4, 128, 128])
    pn_out = dout("pn", [2, 4, 128])
    pm_out = dout("pm", [2, 8])
    pk_out = dout("pk", [128, 128])
    pv_out = dout("pv", [128, 128])
    sC_out = dout("sC", [2, n_samp, 4, 128, 128])
    sn_out = dout("sn", [2, n_samp, 4, 128])
    sm_out = dout("sm", [2, n_samp, 8])
    sk_out = dout("sk", [n_samp, 32, 128])
    sv_out = dout("sv", [n_samp, 32, 128])

    wqk_b = dscr("wqk_b", [2, 2, 128, 8, 512])
    wtok_b = dscr("wtok_b", [2, 5, 128, 8, 512])
    wif_b = dscr("wif_b", [2, 128, 8, 16])
    wouta_b = dscr("wouta_b", [2, 2, 128, 8, 512])
    wkv_b = dscr("wkv_b", [128, 8, 256])
    wkd_b = dscr("wkd_b", [128, 8, 256])
    wq_b = dscr("wq_b", [2, 2, 128, 8, 512])
    woutb_b = dscr("woutb_b", [2, 2, 128, 8, 512])
    wgu_b = dscr("wgu_b", [4, 11, 128, 8, 512])
    wd_b = dscr("wd_b", [4, 6, 128, 8, 512])

    es = ExitStack()
    with es:
        def sb(name, shape, dt):
            return es.enter_context(nc.sbuf_tensor("sb_" + name, list(shape), dt))

        x32T = sb("x32T", [128, 8, 512], F32)
        xT = sb("xT", [128, 8, 512], BF16)
        NSLOT = 4
        wslots = [sb("wslot%d" % i, [128, 4096], BF16) for i in range(NSLOT)]
        wslotB = [Buf("wslot%d" % i) for i in range(NSLOT)]
        qkT = sb("qkT", [128, 8, 512], BF16)
        hT = sb("hT", [128, 8, 512], BF16)
        hffT = sb("hffT", [128, 1 if _os.environ.get("K_SHRINK") else NF, 512], BF16)
        ktok = [sb("ktok%d" % i, [128, 512], BF16) for i in range(4)]
        vraw = [sb("vraw%d" % i, [128, 1024], BF16) for i in range(4)]
        uv = [sb("uv%d" % i, [128, 8, 130], BF16) for i in range(4)]
        og = [sb("og%d" % i, [128, 1024], BF16) for i in range(4)]
        hn = [sb("hn%d" % i, [128, 1024], BF16) for i in range(4)]
        sq = sb("sq", [128, 8, 128], F32)
        f512 = [sb("f512_%d" % i, [128, 512], F32) for i in range(4)]
        b512 = [sb("b512_%d" % i, [128, 512], BF16) for i in range(4)]
        io = [sb("io%d" % i, [128, 1024], F32) for i in range(2)]
        PTST = sb("PTST", [128, 8, 512], BF16)
        PT = [PTST[:, i, :] for i in range(8)]
        ST = [PTST[:, 2 * i:2 * i + 2, :].rearrange("p a (h n) -> p (a h) n", h=4) for i in range(4)]
        kTd = sb("kTd", [128, 2, 5, 128], BF16)
        Vaug = sb("Vaug", [128, 5, 2, 66], BF16)
        kTd_s = sb("kTd_s", [128, 2, 4, 128], BF16)
        Vaug_s = sb("Vaug_s", [128, 4, 2, 66], BF16)
        kvtok = sb("kvtok", [128, 256], F32)
        biasT = sb("biasT", [128, 1 if _os.environ.get("K_SHRINK") else 8, 512], F32)
        NST = 1 + n_samp
        Cst = [[sb("Cst%d_%d" % (l, i), [128, 4, 129], F32) for i in range(NST)] for l in range(2)]
        mst = [[sb("mst%d_%d" % (l, i), [8, 1], F32) for i in range(NST)] for l in range(2)]
        C0b = [sb("C0b%d" % i, [128, 4, 130], BF16) for i in range(2)]
        NR = 4
        gat = [sb("gat%d" % i, [128, 16], F32) for i in range(NR)]
        e1 = [sb("e1_%d" % i, [128, 8], F32) for i in range(NR)]
        spt = [sb("sp%d" % i, [128, 8], F32) for i in range(NR)]
        an = [sb("an%d" % i, [128, 16], F32) for i in range(NR)]
        ul = [sb("ul%d" % i, [128, 16], F32) for i in range(NR)]
        tm16 = [sb("tm16_%d" % i, [128, 16], F32) for i in range(NR)]
        sm8 = [sb("sm8_%d" % i, [8, 16], F32) for i in range(NR)]
        dgG = [sb("dgG%d" % i, [8, 16], F32) for i in range(NR)]
        nrm = [sb("nrm%d" % i, [128, 48], F32) for i in range(NR)]
        dsbt = [sb("dsb%d" % i, [128, 4], F32) for i in range(NR)]
        nrm2 = [sb("nrm2_%d" % i, [128, 48], F32) for i in range(NR)]
        lnst = sb("lnst", [128, 8], F32)
        ident = sb("ident", [128, 128], F32)
        identb = sb("identb", [128, 128], BF16)
        tri = sb("tri", [128, 128], F32)
        trib = sb("trib", [128, 128], BF16)
        onesb = sb("onesb", [128, 128], BF16)
        sel = sb("sel", [8, 128], F32)
        pairmask = sb("pairmask", [8, 4], F32)
        ones8 = sb("ones8", [8, 128], F32)
        valid2 = sb("valid2", [128, 2], F32)
        bgate = sb("bgate", [128, 2, 16], F32)
        gnT = sb("gnT", [128, 2, 8], F32)
        esink = sb("esink", [128, 2, 16], F32)
        lng = sb("lng", [128, 8, 8], F32)
        lnb = sb("lnb", [128, 8, 8], F32)
        keep = sb("keep", [128, 1], F32)

        psum = es.enter_context(nc.psum_tensor("psum", [128, 8, 512], F32))
        sems = {e: es.enter_context(nc.semaphore("s_" + e)) for e in ENGS}
        dma_sems = {}
        for q, n in DMA_SEMS.items():
            for i in range(n):
                dma_sems[("dma", q, i)] = es.enter_context(nc.semaphore("d_%s%d" % (q, i)))
        block = es.enter_context(nc.Block())

        bankB = [Buf("bank%d" % i, excl=True) for i in range(8)]
        bank_state = {"next": 0, "pinned": set()}

        def bank():
            assert len(bank_state["pinned"]) < 8
            while True:
                b = bank_state["next"]
                bank_state["next"] = (b + 1) % 8
                if b not in bank_state["pinned"]:
                    return b

        def pin(b):
            bank_state["pinned"].add(b)

        def unpin(b):
            bank_state["pinned"].discard(b)

        class Ring:
            def __init__(self, tiles, name):
                self.tiles = tiles
                self.bufs = [Buf("%s%d" % (name, i)) for i in range(len(tiles))]
                self.i = 0

            def next(self):
                i = self.i
                self.i = (i + 1) % len(self.tiles)
                return self.tiles[i], self.bufs[i]

        R = {}
        for nm, tl in [("ktok", ktok), ("vraw", vraw), ("uv", uv), ("og", og), ("hn", hn),
                       ("f512", f512), ("b512", b512), ("io", io), ("C0b", C0b), ("gat", gat), ("e1", e1),
                       ("sp", spt), ("an", an), ("ul", ul), ("tm16", tm16), ("sm8", sm8), ("dgG", dgG),
                       ("nrm", nrm), ("PT", PT), ("dsb", dsbt), ("nrm2", nrm2)]:
            R[nm] = Ring(tl, nm)
        sqB = Buf("sq")

        class STRing:
            def __init__(self):
                self.i = 0

            def next(self):
                i = self.i
                self.i = (i + 1) % 4
                return ST[i], [R["PT"].bufs[2 * i], R["PT"].bufs[2 * i + 1]]
        STring = STRing()
        x32B = [Buf("x32_%d" % m) for m in range(8)]
        xTB = [Buf("xT_%d" % m) for m in range(8)]
        qkB = [Buf("qk_%d" % m) for m in range(8)]
        hTB = [Buf("hT_%d" % s) for s in range(4)]
        hffB = [Buf("hff_%d" % f) for f in range(NF)]
        kTdB = Buf("kTd")
        VaugB = Buf("Vaug")
        kTdsB = Buf("kTd_s")
        VaugsB = Buf("Vaug_s")
        kvtokB = Buf("kvtok")
        biasB = Buf("biasT")
        constB = Buf("const")
        CstB = [[[Buf("Cst%d" % q) for q in range(8)] for _ in range(NST)] for _ in range(2)]
        mstB = [[Buf("mst") for _ in range(NST)] for _ in range(2)]
        lnstB = Buf("lnst")
        wscrB = {}

        wstate = {"i": 0}

        def wload(src, view, name):
            i = wstate["i"]
            wstate["i"] = (i + 1) % NSLOT
            dst = view(wslots[i])
            P.dma(SP, lambda e, dst=dst, src=src: e.dma_start(out=dst, in_=src),
                  reads=[wscrB[name]],
                  writes=[wslotB[i]])
            return dst, wslotB[i]

        def v8(n):
            return lambda t: t[:, 0:8 * n].rearrange("p (c n) -> p c n", c=8)

        def mm(out, lhsT, rhs, start, stop, reads, writes):
            P.op(PE, lambda e: e.matmul(out, lhsT=lhsT, rhs=rhs, start=start, stop=stop),
                 reads=reads, writes=writes)

        def act(out, in_, func, reads, writes, **kw):
            P.op(ACT, lambda e: e.activation(out=out, in_=in_, func=func, **kw), reads=reads, writes=writes)

        def tt(eng, out, in0, in1, op, reads, writes):
            P.op(eng, lambda e: e.tensor_tensor(out=out, in0=in0, in1=in1, op=op), reads=reads, writes=writes)

        def ts(eng, out, in0, s1, s2, op0, op1, reads, writes):
            if op1 is None:
                P.op(eng, lambda e: e.tensor_scalar(out=out, in0=in0, scalar1=s1, scalar2=None, op0=op0),
                     reads=reads, writes=writes)
            else:
                P.op(eng, lambda e: e.tensor_scalar(out=out, in0=in0, scalar1=s1, scalar2=s2, op0=op0, op1=op1),
                     reads=reads, writes=writes)

        def stt(out, in0, scalar, in1, op0, op1, reads, writes):
            P.op(DVE, lambda e: e.scalar_tensor_tensor(out=out, in0=in0, scalar=scalar, in1=in1, op0=op0, op1=op1),
                 reads=reads, writes=writes)

        def cp(eng, out, in_, reads, writes):
            if eng == ACT:
                act(out, in_, AF.Copy, reads, writes)
            else:
                P.op(eng, lambda e: e.tensor_copy(out=out, in_=in_), reads=reads, writes=writes)

        def conv(dst, src, name):
            b = wscrB.setdefault(name, Buf(name))
            P.dma(POOL, lambda e: e.dma_start(out=dst, in_=src), writes=[b])

        def cblk(src2d):
            return src2d.rearrange("(c p) n -> p c n", p=128)

        def conv_layer_a(l):
            for i in range(2):
                conv(wqk_b[l, i], cblk(w_in_a[l][:, i * 512:(i + 1) * 512]), "wqk_b")
            conv(wif_b[l], cblk(w_in_a[l][:, 3072:3088]), "wif_b")
            for i in range(5):
                conv(wtok_b[l, i], cblk(w_in_a[l][:, 512 + i * 512:1024 + i * 512]), "wtok_b")
            for i in range(2):
                conv(wouta_b[l, i], cblk(w_out_a[l][:, i * 512:(i + 1) * 512]), "wouta_b")

        def conv_layer_b(j):
            if j == 0:
                conv(wkv_b, cblk(w_kv), "wkv_b")
                for kv in range(2):
                    for dup in range(2):
                        conv(wkd_b[:, :, kv * 128 + dup * 64:kv * 128 + dup * 64 + 64],
                             cblk(w_kv[:, kv * 64:(kv + 1) * 64]), "wkd_b")
            for i in range(2):
                conv(wq_b[j, i], cblk(w_q_b[j][:, i * 512:(i + 1) * 512]), "wq_b")
            for i in range(2):
                conv(woutb_b[j, i], cblk(w_out_b[j][:, i * 512:(i + 1) * 512]), "woutb_b")

        def conv_ffn(l):
            for b in range(11):
                conv(wgu_b[l, b][:, :, 0:256], cblk(w_gu[l][:, b * 256:(b + 1) * 256]), "wgu_b")
                conv(wgu_b[l, b][:, :, 256:512], cblk(w_gu[l][:, DFF + b * 256:DFF + (b + 1) * 256]), "wgu_b")
            for nh in range(2):
                for fb in range(3):
                    nfc = 8 if fb < 2 else 6
                    conv(wd_b[l, nh * 3 + fb][:, 0:nfc, :],
                         w_down[l][fb * 1024:fb * 1024 + nfc * 128, nh * 512:(nh + 1) * 512]
                         .rearrange("(c p) n -> p c n", p=128), "wd_b")

        def ld(q, dst, src, b):
            P.dma(q, lambda e: e.dma_start(out=dst, in_=src), writes=[b])

        ld(SP, ident[:], cin["ident"], constB)
        ld(SP, tri[:], cin["tri"], constB)
        ld(SP, sel[:], cin["sel"], constB)
        ld(SP, pairmask[:], cin["pairmask"], constB)
        ld(SP, ones8[:], cin["ones8"], constB)
        ld(SP, valid2[:], cin["valid2"], constB)
        if not _os.environ.get("K_SHRINK"):
            ld(SP, biasT[:], cin["biasT"].rearrange("g p n -> p g n"), biasB)
        ld(SP, bgate[:], bgate_in.rearrange("l p n -> p l n"), constB)
        ld(SP, gnT[:], gnT_in.rearrange("l p n -> p l n"), constB)
        ld(SP, esink[:], sinks_in.rearrange("l p n -> p l n"), constB)
        ld(SP, lng[:], lng_in.rearrange("l i p n -> p (l i) n"), constB)
        ld(SP, lnb[:], lnb_in.rearrange("l i p n -> p (l i) n"), constB)
        ld(SP, keep[:], keep_in, constB)
        cp(DVE, identb[:], ident[:], [constB], [constB])
        cp(DVE, trib[:], tri[:], [constB], [constB])
        P.op(DVE, lambda e: e.memset(onesb[:], 1.0 / 1024.0), writes=[constB])
        act(esink[:], esink[:], AF.Exp, [constB], [constB])
        for l in range(2):
            P.op(POOL, lambda e, l=l: e.memset(Cst[l][0][:], 0.0), writes=CstB[l][0])
            P.op(POOL, lambda e, l=l: e.memset(mst[l][0][:], 0.0), writes=[mstB[l][0]])
            for i in range(n_samp):
                P.dma(SP, lambda e, l=l, i=i: e.dma_start(out=Cst[l][1 + i][:, :, 0:128], in_=stC_in[l, i].rearrange("j q e -> q j e")), writes=CstB[l][1 + i])
                P.dma(SP, lambda e, l=l, i=i: e.dma_start(out=Cst[l][1 + i][:, :, 128:129],
                                                          in_=stn_in[l, i].rearrange("j (q o) -> q j o", o=1),
                                                          allow_slow_non_contiguous=True),
                      writes=CstB[l][1 + i])
                P.dma(SP, lambda e, l=l, i=i: e.dma_start(out=mst[l][1 + i][:],
                                                          in_=stm_in[l, i].rearrange("(h o) -> h o", o=1),
                                                          allow_slow_non_contiguous=True),
                      writes=[mstB[l][1 + i]])
        P.op(POOL, lambda e: e.memset(kTd[:], 0.0), writes=[kTdB])
        P.op(POOL, lambda e: e.memset(Vaug[:], 1.0), writes=[VaugB])
        P.op(POOL, lambda e: e.memset(Vaug[:, 0], 0.0), writes=[VaugB])
        P.op(POOL, lambda e: e.memset(Vaug_s[:], 1.0), writes=[VaugsB])
        for i in range(4):
            P.op(POOL, lambda e, i=i: e.memset(uv[i][:], 0.0), writes=[R["uv"].bufs[i]])

        conv_layer_a(0)
        conv_ffn(0)

        def feat_mm_group(wv, wB, m_lo, xsrc, xB, TT, kc=8):
            bk = bank()
            for c in range(kc):
                mm(psum[:, bk, 0:TT], wv[:, c, m_lo:m_lo + 128], xsrc[:, c, 0:TT], c == 0, c == kc - 1,
                   [wB] + xB, [bankB[bk]])
            return bk

        def layer_norm(li, TT, mix_banks_fn, release=None, LAG=1):
            release_ln()
            bm = bank()
            pin(bm)
            be = bank()
            pin(be)
            pend = []

            def flush(keep):
                while len(pend) > keep:
                    m, zb, zbB, zq, zqB = pend.pop(0)
                    mm(psum[:, bm, 0:TT], onesb[:], zb[:, 0:TT], m == 0, m == 7, [constB, zbB], [bankB[bm]])
                    mm(psum[:, be, 0:TT], onesb[:], zq[:, 0:TT], m == 0, m == 7, [constB, zqB], [bankB[be]])
            for m in range(8):
                bk = mix_banks_fn(m)
                stt(x32T[:, m, 0:TT], x32T[:, m, 0:TT], ALPHA, psum[:, bk, 0:TT], ALU.mult, ALU.add,
                    [x32B[m], bankB[bk]], [x32B[m]])
                if release is not None:
                    release(m)
                zb, zbB = R["b512"].next()
                act(zb[:, 0:TT], x32T[:, m, 0:TT], AF.Copy, [x32B[m]], [zbB])
                zq, zqB = R["b512"].next()
                act(zq[:, 0:TT], x32T[:, m, 0:TT], AF.Square, [x32B[m]], [zqB])
                pend.append((m, zb, zbB, zq, zqB))
                flush(LAG)
            flush(0)
            msq, msqB = R["f512"].next()
            act(msq[:, 0:TT], psum[:, bm, 0:TT], AF.Square, [bankB[bm]], [msqB])
            var, varB = R["f512"].next()
            tt(DVE, var[:, 0:TT], psum[:, be, 0:TT], msq[:, 0:TT], ALU.subtract, [bankB[be], msqB], [varB])
            ts(DVE, var[:, 0:TT], var[:, 0:TT], 0.0, LN_EPS, ALU.max, ALU.add, [varB], [varB])
            act(var[:, 0:TT], var[:, 0:TT], AF.Ln, [varB], [varB])
            act(psum[:, be, 0:TT], var[:, 0:TT], AF.Exp, [varB], [bankB[be]], scale=-0.5)
            for m in range(8):
                t1, t1B = R["f512"].next()
                tt(DVE, t1[:, 0:TT], x32T[:, m, 0:TT], psum[:, bm, 0:TT], ALU.subtract, [x32B[m], bankB[bm]], [t1B])
                tt(DVE, t1[:, 0:TT], t1[:, 0:TT], psum[:, be, 0:TT], ALU.mult, [t1B, bankB[be]], [t1B])
                ts(LNENG, x32T[:, m, 0:TT], t1[:, 0:TT], lng[:, li, m:m + 1], lnb[:, li, m:m + 1], ALU.mult, ALU.add,
                   [t1B, constB], [x32B[m]])
                act(xT[:, m, 0:TT], t1[:, 0:TT], AF.Identity, [t1B, constB], [xTB[m]],
                    scale=lng[:, li, m:m + 1], bias=lnb[:, li, m:m + 1])
            LNPIN.extend([bm, be])

        LNPIN = []

        def release_ln():
            while LNPIN:
                unpin(LNPIN.pop())

        def couter(mmfns):
            bks = [bank() for _ in mmfns]
            release_ln()
            for c in range(8):
                for fn, bk in zip(mmfns, bks):
                    fn(c, bk)
            return bks

        def out_proj_ln(wblocks, wname, li, TT):
            loaded = {}

            def get(m):
                i = m // 4
                if i not in loaded:
                    loaded[i] = wload(wblocks[i], v8(512), wname)
                wv, wB = loaded[i]
                return feat_mm_group(wv, wB, (m % 4) * 128, hT, hTB, TT)
            layer_norm(li, TT, get)

        def ffn(l, TT):
            def gu_evac(f, bg, bu):
                sg, sgB = R["f512"].next()
                act(sg[:, 0:TT], psum[:, bg, 0:TT], AF.Silu, [bankB[bg]], [sgB])
                tt(DVE, hffT[:, f, 0:TT], sg[:, 0:TT], psum[:, bu, 0:TT], ALU.mult, [sgB, bankB[bu]], [hffB[f]])

            blocks = {}

            def blk(b):
                if b not in blocks:
                    blocks[b] = wload(wgu_b[l, b], v8(512), "wgu_b")
                return blocks[b]

            def gu_mm(f, col0):
                b, ff = f // 2, f % 2
                wv, wB = blk(b)

                def fn(c, bk):
                    mm(psum[:, bk, 0:TT], wv[:, c, col0 + ff * 128:col0 + ff * 128 + 128], xT[:, c, 0:TT],
                       c == 0, c == 7, [wB, xTB[c]], [bankB[bk]])
                return fn
            blk(0)
            blk(1)
            bks = couter([gu_mm(f, col0) for f in range(3) for col0 in (0, 256)])
            for f in range(3):
                gu_evac(f, bks[2 * f], bks[2 * f + 1])
            for f in range(3, NF):
                wv, wB = blk(f // 2)
                ff = f % 2
                bg = feat_mm_group(wv, wB, ff * 128, xT, xTB, TT)
                bu = feat_mm_group(wv, wB, 256 + ff * 128, xT, xTB, TT)
                gu_evac(f, bg, bu)
            banks = {}

            def down_half(nh):
                bks = [bank() for _ in range(4)]
                for b in bks:
                    pin(b)
                for fb in range(3):
                    nfc = 8 if fb < 2 else 6
                    wv, wB = wload(wd_b[l, nh * 3 + fb][:, 0:nfc, :],
                                   lambda t, nfc=nfc: t[:, 0:nfc * 512].rearrange("p (c n) -> p c n", c=nfc), "wd_b")
                    for mi in range(4):
                        for fc in range(nfc):
                            f = fb * 8 + fc
                            mm(psum[:, bks[mi], 0:TT], wv[:, fc, mi * 128:(mi + 1) * 128], hffT[:, f, 0:TT],
                               f == 0, f == NF - 1, [wB, hffB[f]], [bankB[bks[mi]]])
                for mi in range(4):
                    banks[nh * 4 + mi] = bks[mi]

            def get(m):
                if m % 4 == 0:
                    down_half(m // 4)
                return banks[m]

            layer_norm(l * 2 + 1, TT, get, release=lambda m: unpin(banks[m]))

        def load_tile_x(subs, TT):
            release_ln()
            for s, sub in enumerate(subs):
                xt, xtB = R["io"].next()
                if sub["kind"] == "p":
                    r0 = sub["row0"]
                    P.dma(SP, lambda e, xt=xt, r0=r0: e.dma_start(out=xt[:], in_=xin[r0:r0 + 128, :]), writes=[xtB])
                else:
                    P.op(DVE, lambda e, xt=xt: e.memset(xt[:], 0.0), writes=[xtB])
                    i = sub["si"]
                    P.dma(SP, lambda e, xt=xt, i=i: e.dma_start(out=xt[0:32, :], in_=xs_in[i]), writes=[xtB])
                for g in range(2):
                    bk = bank()
                    for cc in range(4):
                        c = g * 4 + cc
                        P.op(PE, lambda e, bk=bk, cc=cc, c=c, xt=xt: e.transpose(
                            out=psum[:, bk, cc * 128:(cc + 1) * 128], in_=xt[:, c * 128:(c + 1) * 128],
                            identity=ident[:]), reads=[xtB, constB], writes=[bankB[bk]])
                    src = psum[:, bk, :].rearrange("p (c n) -> p c n", c=4)
                    if not _os.environ.get("K_NOACT"):
                        act(x32T[:, g * 4:(g + 1) * 4, s * 128:(s + 1) * 128], src, AF.Copy, [bankB[bk]],
                            x32B[g * 4:(g + 1) * 4])
                    if not _os.environ.get("K_NODVE"):
                        cp(_os.environ.get("K_E2", DVE), xT[:, g * 4:(g + 1) * 4, s * 128:(s + 1) * 128], src, [bankB[bk]], xTB[g * 4:(g + 1) * 4])

        def store_tile_y(subs, TT):
            release_ln()
            for s, sub in enumerate(subs):
                if sub.get("discard"):
                    continue
                yt, ytB = R["io"].next()
                for g in range(2):
                    bk = bank()
                    for cc in range(4):
                        c = g * 4 + cc
                        P.op(PE, lambda e, bk=bk, cc=cc, c=c, s=s: e.transpose(
                            out=psum[:, bk, cc * 128:(cc + 1) * 128], in_=x32T[:, c, s * 128:(s + 1) * 128],
                            identity=ident[:]), reads=[x32B[c], constB], writes=[bankB[bk]])
                    if g == 0:
                        act(yt[:, 0:512], psum[:, bk, :], AF.Copy, [bankB[bk]], [ytB])
                    else:
                        cp(DVE, yt[:, 512:1024], psum[:, bk, :], [bankB[bk]], [ytB])
                if sub["kind"] == "p":
                    r0 = sub["yrow0"]
                    P.dma(YQ, lambda e, yt=yt, r0=r0: e.dma_start(out=y_out[r0:r0 + 128, :], in_=yt[:]),
                          reads=[ytB], final=True)
                else:
                    i = sub["si"]
                    P.dma(YQ, lambda e, yt=yt, i=i: e.dma_start(out=ys_out[i], in_=yt[0:32, :]),
                          reads=[ytB], final=True)

        def mlstm_layer(l, subs, TT, state_only=False):
            NS = len(subs)
            if state_only:
                release_ln()
            else:
                wq2 = [wload(wqk_b[l, i], v8(512), "wqk_b") for i in range(2)]

                def qk_mm(m):
                    wv, wB = wq2[m // 4]

                    def fn(c, bk):
                        mm(psum[:, bk, 0:TT], wv[:, c, (m % 4) * 128:(m % 4) * 128 + 128], xT[:, c, 0:TT],
                           c == 0, c == 7, [wB, xTB[c]], [bankB[bk]])
                    return fn

                def qk_evac(m, bk):
                    if m < 4:
                        act(qkT[:, m, 0:TT], psum[:, bk, 0:TT], AF.Copy, [bankB[bk]], [qkB[m]])
                    else:
                        act(qkT[:, m, 0:TT], psum[:, bk, 0:TT], AF.Copy, [bankB[bk]], [qkB[m]], scale=0.125)
                bks = couter([qk_mm(m) for m in range(6)])
                for m in range(6):
                    qk_evac(m, bks[m])
                for m in (6, 7):
                    wv, wB = wq2[1]
                    bk = feat_mm_group(wv, wB, (m % 4) * 128, xT, xTB, TT)
                    qk_evac(m, bk)
            wv, wB = wload(wif_b[l], v8(16), "wif_b")
            G = []
            G0 = []
            for s, sub in enumerate(subs):
                bk = bank()
                for c in range(8):
                    mm(psum[:, bk, 0:16], xT[:, c, s * 128:(s + 1) * 128], wv[:, c, :], c == 0, c == 7,
                       [wB] + xTB, [bankB[bk]])
                ga, gaB = R["gat"].next()
                tt(DVE, ga[:], psum[:, bk, 0:16], bgate[:, l, :], ALU.add, [bankB[bk], constB], [gaB])
                ee, eeB = R["e1"].next()
                act(ee[:], ga[:, 8:16], AF.Exp, [gaB], [eeB], scale=-1.0)
                sp_, spB = R["sp"].next()
                act(sp_[:], ee[:], AF.Ln, [eeB], [spB], bias=1.0)
                if sub["kind"] == "s":
                    ts(DVE, sp_[:], sp_[:], valid2[:, 0:1], None, ALU.mult, None, [spB, constB], [spB])
                    ts(DVE, ga[:, 0:8], ga[:, 0:8], valid2[:, 0:1], valid2[:, 1:2], ALU.mult, ALU.add,
                       [gaB, constB], [gaB])
                G0.append((ga, gaB, sp_, spB))
            KT, VR, OG = [None] * NS, [None] * NS, [None] * NS
            for n in range(0 if state_only else 1, 3 if state_only else 5):
                wv, wB = wload(wtok_b[l, n], v8(512), "wtok_b")
                for s in range(NS):
                    bk = bank()
                    for c in range(8):
                        mm(psum[:, bk, :], xT[:, c, s * 128:(s + 1) * 128], wv[:, c, :], c == 0, c == 7,
                           [wB] + xTB, [bankB[bk]])
                    if n == 0:
                        KT[s] = R["ktok"].next()
                        act(KT[s][0][:], psum[:, bk, :], AF.Copy, [bankB[bk]], [KT[s][1]], scale=0.125)
                    elif n in (1, 2):
                        if n == 1:
                            VR[s] = R["vraw"].next()
                        cp(DVE, VR[s][0][:, (n - 1) * 512:n * 512], psum[:, bk, :], [bankB[bk]], [VR[s][1]])
                    else:
                        if n == 3:
                            OG[s] = R["og"].next()
                        act(OG[s][0][:, (n - 3) * 512:(n - 2) * 512], psum[:, bk, :], AF.Sigmoid, [bankB[bk]],
                            [OG[s][1]])
                    if NS > 2 and n == 0 and s >= 1:
                        pass
            if not state_only:
                for s in range(NS):
                    bk = bank()
                    pbk = psum[:, bk, :].bitcast(BF16)
                    for jj in range(4):
                        P.op(PE, lambda e, pbk=pbk, jj=jj, s=s: e.transpose(
                            out=pbk[:, jj * 128:(jj + 1) * 128], in_=qkT[:, 4 + jj, s * 128:(s + 1) * 128],
                            identity=identb[:]), reads=[qkB[4 + jj], constB], writes=[bankB[bk]])
                    KT[s] = R["ktok"].next()
                    act(KT[s][0][:], pbk[:, 0:512], AF.Copy, [bankB[bk]], [KT[s][1]])
            for s, sub in enumerate(subs):
                ga, gaB, sp_, spB = G0[s]
                bk2 = bank()
                mm(psum[:, bk2, 0:8], tri[:], sp_[:], True, True, [constB, spB], [bankB[bk2]])
                a_, aB = R["an"].next()
                tt(DVE, a_[:, 0:8], ga[:, 0:8], psum[:, bk2, 0:8], ALU.add, [gaB, bankB[bk2]], [aB])
                cp(DVE, a_[:, 8:16], psum[:, bk2, 0:8], [bankB[bk2]], [aB])
                bk3 = bank()
                P.op(PE, lambda e, bk3=bk3, a_=a_: e.transpose(out=psum[0:8, bk3, 0:128], in_=a_[:, 0:8],
                                                               identity=ident[:]),
                     reads=[aB, constB], writes=[bankB[bk3]])
                P.op(PE, lambda e, bk3=bk3, a_=a_: e.transpose(out=psum[0:8, bk3, 128:256], in_=a_[:, 8:16],
                                                               identity=ident[:]),
                     reads=[aB, constB], writes=[bankB[bk3]])
                s8, s8B = R["sm8"].next()
                P.op(DVE, lambda e, s8=s8, bk3=bk3: e.tensor_reduce(out=s8[:, 0:1], in_=psum[0:8, bk3, 0:128],
                                                                    axis=AX.X, op=ALU.max),
                     reads=[bankB[bk3]], writes=[s8B])
                cp(DVE, s8[:, 4:5], psum[0:8, bk3, 255:256], [bankB[bk3]], [s8B])
                G.append((a_, aB, s8, s8B))
            CH = [None] * NS

            def phase_ab(s):
                sub = subs[s]
                si = sub["st"]
                M = mst[l][si]
                MB = mstB[l][si]
                a_, aB, s8, s8B = G[s]
                tt(DVE, s8[:, 1:2], s8[:, 0:1], M[:], ALU.max, [s8B, MB], [s8B])
                tt(DVE, s8[:, 2:3], M[:], s8[:, 1:2], ALU.subtract, [s8B, MB], [s8B])
                tt(DVE, s8[:, 3:4], M[:], s8[:, 1:2], ALU.subtract, [s8B, MB], [s8B])
                tt(DVE, M[:], s8[:, 1:2], s8[:, 4:5], ALU.subtract, [s8B], [MB])
                dg, dgB = R["dgG"].next()
                act(dg[:, 12:14], s8[:, 2:4], AF.Exp, [s8B], [dgB])
                ts(DVE, dg[:, 0:8], ident[0:8, 0:8], s8[:, 1:2], None, ALU.mult, None, [constB, s8B], [dgB])
                ts(DVE, dg[:, 8:12], pairmask[:], dg[:, 12:13], None, ALU.mult, None, [constB, dgB], [dgB])
                ch = {}
                st_banks = []
                if not state_only:
                    stt_, stBs = STring.next()
                    ch["ST"], ch["STB"] = stt_, stBs
                    for hg in range(2):
                        bk2 = bank()
                        for hh in range(4):
                            h = 2 * hh + hg
                            j, p = h // 2, h % 2
                            mm(psum[:, bk2, hh * 128:(hh + 1) * 128],
                               qkT[p * 64:(p + 1) * 64, 4 + j, s * 128:(s + 1) * 128],
                               qkT[p * 64:(p + 1) * 64, j, s * 128:(s + 1) * 128], True, True,
                               [qkB[4 + j], qkB[j]], [bankB[bk2]])
                        st_banks.append(bk2)
                bk = bank()
                mm(psum[:, bk, 0:8], ones8[:], dg[:, 0:8], True, True, [constB, dgB], [bankB[bk]])
                mm(psum[:, bk, 8:12], sel[:], dg[:, 8:12], True, True, [constB, dgB], [bankB[bk]])
                t16, t16B = R["tm16"].next()
                tt(DVE, t16[:, 0:8], a_[:, 0:8], psum[:, bk, 0:8], ALU.subtract, [aB, bankB[bk]], [t16B])
                tt(DVE, t16[:, 8:11], a_[:, 8:14:2], psum[:, bk, 0:6:2], ALU.subtract, [aB, bankB[bk]], [t16B])
                tt(DVE, t16[:, 11:14], a_[:, 9:15:2], psum[:, bk, 1:7:2], ALU.subtract, [aB, bankB[bk]], [t16B])
                tt(DVE, t16[:, 14:16], a_[:, 14:16], psum[:, bk, 6:8], ALU.subtract, [aB, bankB[bk]], [t16B])
                u_, uB = R["ul"].next()
                act(u_[:], t16[:], AF.Exp, [t16B], [uB])
                dsb, dsbB = R["dsb"].next()
                cp(DVE, dsb[:], psum[:, bk, 8:12], [bankB[bk]], [dsbB])
                for hg, bk2 in enumerate(st_banks):
                    if BCAST:
                        tt(DVE, stt_[:, hg:8:2, :], psum[:, bk2, :].rearrange("p (h n) -> p h n", h=4),
                           trib[:].unsqueeze(1).to_broadcast([128, 4, 128]), ALU.mult, [bankB[bk2], constB], stBs)
                    else:
                        for hh in range(4):
                            tt(DVE, stt_[:, 2 * hh + hg, :], psum[:, bk2, hh * 128:(hh + 1) * 128], trib[:],
                               ALU.mult, [bankB[bk2], constB], stBs)
                uvt, uvB = R["uv"].next()
                vr, vrB = VR[s]
                if BCAST:
                    tt(UVENG, uvt[:, :, 0:128], vr[:].rearrange("p (h n) -> p h n", h=8),
                       u_[:, 0:8].unsqueeze(2).to_broadcast([128, 8, 128]), ALU.mult, [vrB, uB], [uvB])
                else:
                    for h in range(8):
                        if h % 2 == 0:
                            act(uvt[:, h, 0:128], vr[:, h * 128:(h + 1) * 128], AF.Identity, [vrB, uB], [uvB],
                                scale=u_[:, h:h + 1])
                        else:
                            ts(DVE, uvt[:, h, 0:128], vr[:, h * 128:(h + 1) * 128], u_[:, h:h + 1], None, ALU.mult,
                               None, [vrB, uB], [uvB])
                cp(UVENG, uvt[:, :, 128:129], u_[:, 0:8].unsqueeze(2), [uB], [uvB])
                ch.update(dict(u=u_, uB=uB, dsb=dsb, dsbB=dsbB, uv=uvt, uvB=uvB))
                CH[s] = ch

            def transposes(s):
                hnt, hnB = CH[s]["hn"], CH[s]["hnB"]
                bk4 = bank()
                pb = psum[:, bk4, :].bitcast(BF16)
                for c in range(8):
                    P.op(PE, lambda e, pb=pb, c=c, hnt=hnt: e.transpose(
                        out=pb[:, c * 128:(c + 1) * 128], in_=hnt[:, c * 128:(c + 1) * 128], identity=identb[:]),
                        reads=[hnB, constB], writes=[bankB[bk4]])
                if BCAST:
                    tt(DVE, hT[:, :, s * 128:(s + 1) * 128], pb.rearrange("p (c n) -> p c n", c=8),
                       gnT[:, l, :].unsqueeze(2).to_broadcast([128, 8, 128]), ALU.mult,
                       [bankB[bk4], constB], [hTB[s]])
                else:
                    for c in range(8):
                        if c % 2 == 0:
                            act(hT[:, c, s * 128:(s + 1) * 128], pb[:, c * 128:(c + 1) * 128], AF.Identity,
                                [bankB[bk4], constB], [hTB[s]], scale=gnT[:, l, c:c + 1])
                        else:
                            ts(DVE, hT[:, c, s * 128:(s + 1) * 128], pb[:, c * 128:(c + 1) * 128],
                               gnT[:, l, c:c + 1], None, ALU.mult, None, [bankB[bk4], constB], [hTB[s]])

            def decay_cb(s):
                si = subs[s]["st"]
                C, CB = Cst[l][si], CstB[l][si]
                ch = CH[s]
                dsb, dsbB = ch["dsb"], ch["dsbB"]
                if BCAST:
                    tt(DVE, C[:], C[:], dsb[:].unsqueeze(2).to_broadcast([128, 4, 129]), ALU.mult, CB + [dsbB], CB)
                else:
                    for jj in range(4):
                        ts(DVE, C[:, jj, :], C[:, jj, :], dsb[:, jj:jj + 1], None, ALU.mult, None, CB + [dsbB], CB)
                ch["cb"], ch["cbB"] = None, None
                if not state_only:
                    cb, cbB = R["C0b"].next()
                    cp(ACT, cb[:, :, 0:129], C[:], CB, [cbB])
                    ch["cb"], ch["cbB"] = cb, cbB

            phase_ab(0)
            for s, sub in enumerate(subs):
                si = sub["st"]
                C = Cst[l][si]
                CB = CstB[l][si]
                ch = CH[s]
                u_, uB, dsb, dsbB, uvt, uvB = ch["u"], ch["uB"], ch["dsb"], ch["dsbB"], ch["uv"], ch["uvB"]
                if "cb" not in ch:
                    decay_cb(s)
                cb, cbB = ch["cb"], ch["cbB"]
                if s + 1 < NS:
                    phase_ab(s + 1)
                if not state_only:
                    stt_, stBs = ch["ST"], ch["STB"]
                    nr, nrB = R["nrm"].next()
                    hnt, hnB = R["hn"].next()
                    ch["hn"], ch["hnB"] = hnt, hnB
                    ogt, ogB = OG[s]
                    slot0 = 0
                    obanks = []
                    for grp in ((0, 2, 4), (1, 3, 5), (6,), (7,)):
                        bk3 = bank()
                        for gi, h in enumerate(grp):
                            j, p = h // 2, h % 2
                            o_ap = psum[:, bk3, gi * 129:(gi + 1) * 129]
                            mm(o_ap, qkT[p * 64:(p + 1) * 64, j, s * 128:(s + 1) * 128],
                               cb[p * 64:(p + 1) * 64, j, 0:129], True, False, [qkB[j], cbB], [bankB[bk3]])
                            mm(o_ap, stt_[:, h, :], uvt[:, h, 0:129], False, True, stBs + [uvB], [bankB[bk3]])
                        ng = len(grp)
                        h0 = slot0
                        slot0 += ng
                        obanks.append((bk3, grp, h0))
                        pv = psum[:, bk3, 0:ng * 129].rearrange("p (h n) -> p h n", h=ng)
                        act(nr[:, h0:h0 + ng].unsqueeze(2), pv[:, :, 128:129], AF.Abs, [bankB[bk3]], [nrB])
                        for gi, h in enumerate(grp):
                            P.op(ACT, lambda e, bk3=bk3, gi=gi, h0=h0, nr=nr: e.activation(
                                out=sq[:, h0 + gi, :], in_=psum[:, bk3, gi * 129:gi * 129 + 128], func=AF.Square,
                                accum_out=nr[:, 8 + h0 + gi:9 + h0 + gi]), reads=[bankB[bk3]], writes=[sqB, nrB])
                kt, ktB = KT[s]
                sbanks = []
                for j in range(4):
                    bk5 = bank()
                    mm(psum[:, bk5, 0:258], kt[:, j * 128:(j + 1) * 128],
                       uvt[:, 2 * j:2 * j + 2, 0:129], True, True, [ktB, uvB], [bankB[bk5]])
                    sbanks.append(bk5)
                for j, bk5 in enumerate(sbanks):
                    tt(DVE, C[0:64, j, :], C[0:64, j, :], psum[0:64, bk5, 0:129], ALU.add, [CB[2 * j], bankB[bk5]], [CB[2 * j]])
                    tt(DVE, C[64:128, j, :], C[64:128, j, :], psum[64:128, bk5, 129:258], ALU.add,
                       [CB[2 * j + 1], bankB[bk5]], [CB[2 * j + 1]])
                if s + 1 < NS:
                    decay_cb(s + 1)
                if not state_only:
                    nr2, nr2B = R["nrm2"].next()
                    tt(DVE, nr2[:, 16:24], nr[:, 0:8], u_[:, 8:16], ALU.max, [nrB, uB], [nr2B])
                    tt(DVE, nr2[:, 24:32], nr2[:, 16:24], nr2[:, 16:24], ALU.mult, [nr2B], [nr2B])
                    ts(DVE, nr2[:, 24:32], nr2[:, 24:32], HN_EPS, None, ALU.mult, None, [nr2B], [nr2B])
                    stt(nr2[:, 24:32], nr[:, 8:16], 1.0 / 128.0, nr2[:, 24:32], ALU.mult, ALU.add, [nrB, nr2B], [nr2B])
                    act(nr2[:, 32:40], nr2[:, 24:32], AF.Ln, [nr2B], [nr2B])
                    act(nr2[:, 40:48], nr2[:, 32:40], AF.Exp, [nr2B], [nr2B], scale=-0.5)
                    for bk3, grp, h0 in obanks:
                        for gi, h in enumerate(grp):
                            act(hnt[:, h * 128:(h + 1) * 128], psum[:, bk3, gi * 129:gi * 129 + 128], AF.Identity,
                                [bankB[bk3], nr2B], [hnB], scale=nr2[:, 40 + h0 + gi:41 + h0 + gi])
                    tt(POOL, hnt[:], hnt[:], ogt[:], ALU.mult, [hnB, ogB], [hnB])
                    if s >= 1:
                        transposes(s - 1)
            if not state_only:
                transposes(NS - 1)
            if state_only:
                return
            out_proj_ln(wouta_b[l], "wouta_b", l * 2, TT)

        def swa_layer(j, subs, TT, first_prompt_tile, halo_keep=False, kv_only=False):
            NS = len(subs)
            if j == 0:
                release_ln()
            sample = subs[0]["kind"] == "s"
            kT_ = kTd_s if sample else kTd
            kTB = kTdsB if sample else kTdB
            VA = Vaug_s if sample else Vaug
            VAB = VaugsB if sample else VaugB

            def blocks(s):
                return (2 * s, 2 * s + 1) if sample else (s, s + 1)
            if j == 0:
                if sample:
                    for s, sub in enumerate(subs):
                        i = sub["si"]
                        pb_, ob_ = blocks(s)
                        ct, ctB = R["io"].next()
                        P.dma(SP, lambda e, ct=ct, i=i: e.dma_start(out=ct[:, 0:128], in_=ck_in[i]), writes=[ctB])
                        P.dma(SP, lambda e, ct=ct, i=i: e.dma_start(out=ct[:, 128:256], in_=cv_in[i]), writes=[ctB])
                        kd, kdB = R["b512"].next()
                        for kv in range(2):
                            for dup in range(2):
                                cp(DVE, kd[:, kv * 128 + dup * 64:kv * 128 + dup * 64 + 64],
                                   ct[:, kv * 64:(kv + 1) * 64], [ctB], [kdB])
                        cp(DVE, VA[:, pb_, :, 0:64], ct[:, 128:256].rearrange("p (k d) -> p k d", k=2), [ctB], [VAB])
                        bk = bank()
                        pb = psum[:, bk, :].bitcast(BF16)
                        for kv in range(2):
                            P.op(PE, lambda e, pb=pb, kv=kv, kd=kd: e.transpose(
                                out=pb[:, kv * 128:(kv + 1) * 128], in_=kd[:, kv * 128:(kv + 1) * 128],
                                identity=identb[:]), reads=[kdB, constB], writes=[bankB[bk]])
                        cp(DVE, kT_[:, :, pb_, :], pb[:, 0:256].rearrange("p (k n) -> p k n", k=2),
                           [bankB[bk]], [kTB])
                elif not first_prompt_tile:
                    cp(DVE, kT_[:, :, 0, :], kT_[:, :, 4, :], [kTB], [kTB])
                    cp(DVE, VA[:, 0], VA[:, 4], [VAB], [VAB])
                    if halo_keep:
                        ts(DVE, VA[:, 0].rearrange("p k n -> p (k n)"), VA[:, 0].rearrange("p k n -> p (k n)"),
                           keep[:, 0:1], None, ALU.mult, None, [VAB, constB], [VAB])
                wv, wB = wload(wkv_b, v8(256), "wkv_b")
                for s, sub in enumerate(subs):
                    pb_, ob_ = blocks(s)
                    bk = bank()
                    for c in range(8):
                        mm(psum[:, bk, 0:256], xT[:, c, s * 128:(s + 1) * 128], wv[:, c, :], c == 0, c == 7,
                           [wB] + xTB, [bankB[bk]])
                    act(VA[:, ob_, :, 0:64], psum[:, bk, 128:256].rearrange("p (k d) -> p k d", k=2), AF.Copy,
                        [bankB[bk]], [VAB])
                    if sample:
                        P.op(POOL, lambda e, ob_=ob_: e.memset(VA[32:64, ob_], 0.0), writes=[VAB])
                        P.op(POOL, lambda e, ob_=ob_: e.memset(VA[64:128, ob_], 0.0), writes=[VAB])
                    if sample or sub.get("last"):
                        cp(DVE, kvtok[:], psum[:, bk, 0:256], [bankB[bk], kvtokB], [kvtokB])
                        if sample:
                            i = sub["si"]
                            P.dma(ACT, lambda e, i=i: e.dma_start(out=sk_out[i], in_=kvtok[0:32, 0:128]),
                                  reads=[kvtokB], final=True)
                            P.dma(ACT, lambda e, i=i: e.dma_start(out=sv_out[i], in_=kvtok[0:32, 128:256]),
                                  reads=[kvtokB], final=True)
                        else:
                            P.dma(ACT, lambda e: e.dma_start(out=pk_out, in_=kvtok[:, 0:128]),
                                  reads=[kvtokB], final=True)
                            P.dma(ACT, lambda e: e.dma_start(out=pv_out, in_=kvtok[:, 128:256]),
                                  reads=[kvtokB], final=True)
                wv, wB = wload(wkd_b, v8(256), "wkd_b")
                for kv in range(2):
                    bk = feat_mm_group(wv, wB, kv * 128, xT, xTB, TT)
                    if sample:
                        for s in range(NS):
                            act(kT_[:, kv, 2 * s + 1, :], psum[:, bk, s * 128:(s + 1) * 128], AF.Copy,
                                [bankB[bk]], [kTB])
                    else:
                        act(kT_[:, kv, 1:5, :], psum[:, bk, :].rearrange("p (b n) -> p b n", b=4), AF.Copy,
                            [bankB[bk]], [kTB])
            if kv_only:
                return
            wq2 = [wload(wq_b[j, i], v8(512), "wq_b") for i in range(2)]

            def q_mm(m):
                wv, wB = wq2[m // 4]

                def fn(c, bk):
                    mm(psum[:, bk, 0:TT], wv[:, c, (m % 4) * 128:(m % 4) * 128 + 128], xT[:, c, 0:TT],
                       c == 0, c == 7, [wB, xTB[c]], [bankB[bk]])
                return fn
            nco = 6 if j == 1 else 0
            if nco:
                bks = couter([q_mm(m) for m in range(nco)])
                for m in range(nco):
                    act(qkT[:, m, 0:TT], psum[:, bks[m], 0:TT], AF.Copy, [bankB[bks[m]]], [qkB[m]], scale=0.125)
            for m in range(nco, 8):
                wv, wB = wq2[m // 4]
                bk = feat_mm_group(wv, wB, (m % 4) * 128, xT, xTB, TT)
                act(qkT[:, m, 0:TT], psum[:, bk, 0:TT], AF.Copy, [bankB[bk]], [qkB[m]], scale=0.125)
            def scores(s):
                blk = blocks(s)
                PTs = {}
                for kb in range(2):
                    for kv in range(2):
                        for p in range(2):
                            bk = bank()
                            mm(psum[:, bk, :], kT_[p * 64:(p + 1) * 64, kv, blk[kb], :],
                               qkT[p * 64:(p + 1) * 64, kv * 4:(kv + 1) * 4, s * 128:(s + 1) * 128],
                               True, True, [kTB] + qkB[kv * 4:(kv + 1) * 4], [bankB[bk]])
                            tmp, tmpB = R["f512"].next()
                            tt(DVE, tmp[:], psum[:, bk, :], biasT[:, kb * 4 + kv * 2 + p, :], ALU.add,
                               [bankB[bk], biasB], [tmpB])
                            pt, ptB = R["PT"].next()
                            act(pt[:], tmp[:], AF.Exp, [tmpB], [ptB])
                            PTs[(kb, kv, p)] = (pt, ptB)
                return PTs

            def pv_norm(s, PTs):
                blk = blocks(s)
                ont, onB = R["hn"].next()
                nr, nrB = R["nrm"].next()
                for grp in (tuple(range(0, 7)), tuple(range(7, 14)), (14, 15)):
                    bk = bank()
                    for gi, head in enumerate(grp):
                        kv, g_ = head // 8, head % 8
                        hh, p = g_ // 2, g_ % 2
                        for kb in range(2):
                            pt, ptB = PTs[(kb, kv, p)]
                            mm(psum[:, bk, gi * 65:(gi + 1) * 65], pt[:, hh * 128:(hh + 1) * 128],
                               VA[:, blk[kb], kv, 0:65], kb == 0, kb == 1, [ptB, VAB], [bankB[bk]])
                    ng = len(grp)
                    h0 = grp[0]
                    pv = psum[:, bk, 0:ng * 65].rearrange("p (h n) -> p h n", h=ng)
                    tt(DVE, nr[:, h0:h0 + ng].unsqueeze(2), pv[:, :, 64:65], esink[:, j, h0:h0 + ng].unsqueeze(2),
                       ALU.add, [bankB[bk], constB], [nrB])
                    P.op(DVE, lambda e, nr=nr, h0=h0, ng=ng: e.reciprocal(out=nr[:, 16 + h0:16 + h0 + ng],
                                                                          in_=nr[:, h0:h0 + ng]),
                         reads=[nrB], writes=[nrB])
                    if BCAST:
                        tt(DVE, ont[:, h0 * 64:(h0 + ng) * 64].rearrange("p (h n) -> p h n", h=ng), pv[:, :, 0:64],
                           nr[:, 16 + h0:16 + h0 + ng].unsqueeze(2).to_broadcast([128, ng, 64]), ALU.mult,
                           [bankB[bk], nrB], [onB])
                    else:
                        for gi, head in enumerate(grp):
                            if head % 2 == 0:
                                ts(DVE, ont[:, head * 64:(head + 1) * 64], psum[:, bk, gi * 65:gi * 65 + 64],
                                   nr[:, 16 + head:17 + head], None, ALU.mult, None, [bankB[bk], nrB], [onB])
                            else:
                                act(ont[:, head * 64:(head + 1) * 64], psum[:, bk, gi * 65:gi * 65 + 64],
                                    AF.Identity, [bankB[bk], nrB], [onB], scale=nr[:, 16 + head:17 + head])
                return ont, onB

            def attn_transposes(s, ont, onB):
                bk4 = bank()
                pb = psum[:, bk4, :].bitcast(BF16)
                for c in range(8):
                    P.op(PE, lambda e, pb=pb, c=c, ont=ont: e.transpose(
                        out=pb[:, c * 128:(c + 1) * 128], in_=ont[:, c * 128:(c + 1) * 128], identity=identb[:]),
                        reads=[onB, constB], writes=[bankB[bk4]])
                act(hT[:, :, s * 128:(s + 1) * 128], pb.rearrange("p (c n) -> p c n", c=8), AF.Copy,
                    [bankB[bk4]], [hTB[s]])

            PTn = scores(0)
            for s in range(NS):
                ont, onB = pv_norm(s, PTn)
                if s + 1 < NS:
                    PTn = scores(s + 1)
                attn_transposes(s, ont, onB)
            out_proj_ln(woutb_b[j], "woutb_b", (2 + j) * 2, TT)

        tiles = []
        for t in range(NT_P):
            subs = [dict(kind="p", st=0, row0=t * 512 + s * 128, yrow0=(t - NPRE) * 512 + s * 128,
                         discard=(t < NPRE)) for s in range(4)]
            if t == NT_P - 1:
                subs[-1]["last"] = True
            tiles.append(subs)
        if n_samp and not _os.environ.get("K_NOSAMP"):
            tiles.append([dict(kind="s", st=1 + i, si=i) for i in range(n_samp)])

        STOP = int(_os.environ.get("K_STOP", "99"))
        for ti, subs in enumerate(tiles):
            TT = 128 * len(subs)
            state_mode = subs[0]["kind"] == "p" and ti < NPRE - 1
            load_tile_x(subs, TT)
            spread = NPRE >= 5
            if ti == 0:
                conv_layer_a(1)
            mlstm_layer(0, subs, TT)
            if (ti == 1 and spread) or (ti == 0 and not spread):
                conv_ffn(1)
            ffn(0, TT)
            if (ti == 1 and spread) or (ti == 0 and not spread):
                conv_layer_b(0)
            mlstm_layer(1, subs, TT, state_only=state_mode)
            if (ti == 2 and spread) or (ti == 0 and not spread):
                conv_ffn(2)
            if (ti == 3 and spread) or (ti == 0 and not spread):
                conv_layer_b(1)
                conv_ffn(3)
            halo_tile = subs[0]["kind"] == "p" and NPRE > 0 and ti == NPRE - 1
            if halo_tile:
                ffn(1, TT)
                swa_layer(0, subs, TT, True, kv_only=True)
                release_ln()
            elif not state_mode:
                ffn(1, TT)
                for j in range(2):
                    swa_layer(j, subs, TT, ti == max(NPRE - 1, 0), halo_keep=(NPRE > 0 and ti == NPRE))
                    ffn(2 + j, TT)
                store_tile_y(subs, TT)
            if NPRE and ti == NPRE - 1:
                for l in range(2):
                    for jj in range(4):
                        ts(DVE, Cst[l][0][:, jj, :], Cst[l][0][:, jj, :], keep[:, 0:1], None, ALU.mult, None,
                           CstB[l][0] + [constB], CstB[l][0])
                    ts(DVE, mst[l][0][:], mst[l][0][:], keep[0:8, 0:1], None, ALU.mult, None,
                       [mstB[l][0], constB], [mstB[l][0]])
            last_prompt = (ti == NT_P - 1)
            if last_prompt or subs[0]["kind"] == "s":
                for l in range(2):
                    for sub in (subs if subs[0]["kind"] == "s" else subs[:1]):
                        si = sub["st"]
                        if sub["kind"] == "p":
                            dC, dn, dm = pC_out[l], pn_out[l], pm_out[l]
                        else:
                            dC, dn, dm = sC_out[l, sub["si"]], sn_out[l, sub["si"]], sm_out[l, sub["si"]]
                        P.dma(ACT, lambda e, dC=dC, l=l, si=si: e.dma_start(
                            out=dC.rearrange("j q e -> q j e"), in_=Cst[l][si][:, :, 0:128]),
                            reads=CstB[l][si], final=True)
                        P.dma(ACT, lambda e, dn=dn, l=l, si=si: e.dma_start(
                            out=dn.rearrange("j (q o) -> q j o", o=1), in_=Cst[l][si][:, :, 128:129],
                            allow_slow_non_contiguous=True), reads=CstB[l][si], final=True)
                        P.dma(ACT, lambda e, dm=dm, l=l, si=si: e.dma_start(
                            out=dm.rearrange("(h o) -> h o", o=1), in_=mst[l][si][:],
                            allow_slow_non_contiguous=True), reads=[mstB[l][si]], final=True)

        P.emit(block, sems, dma_sems)
    return nc


def make_in_maps(inputs, NT_P, n_cores=8, n_samp=2, NPRE=0):
    c = _consts()
    f = lambda a: np.ascontiguousarray(np.asarray(a, dtype=np.float32))
    xp = f(inputs["x_prompt"])
    xs = f(inputs["x_sample"])
    sC, sn, sm = f(inputs["state_C"]), f(inputs["state_n"]), f(inputs["state_m"])
    ck, cv = f(inputs["cache_k"]), f(inputs["cache_v"])
    shared = {k: f(inputs[k]) for k in ["w_in_a", "w_out_a", "w_kv", "w_q_b", "w_out_b", "w_gu", "w_down"]}
    shared["bgate"] = f(np.broadcast_to(inputs["b_gate_a"][:, None, :], (2, 128, 16)))
    shared["gnT"] = f(np.asarray(inputs["g_norm_a"]).reshape(2, 8, 128).transpose(0, 2, 1))
    shared["sinksrep"] = f(np.broadcast_to(inputs["sinks_b"][:, None, :], (2, 128, 16)))
    shared["lng"] = f(np.asarray(inputs["ln_g"]).reshape(4, 2, 8, 128).transpose(0, 1, 3, 2))
    shared["lnb"] = f(np.asarray(inputs["ln_b"]).reshape(4, 2, 8, 128).transpose(0, 1, 3, 2))
    for k, v in c.items():
        shared["c_" + k] = v
    maps = []
    nb = xp.shape[0]
    NOWN = (NT_P - NPRE) * 512
    for core in range(n_cores):
        m = dict(shared)
        if NPRE:
            b, half = core // 2, core % 2
            if half == 1:
                m["xin"] = f(xp[b, :NT_P * 512])
            else:
                m["xin"] = f(np.concatenate([np.zeros((NPRE * 512, D), np.float32), xp[b, :NOWN]], 0))
            m["keep"] = np.full((128, 1), float(half), np.float32)
        else:
            b = core % nb
            m["xin"] = f(xp[b, :NT_P * 512])
            m["keep"] = np.ones((128, 1), np.float32)
        sl = slice(core * n_samp, (core + 1) * n_samp)
        m["xs"] = f(xs[sl])
        m["stC"] = f(sC[:, sl].reshape(2, n_samp, 4, 128, 128))
        m["stn"] = f(sn[:, sl].reshape(2, n_samp, 4, 128))
        m["stm"] = f(sm[:, sl])
        m["ck"] = f(ck[sl].reshape(n_samp, 128, 128))
        m["cv"] = f(cv[sl].reshape(n_samp, 128, 128))
        maps.append(m)
    return maps


def assemble(results, NT_P, nb=4, n_cores=8, n_samp=2, NPRE=0):
    if NPRE:
        y = np.stack([np.concatenate([results[2 * b]["y"], results[2 * b + 1]["y"]], 0) for b in range(nb)], 0)
        pc = [2 * b + 1 for b in range(nb)]
    else:
        y = np.stack([results[b]["y"] for b in range(nb)], 0)
        pc = list(range(nb))
    ys = np.concatenate([results[c]["ys"] for c in range(n_cores)], 0)
    pC = np.stack([results[c]["pC"].reshape(2, 8, 64, 128) for c in pc], 1)
    pn = np.stack([results[c]["pn"].reshape(2, 8, 64) for c in pc], 1)
    pm = np.stack([results[c]["pm"] for c in pc], 1)
    pk = np.stack([results[c]["pk"].reshape(128, 2, 64) for c in pc], 0)
    pv = np.stack([results[c]["pv"].reshape(128, 2, 64) for c in pc], 0)
    sC = np.concatenate([results[c]["sC"].reshape(2, n_samp, 8, 64, 128) for c in range(n_cores)], 1)
    sn = np.concatenate([results[c]["sn"].reshape(2, n_samp, 8, 64) for c in range(n_cores)], 1)
    sm = np.concatenate([results[c]["sm"] for c in range(n_cores)], 1)
    sk = np.concatenate([results[c]["sk"].reshape(n_samp, 32, 2, 64) for c in range(n_cores)], 0)
    sv = np.concatenate([results[c]["sv"].reshape(n_samp, 32, 2, 64) for c in range(n_cores)], 0)
    outs = (y, ys, pC, pn, pm, pk, pv, sC, sn, sm, sk, sv)
    return tuple(np.ascontiguousarray(o, dtype=np.float32) for o in outs)


def kernel(**inputs):
    NT_P, NPRE = 16, 8
    nc = build(NT_P, NPRE=NPRE)
    maps = make_in_maps(inputs, NT_P, NPRE=NPRE)
    res = run_bass_kernel_spmd(nc, maps, core_ids=list(range(8)))
    return assemble(res.results, NT_P, NPRE=NPRE)
```

```python
import math
from contextlib import ExitStack
import numpy as np
import concourse.bass as bass
import concourse.mybir as mybir
from concourse.bass_utils import run_bass_kernel_spmd

F32 = mybir.dt.float32
BF16 = mybir.dt.bfloat16
AF = mybir.ActivationFunctionType
ALU = mybir.AluOpType
AX = mybir.AxisListType

PE, ACT, DVE, POOL, SP = "pe", "act", "dve", "pool", "sp"
ENGS = (PE, ACT, DVE, POOL, SP)
DMA_SEMS = {SP: 12, POOL: 10, ACT: 6}

D = 1024
KC = 8
DFF = 2816
NF = 22
ALPHA = 8.0 ** 0.25
LNENG = "dve"
import os as _os
YQ = _os.environ.get("K_YQ", "act")
BCAST = bool(int(_os.environ.get("K_BC", "1")))
UVENG = _os.environ.get("K_UVENG", "pool")
LN_EPS = 1e-5
HN_EPS = 1e-6
NEG = -30000.0


class Buf:
    __slots__ = ("w", "r", "name", "excl")

    def __init__(self, name="", excl=False):
        self.w = {}
        self.r = {}
        self.name = name
        self.excl = excl


class Ins:
    __slots__ = ("fn", "waits", "dma")

    def __init__(self, fn):
        self.fn = fn
        self.waits = []
        self.dma = None


class Prog:
    def __init__(self):
        self.streams = {e: [] for e in ENGS}
        self.waited = {e: {} for e in ENGS}
        self.needed = {e: set() for e in ENGS}
        self.dma_next = {q: 0 for q in DMA_SEMS}
        self.dma_val = {q: [0] * n for q, n in DMA_SEMS.items()}
        self.final_tokens = []

    def _deps(self, me, reads, writes):
        deps = {}
        for b in reads:
            for k, v in b.w.items():
                if deps.get(k, -1) < v:
                    deps[k] = v
            if b.excl:
                for k, v in b.r.items():
                    if k != me and deps.get(k, -1) < v:
                        deps[k] = v
        for b in writes:
            for k, v in b.w.items():
                if (k != me or me != PE) and deps.get(k, -1) < v:
                    deps[k] = v
            for k, v in b.r.items():
                if (k != me or me != PE) and deps.get(k, -1) < v:
                    deps[k] = v
        return deps

    def _commit(self, eng, ins, deps):
        wd = self.waited[eng]
        for k, v in deps.items():
            if wd.get(k, -1) < v:
                wd[k] = v
                ins.waits.append((k, v))
                if not isinstance(k, tuple):
                    self.needed[k].add(v)

    def op(self, eng, fn, reads=(), writes=()):
        ins = Ins(fn)
        n = len(self.streams[eng])
        self._commit(eng, ins, self._deps(eng, reads, writes))
        self.streams[eng].append(ins)
        for b in reads:
            b.r[eng] = n
        for b in writes:
            b.w[eng] = n
        return ins

    def dma(self, q, fn, reads=(), writes=(), final=False):
        ins = Ins(fn)
        si = self.dma_next[q]
        self.dma_next[q] = (si + 1) % DMA_SEMS[q]
        key = ("dma", q, si)
        deps = self._deps(key, reads, writes)
        pv = self.dma_val[q][si]
        if pv > 0 and deps.get(key, -1) < pv:
            deps[key] = pv
        self._commit(q, ins, deps)
        val = pv + 16
        self.dma_val[q][si] = val
        ins.dma = (key, val)
        self.streams[q].append(ins)
        for b in reads:
            b.r[key] = val
        for b in writes:
            b.w[key] = val
        if final:
            self.final_tokens.append((key, val))
        return ins

    def emit(self, block, sems, dma_sems):
        rank = {e: {s: i + 1 for i, s in enumerate(sorted(self.needed[e]))} for e in ENGS}
        handles = {PE: "tensor", ACT: "scalar", DVE: "vector", POOL: "gpsimd", SP: "sync"}
        prog = self

        def make(e):
            def body(eng):
                rk = rank[e]
                for n, ins in enumerate(prog.streams[e]):
                    for k, v in ins.waits:
                        if isinstance(k, tuple):
                            eng.wait_ge(dma_sems[k], v)
                        else:
                            eng.wait_ge(sems[k], rank[k][v])
                    inst = ins.fn(eng)
                    if ins.dma is not None:
                        inst.then_inc(dma_sems[ins.dma[0]], 16)
                    elif n in rk:
                        inst.then_inc(sems[e], 1)
                if e == SP:
                    for k, v in prog.final_tokens:
                        eng.wait_ge(dma_sems[k], v)
            return body
        for e in ENGS:
            getattr(block, handles[e])(make(e))


def _consts():
    c = {}
    c["ident"] = np.eye(128, dtype=np.float32)
    s = np.arange(128)
    c["tri"] = (s[:, None] <= s[None, :]).astype(np.float32)
    sel = np.zeros((8, 128), np.float32)
    for k in range(8):
        sel[k, (k % 2) * 64:(k % 2) * 64 + 64] = 1.0
    c["sel"] = sel
    pm = np.zeros((8, 4), np.float32)
    for k in range(8):
        pm[k, k // 2] = 1.0
    c["pairmask"] = pm
    c["ones8"] = np.ones((8, 128), np.float32)
    valid = (s < 32).astype(np.float32)
    c["valid2"] = np.stack([valid, (valid - 1.0) * 30000.0], 1).astype(np.float32)
    slopes = np.exp2(-8.0 * np.arange(1, 17, dtype=np.float64) / 16)
    bt = np.zeros((8, 128, 512), np.float32)
    k = np.arange(128)[:, None]
    q = np.arange(128)[None, :]
    for kb in range(2):
        kpos = k - 128 if kb == 0 else k
        dist = np.abs(q - kpos).astype(np.float64)
        cq = q // 64
        ck = (k // 64) - 2 if kb == 0 else (k // 64)
        vis = (ck >= cq - 2) & (ck <= cq)
        for kv in range(2):
            for p in range(2):
                for hh in range(4):
                    head = kv * 8 + 2 * hh + p
                    t = np.where(vis, -slopes[head] * dist, NEG)
                    bt[kb * 4 + kv * 2 + p][:, hh * 128:(hh + 1) * 128] = t.astype(np.float32)
    c["biasT"] = bt
    return c


CONST_SHAPES = {"ident": [128, 128], "tri": [128, 128], "sel": [8, 128], "pairmask": [8, 4],
                "ones8": [8, 128], "valid2": [128, 2], "biasT": [8, 128, 512]}


def build(NT_P, n_samp=2, NPRE=0):
    nc = bass.Bass("TRN2", target_bir_lowering=False)
    P = Prog()

    def din(name, shape, dt=F32):
        return nc.dram_tensor(name, list(shape), dt, kind="ExternalInput").ap()

    def dout(name, shape, dt=F32):
        return nc.dram_tensor(name, list(shape), dt, kind="ExternalOutput").ap()

    def dscr(name, shape, dt=BF16):
        return nc.dram_tensor(name, list(shape), dt, kind="Internal").ap()

    NTOK = NT_P * 512
    NOUT = (NT_P - NPRE) * 512
    keep_in = None
    xin = din("xin", [NTOK, D])
    xs_in = din("xs", [n_samp, 32, D])
    stC_in = din("stC", [2, n_samp, 4, 128, 128])
    stn_in = din("stn", [2, n_samp, 4, 128])
    stm_in = din("stm", [2, n_samp, 8])
    ck_in = din("ck", [n_samp, 128, 128])
    cv_in = din("cv", [n_samp, 128, 128])
    w_in_a = din("w_in_a", [2, D, 3088])
    w_out_a = din("w_out_a", [2, D, D])
    w_kv = din("w_kv", [D, 256])
    w_q_b = din("w_q_b", [2, D, D])
    w_out_b = din("w_out_b", [2, D, D])
    w_gu = din("w_gu", [4, D, 2 * DFF])
    w_down = din("w_down", [4, DFF, D])
    bgate_in = din("bgate", [2, 128, 16])
    gnT_in = din("gnT", [2, 128, 8])
    sinks_in = din("sinksrep", [2, 128, 16])
    lng_in = din("lng", [4, 2, 128, 8])
    lnb_in = din("lnb", [4, 2, 128, 8])
    cin = {k: din("c_" + k, shp) for k, shp in CONST_SHAPES.items()}

    y_out = dout("y", [NOUT, D])
    keep_in = din("keep", [128, 1])
    ys_out = dout("ys", [n_samp, 32, D])
    pC_out = dout("pC", [2, 4, 128, 128])
    pn_out = dout("pn", [2, 4, 128])
    pm_out = dout("pm", [2, 8])
    pk_out = dout("pk", [128, 128])
    pv_out = dout("pv", [128, 128])
    sC_out = dout("sC", [2, n_samp, 4, 128, 128])
    sn_out = dout("sn", [2, n_samp, 4, 128])
    sm_out = dout("sm", [2, n_samp, 8])
    sk_out = dout("sk", [n_samp, 32, 128])
    sv_out = dout("sv", [n_samp, 32, 128])

    wqk_b = dscr("wqk_b", [2, 2, 128, 8, 512])
    wtok_b = dscr("wtok_b", [2, 5, 128, 8, 512])
    wif_b = dscr("wif_b", [2, 128, 8, 16])
    wouta_b = dscr("wouta_b", [2, 2, 128, 8, 512])
    wkv_b = dscr("wkv_b", [128, 8, 256])
    wkd_b = dscr("wkd_b", [128, 8, 256])
    wq_b = dscr("wq_b", [2, 2, 128, 8, 512])
    woutb_b = dscr("woutb_b", [2, 2, 128, 8, 512])
    wgu_b = dscr("wgu_b", [4, 11, 128, 8, 512])
    wd_b = dscr("wd_b", [4, 6, 128, 8, 512])

    es = ExitStack()
    with es:
        def sb(name, shape, dt):
            return es.enter_context(nc.sbuf_tensor("sb_" + name, list(shape), dt))

        x32T = sb("x32T", [128, 8, 512], F32)
        xT = sb("xT", [128, 8, 512], BF16)
        NSLOT = 4
        wslots = [sb("wslot%d" % i, [128, 4096], BF16) for i in range(NSLOT)]
        wslotB = [Buf("wslot%d" % i) for i in range(NSLOT)]
        qkT = sb("qkT", [128, 8, 512], BF16)
        hT = sb("hT", [128, 8, 512], BF16)
        hffT = sb("hffT", [128, 1 if _os.environ.get("K_SHRINK") else NF, 512], BF16)
        ktok = [sb("ktok%d" % i, [128, 512], BF16) for i in range(4)]
        vraw = [sb("vraw%d" % i, [128, 1024], BF16) for i in range(4)]
        uv = [sb("uv%d" % i, [128, 8, 130], BF16) for i in range(4)]
        og = [sb("og%d" % i, [128, 1024], BF16) for i in range(4)]
        hn = [sb("hn%d" % i, [128, 1024], BF16) for i in range(4)]
        sq = sb("sq", [128, 8, 128], F32)
        f512 = [sb("f512_%d" % i, [128, 512], F32) for i in range(4)]
        b512 = [sb("b512_%d" % i, [128, 512], BF16) for i in range(4)]
        io = [sb("io%d" % i, [128, 1024], F32) for i in range(2)]
        PTST = sb("PTST", [128, 8, 512], BF16)
        PT = [PTST[:, i, :] for i in range(8)]
        ST = [PTST[:, 2 * i:2 * i + 2, :].rearrange("p a (h n) -> p (a h) n", h=4) for i in range(4)]
        kTd = sb("kTd", [128, 2, 5, 128], BF16)
        Vaug = sb("Vaug", [128, 5, 2, 66], BF16)
        kTd_s = sb("kTd_s", [128, 2, 4, 128], BF16)
        Vaug_s = sb("Vaug_s", [128, 4, 2, 66], BF16)
        kvtok = sb("kvtok", [128, 256], F32)
        biasT = sb("biasT", [128, 1 if _os.environ.get("K_SHRINK") else 8, 512], F32)
        NST = 1 + n_samp
        Cst = [[sb("Cst%d_%d" % (l, i), [128, 4, 129], F32) for i in range(NST)] for l in range(2)]
        mst = [[sb("mst%d_%d" % (l, i), [8, 1], F32) for i in range(NST)] for l in range(2)]
        C0b = [sb("C0b%d" % i, [128, 4, 130], BF16) for i in range(2)]
        NR = 4
        gat = [sb("gat%d" % i, [128, 16], F32) for i in range(NR)]
        e1 = [sb("e1_%d" % i, [128, 8], F32) for i in range(NR)]
        spt = [sb("sp%d" % i, [128, 8], F32) for i in range(NR)]
        an = [sb("an%d" % i, [128, 16], F32) for i in range(NR)]
        ul = [sb("ul%d" % i, [128, 16], F32) for i in range(NR)]
        tm16 = [sb("tm16_%d" % i, [128, 16], F32) for i in range(NR)]
        sm8 = [sb("sm8_%d" % i, [8, 16], F32) for i in range(NR)]
        dgG = [sb("dgG%d" % i, [8, 16], F32) for i in range(NR)]
        nrm = [sb("nrm%d" % i, [128, 48], F32) for i in range(NR)]
        dsbt = [sb("dsb%d" % i, [128, 4], F32) for i in range(NR)]
        nrm2 = [sb("nrm2_%d" % i, [128, 48], F32) for i in range(NR)]
        lnst = sb("lnst", [128, 8], F32)
        ident = sb("ident", [128, 128], F32)
        identb = sb("identb", [128, 128], BF16)
        tri = sb("tri", [128, 128], F32)
        trib = sb("trib", [128, 128], BF16)
        onesb = sb("onesb", [128, 128], BF16)
        sel = sb("sel", [8, 128], F32)
        pairmask = sb("pairmask", [8, 4], F32)
        ones8 = sb("ones8", [8, 128], F32)
        valid2 = sb("valid2", [128, 2], F32)
        bgate = sb("bgate", [128, 2, 16], F32)
        gnT = sb("gnT", [128, 2, 8], F32)
        esink = sb("esink", [128, 2, 16], F32)
        lng = sb("lng", [128, 8, 8], F32)
        lnb = sb("lnb", [128, 8, 8], F32)
        keep = sb("keep", [128, 1], F32)

        psum = es.enter_context(nc.psum_tensor("psum", [128, 8, 512], F32))
        sems = {e: es.enter_context(nc.semaphore("s_" + e)) for e in ENGS}
        dma_sems = {}
        for q, n in DMA_SEMS.items():
            for i in range(n):
                dma_sems[("dma", q, i)] = es.enter_context(nc.semaphore("d_%s%d" % (q, i)))
        block = es.enter_context(nc.Block())

        bankB = [Buf("bank%d" % i, excl=True) for i in range(8)]
        bank_state = {"next": 0, "pinned": set()}

        def bank():
            assert len(bank_state["pinned"]) < 8
            while True:
                b = bank_state["next"]
                bank_state["next"] = (b + 1) % 8
                if b not in bank_state["pinned"]:
                    return b

        def pin(b):
            bank_state["pinned"].add(b)

        def unpin(b):
            bank_state["pinned"].discard(b)

        class Ring:
            def __init__(self, tiles, name):
                self.tiles = tiles
                self.bufs = [Buf("%s%d" % (name, i)) for i in range(len(tiles))]
                self.i = 0

            def next(self):
                i = self.i
                self.i = (i + 1) % len(self.tiles)
                return self.tiles[i], self.bufs[i]

        R = {}
        for nm, tl in [("ktok", ktok), ("vraw", vraw), ("uv", uv), ("og", og), ("hn", hn),
                       ("f512", f512), ("b512", b512), ("io", io), ("C0b", C0b), ("gat", gat), ("e1", e1),
                       ("sp", spt), ("an", an), ("ul", ul), ("tm16", tm16), ("sm8", sm8), ("dgG", dgG),
                       ("nrm", nrm), ("PT", PT), ("dsb", dsbt), ("nrm2", nrm2)]:
            R[nm] = Ring(tl, nm)
        sqB = Buf("sq")

        class STRing:
            def __init__(self):
                self.i = 0

            def next(self):
                i = self.i
                self.i = (i + 1) % 4
                return ST[i], [R["PT"].bufs[2 * i], R["PT"].bufs[2 * i + 1]]
        STring = STRing()
        x32B = [Buf("x32_%d" % m) for m in range(8)]
        xTB = [Buf("xT_%d" % m) for m in range(8)]
        qkB = [Buf("qk_%d" % m) for m in range(8)]
        hTB = [Buf("hT_%d" % s) for s in range(4)]
        hffB = [Buf("hff_%d" % f) for f in range(NF)]
        kTdB = Buf("kTd")
        VaugB = Buf("Vaug")
        kTdsB = Buf("kTd_s")
        VaugsB = Buf("Vaug_s")
        kvtokB = Buf("kvtok")
        biasB = Buf("biasT")
        constB = Buf("const")
        CstB = [[[Buf("Cst%d" % q) for q in range(8)] for _ in range(NST)] for _ in range(2)]
        mstB = [[Buf("mst") for _ in range(NST)] for _ in range(2)]
        lnstB = Buf("lnst")
        wscrB = {}

        wstate = {"i": 0}

        def wload(src, view, name):
            i = wstate["i"]
            wstate["i"] = (i + 1) % NSLOT
            dst = view(wslots[i])
            P.dma(SP, lambda e, dst=dst, src=src: e.dma_start(out=dst, in_=src),
                  reads=[wscrB[name]],
                  writes=[wslotB[i]])
            return dst, wslotB[i]

        def v8(n):
            return lambda t: t[:, 0:8 * n].rearrange("p (c n) -> p c n", c=8)

        def mm(out, lhsT, rhs, start, stop, reads, writes):
            P.op(PE, lambda e: e.matmul(out, lhsT=lhsT, rhs=rhs, start=start, stop=stop),
                 reads=reads, writes=writes)

        def act(out, in_, func, reads, writes, **kw):
            P.op(ACT, lambda e: e.activation(out=out, in_=in_, func=func, **kw), reads=reads, writes=writes)

        def tt(eng, out, in0, in1, op, reads, writes):
            P.op(eng, lambda e: e.tensor_tensor(out=out, in0=in0, in1=in1, op=op), reads=reads, writes=writes)

        def ts(eng, out, in0, s1, s2, op0, op1, reads, writes):
            if op1 is None:
                P.op(eng, lambda e: e.tensor_scalar(out=out, in0=in0, scalar1=s1, scalar2=None, op0=op0),
                     reads=reads, writes=writes)
            else:
                P.op(eng, lambda e: e.tensor_scalar(out=out, in0=in0, scalar1=s1, scalar2=s2, op0=op0, op1=op1),
                     reads=reads, writes=writes)

        def stt(out, in0, scalar, in1, op0, op1, reads, writes):
            P.op(DVE, lambda e: e.scalar_tensor_tensor(out=out, in0=in0, scalar=scalar, in1=in1, op0=op0, op1=op1),
                 reads=reads, writes=writes)

        def cp(eng, out, in_, reads, writes):
            if eng == ACT:
                act(out, in_, AF.Copy, reads, writes)
            else:
                P.op(eng, lambda e: e.tensor_copy(out=out, in_=in_), reads=reads, writes=writes)

        def conv(dst, src, name):
            b = wscrB.setdefault(name, Buf(name))
            P.dma(POOL, lambda e: e.dma_start(out=dst, in_=src), writes=[b])

        def cblk(src2d):
            return src2d.rearrange("(c p) n -> p c n", p=128)

        def conv_layer_a(l):
            for i in range(2):
                conv(wqk_b[l, i], cblk(w_in_a[l][:, i * 512:(i + 1) * 512]), "wqk_b")
            conv(wif_b[l], cblk(w_in_a[l][:, 3072:3088]), "wif_b")
            for i in range(5):
                conv(wtok_b[l, i], cblk(w_in_a[l][:, 512 + i * 512:1024 + i * 512]), "wtok_b")
            for i in range(2):
                conv(wouta_b[l, i], cblk(w_out_a[l][:, i * 512:(i + 1) * 512]), "wouta_b")

        def conv_layer_b(j):
            if j == 0:
                conv(wkv_b, cblk(w_kv), "wkv_b")
                for kv in range(2):
                    for dup in range(2):
                        conv(wkd_b[:, :, kv * 128 + dup * 64:kv * 128 + dup * 64 + 64],
                             cblk(w_kv[:, kv * 64:(kv + 1) * 64]), "wkd_b")
            for i in range(2):
                conv(wq_b[j, i], cblk(w_q_b[j][:, i * 512:(i + 1) * 512]), "wq_b")
            for i in range(2):
                conv(woutb_b[j, i], cblk(w_out_b[j][:, i * 512:(i + 1) * 512]), "woutb_b")

        def conv_ffn(l):
            for b in range(11):
                conv(wgu_b[l, b][:, :, 0:256], cblk(w_gu[l][:, b * 256:(b + 1) * 256]), "wgu_b")
                conv(wgu_b[l, b][:, :, 256:512], cblk(w_gu[l][:, DFF + b * 256:DFF + (b + 1) * 256]), "wgu_b")
            for nh in range(2):
                for fb in range(3):
                    nfc = 8 if fb < 2 else 6
                    conv(wd_b[l, nh * 3 + fb][:, 0:nfc, :],
                         w_down[l][fb * 1024:fb * 1024 + nfc * 128, nh * 512:(nh + 1) * 512]
                         .rearrange("(c p) n -> p c n", p=128), "wd_b")

        def ld(q, dst, src, b):
            P.dma(q, lambda e: e.dma_start(out=dst, in_=src), writes=[b])

        ld(SP, ident[:], cin["ident"], constB)
        ld(SP, tri[:], cin["tri"], constB)
        ld(SP, sel[:], cin["sel"], constB)
        ld(SP, pairmask[:], cin["pairmask"], constB)
        ld(SP, ones8[:], cin["ones8"], constB)
        ld(SP, valid2[:], cin["valid2"], constB)
        if not _os.environ.get("K_SHRINK"):
            ld(SP, biasT[:], cin["biasT"].rearrange("g p n -> p g n"), biasB)
        ld(SP, bgate[:], bgate_in.rearrange("l p n -> p l n"), constB)
        ld(SP, gnT[:], gnT_in.rearrange("l p n -> p l n"), constB)
        ld(SP, esink[:], sinks_in.rearrange("l p n -> p l n"), constB)
        ld(SP, lng[:], lng_in.rearrange("l i p n -> p (l i) n"), constB)
        ld(SP, lnb[:], lnb_in.rearrange("l i p n -> p (l i) n"), constB)
        ld(SP, keep[:], keep_in, constB)
        cp(DVE, identb[:], ident[:], [constB], [constB])
        cp(DVE, trib[:], tri[:], [constB], [constB])
        P.op(DVE, lambda e: e.memset(onesb[:], 1.0 / 1024.0), writes=[constB])
        act(esink[:], esink[:], AF.Exp, [constB], [constB])
        for l in range(2):
            P.op(POOL, lambda e, l=l: e.memset(Cst[l][0][:], 0.0), writes=CstB[l][0])
            P.op(POOL, lambda e, l=l: e.memset(mst[l][0][:], 0.0), writes=[mstB[l][0]])
            for i in range(n_samp):
                P.dma(SP, lambda e, l=l, i=i: e.dma_start(out=Cst[l][1 + i][:, :, 0:128], in_=stC_in[l, i].rearrange("j q e -> q j e")), writes=CstB[l][1 + i])
                P.dma(SP, lambda e, l=l, i=i: e.dma_start(out=Cst[l][1 + i][:, :, 128:129],
                                                          in_=stn_in[l, i].rearrange("j (q o) -> q j o", o=1),
                                                          allow_slow_non_contiguous=True),
                      writes=CstB[l][1 + i])
                P.dma(SP, lambda e, l=l, i=i: e.dma_start(out=mst[l][1 + i][:],
                                                          in_=stm_in[l, i].rearrange("(h o) -> h o", o=1),
                                                          allow_slow_non_contiguous=True),
                      writes=[mstB[l][1 + i]])
        P.op(POOL, lambda e: e.memset(kTd[:], 0.0), writes=[kTdB])
        P.op(POOL, lambda e: e.memset(Vaug[:], 1.0), writes=[VaugB])
        P.op(POOL, lambda e: e.memset(Vaug[:, 0], 0.0), writes=[VaugB])
        P.op(POOL, lambda e: e.memset(Vaug_s[:], 1.0), writes=[VaugsB])
        for i in range(4):
            P.op(POOL, lambda e, i=i: e.memset(uv[i][:], 0.0), writes=[R["uv"].bufs[i]])

        conv_layer_a(0)
        conv_ffn(0)

        def feat_mm_group(wv, wB, m_lo, xsrc, xB, TT, kc=8):
            bk = bank()
            for c in range(kc):
                mm(psum[:, bk, 0:TT], wv[:, c, m_lo:m_lo + 128], xsrc[:, c, 0:TT], c == 0, c == kc - 1,
                   [wB] + xB, [bankB[bk]])
            return bk

        def layer_norm(li, TT, mix_banks_fn, release=None, LAG=1):
            release_ln()
            bm = bank()
            pin(bm)
            be = bank()
            pin(be)
            pend = []

            def flush(keep):
                while len(pend) > keep:
                    m, zb, zbB, zq, zqB = pend.pop(0)
                    mm(psum[:, bm, 0:TT], onesb[:], zb[:, 0:TT], m == 0, m == 7, [constB, zbB], [bankB[bm]])
                    mm(psum[:, be, 0:TT], onesb[:], zq[:, 0:TT], m == 0, m == 7, [constB, zqB], [bankB[be]])
            for m in range(8):
                bk = mix_banks_fn(m)
                stt(x32T[:, m, 0:TT], x32T[:, m, 0:TT], ALPHA, psum[:, bk, 0:TT], ALU.mult, ALU.add,
                    [x32B[m], bankB[bk]], [x32B[m]])
                if release is not None:
                    release(m)
                zb, zbB = R["b512"].next()
                act(zb[:, 0:TT], x32T[:, m, 0:TT], AF.Copy, [x32B[m]], [zbB])
                zq, zqB = R["b512"].next()
                act(zq[:, 0:TT], x32T[:, m, 0:TT], AF.Square, [x32B[m]], [zqB])
                pend.append((m, zb, zbB, zq, zqB))
                flush(LAG)
            flush(0)
            msq, msqB = R["f512"].next()
            act(msq[:, 0:TT], psum[:, bm, 0:TT], AF.Square, [bankB[bm]], [msqB])
            var, varB = R["f512"].next()
            tt(DVE, var[:, 0:TT], psum[:, be, 0:TT], msq[:, 0:TT], ALU.subtract, [bankB[be], msqB], [varB])
            ts(DVE, var[:, 0:TT], var[:, 0:TT], 0.0, LN_EPS, ALU.max, ALU.add, [varB], [varB])
            act(var[:, 0:TT], var[:, 0:TT], AF.Ln, [varB], [varB])
            act(psum[:, be, 0:TT], var[:, 0:TT], AF.Exp, [varB], [bankB[be]], scale=-0.5)
            for m in range(8):
                t1, t1B = R["f512"].next()
                tt(DVE, t1[:, 0:TT], x32T[:, m, 0:TT], psum[:, bm, 0:TT], ALU.subtract, [x32B[m], bankB[bm]], [t1B])
                tt(DVE, t1[:, 0:TT], t1[:, 0:TT], psum[:, be, 0:TT], ALU.mult, [t1B, bankB[be]], [t1B])
                ts(LNENG, x32T[:, m, 0:TT], t1[:, 0:TT], lng[:, li, m:m + 1], lnb[:, li, m:m + 1], ALU.mult, ALU.add,
                   [t1B, constB], [x32B[m]])
                act(xT[:, m, 0:TT], t1[:, 0:TT], AF.Identity, [t1B, constB], [xTB[m]],
                    scale=lng[:, li, m:m + 1], bias=lnb[:, li, m:m + 1])
            LNPIN.extend([bm, be])

        LNPIN = []

        def release_ln():
            while LNPIN:
                unpin(LNPIN.pop())

        def couter(mmfns):
            bks = [bank() for _ in mmfns]
            release_ln()
            for c in range(8):
                for fn, bk in zip(mmfns, bks):
                    fn(c, bk)
            return bks

        def out_proj_ln(wblocks, wname, li, TT):
            loaded = {}

            def get(m):
                i = m // 4
                if i not in loaded:
                    loaded[i] = wload(wblocks[i], v8(512), wname)
                wv, wB = loaded[i]
                return feat_mm_group(wv, wB, (m % 4) * 128, hT, hTB, TT)
            layer_norm(li, TT, get)

        def ffn(l, TT):
            def gu_evac(f, bg, bu):
                sg, sgB = R["f512"].next()
                act(sg[:, 0:TT], psum[:, bg, 0:TT], AF.Silu, [bankB[bg]], [sgB])
                tt(DVE, hffT[:, f, 0:TT], sg[:, 0:TT], psum[:, bu, 0:TT], ALU.mult, [sgB, bankB[bu]], [hffB[f]])

            blocks = {}

            def blk(b):
                if b not in blocks:
                    blocks[b] = wload(wgu_b[l, b], v8(512), "wgu_b")
                return blocks[b]

            def gu_mm(f, col0):
                b, ff = f // 2, f % 2
                wv, wB = blk(b)

                def fn(c, bk):
                    mm(psum[:, bk, 0:TT], wv[:, c, col0 + ff * 128:col0 + ff * 128 + 128], xT[:, c, 0:TT],
                       c == 0, c == 7, [wB, xTB[c]], [bankB[bk]])
                return fn
            blk(0)
            blk(1)
            bks = couter([gu_mm(f, col0) for f in range(3) for col0 in (0, 256)])
            for f in range(3):
                gu_evac(f, bks[2 * f], bks[2 * f + 1])
            for f in range(3, NF):
                wv, wB = blk(f // 2)
                ff = f % 2
                bg = feat_mm_group(wv, wB, ff * 128, xT, xTB, TT)
                bu = feat_mm_group(wv, wB, 256 + ff * 128, xT, xTB, TT)
                gu_evac(f, bg, bu)
            banks = {}

            def down_half(nh):
                bks = [bank() for _ in range(4)]
                for b in bks:
                    pin(b)
                for fb in range(3):
                    nfc = 8 if fb < 2 else 6
                    wv, wB = wload(wd_b[l, nh * 3 + fb][:, 0:nfc, :],
                                   lambda t, nfc=nfc: t[:, 0:nfc * 512].rearrange("p (c n) -> p c n", c=nfc), "wd_b")
                    for mi in range(4):
                        for fc in range(nfc):
                            f = fb * 8 + fc
                            mm(psum[:, bks[mi], 0:TT], wv[:, fc, mi * 128:(mi + 1) * 128], hffT[:, f, 0:TT],
                               f == 0, f == NF - 1, [wB, hffB[f]], [bankB[bks[mi]]])
                for mi in range(4):
                    banks[nh * 4 + mi] = bks[mi]

            def get(m):
                if m % 4 == 0:
                    down_half(m // 4)
                return banks[m]

            layer_norm(l * 2 + 1, TT, get, release=lambda m: unpin(banks[m]))

        def load_tile_x(subs, TT):
            release_ln()
            for s, sub in enumerate(subs):
                xt, xtB = R["io"].next()
                if sub["kind"] == "p":
                    r0 = sub["row0"]
                    P.dma(SP, lambda e, xt=xt, r0=r0: e.dma_start(out=xt[:], in_=xin[r0:r0 + 128, :]), writes=[xtB])
                else:
                    P.op(DVE, lambda e, xt=xt: e.memset(xt[:], 0.0), writes=[xtB])
                    i = sub["si"]
                    P.dma(SP, lambda e, xt=xt, i=i: e.dma_start(out=xt[0:32, :], in_=xs_in[i]), writes=[xtB])
                for g in range(2):
                    bk = bank()
                    for cc in range(4):
                        c = g * 4 + cc
                        P.op(PE, lambda e, bk=bk, cc=cc, c=c, xt=xt: e.transpose(
                            out=psum[:, bk, cc * 128:(cc + 1) * 128], in_=xt[:, c * 128:(c + 1) * 128],
                            identity=ident[:]), reads=[xtB, constB], writes=[bankB[bk]])
                    src = psum[:, bk, :].rearrange("p (c n) -> p c n", c=4)
                    if not _os.environ.get("K_NOACT"):
                        act(x32T[:, g * 4:(g + 1) * 4, s * 128:(s + 1) * 128], src, AF.Copy, [bankB[bk]],
                            x32B[g * 4:(g + 1) * 4])
                    if not _os.environ.get("K_NODVE"):
                        cp(_os.environ.get("K_E2", DVE), xT[:, g * 4:(g + 1) * 4, s * 128:(s + 1) * 128], src, [bankB[bk]], xTB[g * 4:(g + 1) * 4])

        def store_tile_y(subs, TT):
            release_ln()
            for s, sub in enumerate(subs):
                if sub.get("discard"):
                    continue
                yt = hffT[:, 4 * s:4 * s + 4, :].rearrange("p f n -> p (f n)").bitcast(F32)
                ytBs = hffB[4 * s:4 * s + 4]
                for g in range(2):
                    bk = bank()
                    for cc in range(4):
                        c = g * 4 + cc
                        P.op(PE, lambda e, bk=bk, cc=cc, c=c, s=s: e.transpose(
                            out=psum[:, bk, cc * 128:(cc + 1) * 128], in_=x32T[:, c, s * 128:(s + 1) * 128],
                            identity=ident[:]), reads=[x32B[c], constB], writes=[bankB[bk]])
                    if g == 0:
                        act(yt[:, 0:512], psum[:, bk, :], AF.Copy, [bankB[bk]], ytBs[0:2])
                    else:
                        cp(DVE, yt[:, 512:1024], psum[:, bk, :], [bankB[bk]], ytBs[2:4])
                if sub["kind"] == "p":
                    r0 = sub["yrow0"]
                    P.dma(YQ, lambda e, yt=yt, r0=r0: e.dma_start(out=y_out[r0:r0 + 128, :], in_=yt[:]),
                          reads=ytBs, final=True)
                else:
                    i = sub["si"]
                    P.dma(YQ, lambda e, yt=yt, i=i: e.dma_start(out=ys_out[i], in_=yt[0:32, :]),
                          reads=ytBs, final=True)

        def mlstm_layer(l, subs, TT, state_only=False):
            NS = len(subs)
            if state_only:
                release_ln()
            else:
                wq2 = [wload(wqk_b[l, i], v8(512), "wqk_b") for i in range(2)]

                def qk_mm(m):
                    wv, wB = wq2[m // 4]

                    def fn(c, bk):
                        mm(psum[:, bk, 0:TT], wv[:, c, (m % 4) * 128:(m % 4) * 128 + 128], xT[:, c, 0:TT],
                           c == 0, c == 7, [wB, xTB[c]], [bankB[bk]])
                    return fn

                def qk_evac(m, bk):
                    if m < 4:
                        act(qkT[:, m, 0:TT], psum[:, bk, 0:TT], AF.Copy, [bankB[bk]], [qkB[m]])
                    else:
                        act(qkT[:, m, 0:TT], psum[:, bk, 0:TT], AF.Copy, [bankB[bk]], [qkB[m]], scale=0.125)
                bks = couter([qk_mm(m) for m in range(6)])
                for m in range(6):
                    qk_evac(m, bks[m])
                for m in (6, 7):
                    wv, wB = wq2[1]
                    bk = feat_mm_group(wv, wB, (m % 4) * 128, xT, xTB, TT)
                    qk_evac(m, bk)
            wv, wB = wload(wif_b[l], v8(16), "wif_b")
            G = []
            G0 = []
            for s, sub in enumerate(subs):
                bk = bank()
                for c in range(8):
                    mm(psum[:, bk, 0:16], xT[:, c, s * 128:(s + 1) * 128], wv[:, c, :], c == 0, c == 7,
                       [wB] + xTB, [bankB[bk]])
                ga, gaB = R["gat"].next()
                tt(DVE, ga[:], psum[:, bk, 0:16], bgate[:, l, :], ALU.add, [bankB[bk], constB], [gaB])
                ee, eeB = R["e1"].next()
                act(ee[:], ga[:, 8:16], AF.Exp, [gaB], [eeB], scale=-1.0)
                sp_, spB = R["sp"].next()
                act(sp_[:], ee[:], AF.Ln, [eeB], [spB], bias=1.0)
                if sub["kind"] == "s":
                    ts(DVE, sp_[:], sp_[:], valid2[:, 0:1], None, ALU.mult, None, [spB, constB], [spB])
                    ts(DVE, ga[:, 0:8], ga[:, 0:8], valid2[:, 0:1], valid2[:, 1:2], ALU.mult, ALU.add,
                       [gaB, constB], [gaB])
                G0.append((ga, gaB, sp_, spB))
            KT, VR, OG = [None] * NS, [None] * NS, [None] * NS
            if not state_only:
                for s in range(NS):
                    bk = bank()
                    pbk = psum[:, bk, :].bitcast(BF16)
                    for jj in range(4):
                        P.op(PE, lambda e, pbk=pbk, jj=jj, s=s: e.transpose(
                            out=pbk[:, jj * 128:(jj + 1) * 128], in_=qkT[:, 4 + jj, s * 128:(s + 1) * 128],
                            identity=identb[:]), reads=[qkB[4 + jj], constB], writes=[bankB[bk]])
                    KT[s] = R["ktok"].next()
                    act(KT[s][0][:], pbk[:, 0:512], AF.Copy, [bankB[bk]], [KT[s][1]])
            for n in range(0 if state_only else 1, 3 if state_only else 5):
                wv, wB = wload(wtok_b[l, n], v8(512), "wtok_b")
                for s in range(NS):
                    bk = bank()
                    for c in range(8):
                        mm(psum[:, bk, :], xT[:, c, s * 128:(s + 1) * 128], wv[:, c, :], c == 0, c == 7,
                           [wB] + xTB, [bankB[bk]])
                    if n == 0:
                        KT[s] = R["ktok"].next()
                        act(KT[s][0][:], psum[:, bk, :], AF.Copy, [bankB[bk]], [KT[s][1]], scale=0.125)
                    elif n in (1, 2):
                        if n == 1:
                            VR[s] = R["vraw"].next()
                        cp(DVE, VR[s][0][:, (n - 1) * 512:n * 512], psum[:, bk, :], [bankB[bk]], [VR[s][1]])
                    else:
                        if n == 3:
                            OG[s] = R["og"].next()
                        act(OG[s][0][:, (n - 3) * 512:(n - 2) * 512], psum[:, bk, :], AF.Sigmoid, [bankB[bk]],
                            [OG[s][1]])
                    if NS > 2 and n == 0 and s >= 1:
                        pass
            for s, sub in enumerate(subs):
                ga, gaB, sp_, spB = G0[s]
                bk2 = bank()
                mm(psum[:, bk2, 0:8], tri[:], sp_[:], True, True, [constB, spB], [bankB[bk2]])
                a_, aB = R["an"].next()
                tt(DVE, a_[:, 0:8], ga[:, 0:8], psum[:, bk2, 0:8], ALU.add, [gaB, bankB[bk2]], [aB])
                cp(DVE, a_[:, 8:16], psum[:, bk2, 0:8], [bankB[bk2]], [aB])
                bk3 = bank()
                P.op(PE, lambda e, bk3=bk3, a_=a_: e.transpose(out=psum[0:8, bk3, 0:128], in_=a_[:, 0:8],
                                                               identity=ident[:]),
                     reads=[aB, constB], writes=[bankB[bk3]])
                P.op(PE, lambda e, bk3=bk3, a_=a_: e.transpose(out=psum[0:8, bk3, 128:256], in_=a_[:, 8:16],
                                                               identity=ident[:]),
                     reads=[aB, constB], writes=[bankB[bk3]])
                s8, s8B = R["sm8"].next()
                P.op(DVE, lambda e, s8=s8, bk3=bk3: e.tensor_reduce(out=s8[:, 0:1], in_=psum[0:8, bk3, 0:128],
                                                                    axis=AX.X, op=ALU.max),
                     reads=[bankB[bk3]], writes=[s8B])
                cp(DVE, s8[:, 4:5], psum[0:8, bk3, 255:256], [bankB[bk3]], [s8B])
                G.append((a_, aB, s8, s8B))
            CH = [None] * NS

            def phase_ab(s):
                sub = subs[s]
                si = sub["st"]
                M = mst[l][si]
                MB = mstB[l][si]
                a_, aB, s8, s8B = G[s]
                tt(DVE, s8[:, 1:2], s8[:, 0:1], M[:], ALU.max, [s8B, MB], [s8B])
                tt(DVE, s8[:, 2:3], M[:], s8[:, 1:2], ALU.subtract, [s8B, MB], [s8B])
                tt(DVE, s8[:, 3:4], M[:], s8[:, 1:2], ALU.subtract, [s8B, MB], [s8B])
                tt(DVE, M[:], s8[:, 1:2], s8[:, 4:5], ALU.subtract, [s8B], [MB])
                dg, dgB = R["dgG"].next()
                act(dg[:, 12:14], s8[:, 2:4], AF.Exp, [s8B], [dgB])
                ts(DVE, dg[:, 0:8], ident[0:8, 0:8], s8[:, 1:2], None, ALU.mult, None, [constB, s8B], [dgB])
                ts(DVE, dg[:, 8:12], pairmask[:], dg[:, 12:13], None, ALU.mult, None, [constB, dgB], [dgB])
                ch = {}
                st_banks = []
                if not state_only:
                    stt_, stBs = STring.next()
                    ch["ST"], ch["STB"] = stt_, stBs
                    for hg in range(2):
                        bk2 = bank()
                        for hh in range(4):
                            h = 2 * hh + hg
                            j, p = h // 2, h % 2
                            mm(psum[:, bk2, hh * 128:(hh + 1) * 128],
                               qkT[p * 64:(p + 1) * 64, 4 + j, s * 128:(s + 1) * 128],
                               qkT[p * 64:(p + 1) * 64, j, s * 128:(s + 1) * 128], True, True,
                               [qkB[4 + j], qkB[j]], [bankB[bk2]])
                        st_banks.append(bk2)
                bk = bank()
                mm(psum[:, bk, 0:8], ones8[:], dg[:, 0:8], True, True, [constB, dgB], [bankB[bk]])
                mm(psum[:, bk, 8:12], sel[:], dg[:, 8:12], True, True, [constB, dgB], [bankB[bk]])
                t16, t16B = R["tm16"].next()
                tt(DVE, t16[:, 0:8], a_[:, 0:8], psum[:, bk, 0:8], ALU.subtract, [aB, bankB[bk]], [t16B])
                tt(DVE, t16[:, 8:11], a_[:, 8:14:2], psum[:, bk, 0:6:2], ALU.subtract, [aB, bankB[bk]], [t16B])
                tt(DVE, t16[:, 11:14], a_[:, 9:15:2], psum[:, bk, 1:7:2], ALU.subtract, [aB, bankB[bk]], [t16B])
                tt(DVE, t16[:, 14:16], a_[:, 14:16], psum[:, bk, 6:8], ALU.subtract, [aB, bankB[bk]], [t16B])
                u_, uB = R["ul"].next()
                act(u_[:], t16[:], AF.Exp, [t16B], [uB])
                dsb, dsbB = R["dsb"].next()
                cp(DVE, dsb[:], psum[:, bk, 8:12], [bankB[bk]], [dsbB])
                for hg, bk2 in enumerate(st_banks):
                    if BCAST:
                        tt(DVE, stt_[:, hg:8:2, :], psum[:, bk2, :].rearrange("p (h n) -> p h n", h=4),
                           trib[:].unsqueeze(1).to_broadcast([128, 4, 128]), ALU.mult, [bankB[bk2], constB], stBs)
                    else:
                        for hh in range(4):
                            tt(DVE, stt_[:, 2 * hh + hg, :], psum[:, bk2, hh * 128:(hh + 1) * 128], trib[:],
                               ALU.mult, [bankB[bk2], constB], stBs)
                uvt, uvB = R["uv"].next()
                vr, vrB = VR[s]
                if BCAST:
                    tt(UVENG, uvt[:, :, 0:128], vr[:].rearrange("p (h n) -> p h n", h=8),
                       u_[:, 0:8].unsqueeze(2).to_broadcast([128, 8, 128]), ALU.mult, [vrB, uB], [uvB])
                else:
                    for h in range(8):
                        if h % 2 == 0:
                            act(uvt[:, h, 0:128], vr[:, h * 128:(h + 1) * 128], AF.Identity, [vrB, uB], [uvB],
                                scale=u_[:, h:h + 1])
                        else:
                            ts(DVE, uvt[:, h, 0:128], vr[:, h * 128:(h + 1) * 128], u_[:, h:h + 1], None, ALU.mult,
                               None, [vrB, uB], [uvB])
                cp(UVENG, uvt[:, :, 128:129], u_[:, 0:8].unsqueeze(2), [uB], [uvB])
                ch.update(dict(u=u_, uB=uB, dsb=dsb, dsbB=dsbB, uv=uvt, uvB=uvB))
                CH[s] = ch

            def transposes(s):
                hnt, hnB = CH[s]["hn"], CH[s]["hnB"]
                bk4 = bank()
                pb = psum[:, bk4, :].bitcast(BF16)
                for c in range(8):
                    P.op(PE, lambda e, pb=pb, c=c, hnt=hnt: e.transpose(
                        out=pb[:, c * 128:(c + 1) * 128], in_=hnt[:, c * 128:(c + 1) * 128], identity=identb[:]),
                        reads=[hnB, constB], writes=[bankB[bk4]])
                if BCAST:
                    tt(DVE, hT[:, :, s * 128:(s + 1) * 128], pb.rearrange("p (c n) -> p c n", c=8),
                       gnT[:, l, :].unsqueeze(2).to_broadcast([128, 8, 128]), ALU.mult,
                       [bankB[bk4], constB], [hTB[s]])
                else:
                    for c in range(8):
                        if c % 2 == 0:
                            act(hT[:, c, s * 128:(s + 1) * 128], pb[:, c * 128:(c + 1) * 128], AF.Identity,
                                [bankB[bk4], constB], [hTB[s]], scale=gnT[:, l, c:c + 1])
                        else:
                            ts(DVE, hT[:, c, s * 128:(s + 1) * 128], pb[:, c * 128:(c + 1) * 128],
                               gnT[:, l, c:c + 1], None, ALU.mult, None, [bankB[bk4], constB], [hTB[s]])

            def decay_cb(s):
                si = subs[s]["st"]
                C, CB = Cst[l][si], CstB[l][si]
                ch = CH[s]
                dsb, dsbB = ch["dsb"], ch["dsbB"]
                if BCAST:
                    tt(DVE, C[:], C[:], dsb[:].unsqueeze(2).to_broadcast([128, 4, 129]), ALU.mult, CB + [dsbB], CB)
                else:
                    for jj in range(4):
                        ts(DVE, C[:, jj, :], C[:, jj, :], dsb[:, jj:jj + 1], None, ALU.mult, None, CB + [dsbB], CB)
                ch["cb"], ch["cbB"] = None, None
                if not state_only:
                    cb, cbB = R["C0b"].next()
                    cp(ACT, cb[:, :, 0:129], C[:], CB, [cbB])
                    ch["cb"], ch["cbB"] = cb, cbB

            phase_ab(0)
            for s, sub in enumerate(subs):
                si = sub["st"]
                C = Cst[l][si]
                CB = CstB[l][si]
                ch = CH[s]
                u_, uB, dsb, dsbB, uvt, uvB = ch["u"], ch["uB"], ch["dsb"], ch["dsbB"], ch["uv"], ch["uvB"]
                if "cb" not in ch:
                    decay_cb(s)
                cb, cbB = ch["cb"], ch["cbB"]
                if s + 1 < NS:
                    phase_ab(s + 1)
                if not state_only:
                    stt_, stBs = ch["ST"], ch["STB"]
                    nr, nrB = R["nrm"].next()
                    hnt, hnB = R["hn"].next()
                    ch["hn"], ch["hnB"] = hnt, hnB
                    ogt, ogB = OG[s]
                    slot0 = 0
                    obanks = []
                    for grp in ((0, 2, 4), (1, 3, 5), (6,), (7,)):
                        bk3 = bank()
                        for gi, h in enumerate(grp):
                            j, p = h // 2, h % 2
                            o_ap = psum[:, bk3, gi * 129:(gi + 1) * 129]
                            mm(o_ap, qkT[p * 64:(p + 1) * 64, j, s * 128:(s + 1) * 128],
                               cb[p * 64:(p + 1) * 64, j, 0:129], True, False, [qkB[j], cbB], [bankB[bk3]])
                            mm(o_ap, stt_[:, h, :], uvt[:, h, 0:129], False, True, stBs + [uvB], [bankB[bk3]])
                        ng = len(grp)
                        h0 = slot0
                        slot0 += ng
                        obanks.append((bk3, grp, h0))
                        pv = psum[:, bk3, 0:ng * 129].rearrange("p (h n) -> p h n", h=ng)
                        act(nr[:, h0:h0 + ng].unsqueeze(2), pv[:, :, 128:129], AF.Abs, [bankB[bk3]], [nrB])
                        for gi, h in enumerate(grp):
                            P.op(ACT, lambda e, bk3=bk3, gi=gi, h0=h0, nr=nr: e.activation(
                                out=sq[:, h0 + gi, :], in_=psum[:, bk3, gi * 129:gi * 129 + 128], func=AF.Square,
                                accum_out=nr[:, 8 + h0 + gi:9 + h0 + gi]), reads=[bankB[bk3]], writes=[sqB, nrB])
                kt, ktB = KT[s]
                sbanks = []
                for j in range(4):
                    bk5 = bank()
                    mm(psum[:, bk5, 0:258], kt[:, j * 128:(j + 1) * 128],
                       uvt[:, 2 * j:2 * j + 2, 0:129], True, True, [ktB, uvB], [bankB[bk5]])
                    sbanks.append(bk5)
                for j, bk5 in enumerate(sbanks):
                    tt(DVE, C[0:64, j, :], C[0:64, j, :], psum[0:64, bk5, 0:129], ALU.add, [CB[2 * j], bankB[bk5]], [CB[2 * j]])
                    tt(DVE, C[64:128, j, :], C[64:128, j, :], psum[64:128, bk5, 129:258], ALU.add,
                       [CB[2 * j + 1], bankB[bk5]], [CB[2 * j + 1]])
                if s + 1 < NS:
                    decay_cb(s + 1)
                if not state_only:
                    nr2, nr2B = R["nrm2"].next()
                    tt(DVE, nr2[:, 16:24], nr[:, 0:8], u_[:, 8:16], ALU.max, [nrB, uB], [nr2B])
                    tt(DVE, nr2[:, 24:32], nr2[:, 16:24], nr2[:, 16:24], ALU.mult, [nr2B], [nr2B])
                    ts(DVE, nr2[:, 24:32], nr2[:, 24:32], HN_EPS, None, ALU.mult, None, [nr2B], [nr2B])
                    stt(nr2[:, 24:32], nr[:, 8:16], 1.0 / 128.0, nr2[:, 24:32], ALU.mult, ALU.add, [nrB, nr2B], [nr2B])
                    act(nr2[:, 32:40], nr2[:, 24:32], AF.Ln, [nr2B], [nr2B])
                    act(nr2[:, 40:48], nr2[:, 32:40], AF.Exp, [nr2B], [nr2B], scale=-0.5)
                    for bk3, grp, h0 in obanks:
                        for gi, h in enumerate(grp):
                            act(hnt[:, h * 128:(h + 1) * 128], psum[:, bk3, gi * 129:gi * 129 + 128], AF.Identity,
                                [bankB[bk3], nr2B], [hnB], scale=nr2[:, 40 + h0 + gi:41 + h0 + gi])
                    tt(POOL, hnt[:], hnt[:], ogt[:], ALU.mult, [hnB, ogB], [hnB])
                    if s >= 1:
                        transposes(s - 1)
            if not state_only:
                transposes(NS - 1)
            if state_only:
                return
            out_proj_ln(wouta_b[l], "wouta_b", l * 2, TT)

        def swa_layer(j, subs, TT, first_prompt_tile, halo_keep=False, kv_only=False):
            NS = len(subs)
            if j == 0:
                release_ln()
            sample = subs[0]["kind"] == "s"
            kT_ = kTd_s if sample else kTd
            kTB = kTdsB if sample else kTdB
            VA = Vaug_s if sample else Vaug
            VAB = VaugsB if sample else VaugB

            def blocks(s):
                return (2 * s, 2 * s + 1) if sample else (s, s + 1)
            if j == 0:
                if sample:
                    for s, sub in enumerate(subs):
                        i = sub["si"]
                        pb_, ob_ = blocks(s)
                        ct, ctB = R["io"].next()
                        P.dma(SP, lambda e, ct=ct, i=i: e.dma_start(out=ct[:, 0:128], in_=ck_in[i]), writes=[ctB])
                        P.dma(SP, lambda e, ct=ct, i=i: e.dma_start(out=ct[:, 128:256], in_=cv_in[i]), writes=[ctB])
                        kd, kdB = R["b512"].next()
                        for kv in range(2):
                            for dup in range(2):
                                cp(DVE, kd[:, kv * 128 + dup * 64:kv * 128 + dup * 64 + 64],
                                   ct[:, kv * 64:(kv + 1) * 64], [ctB], [kdB])
                        cp(DVE, VA[:, pb_, :, 0:64], ct[:, 128:256].rearrange("p (k d) -> p k d", k=2), [ctB], [VAB])
                        bk = bank()
                        pb = psum[:, bk, :].bitcast(BF16)
                        for kv in range(2):
                            P.op(PE, lambda e, pb=pb, kv=kv, kd=kd: e.transpose(
                                out=pb[:, kv * 128:(kv + 1) * 128], in_=kd[:, kv * 128:(kv + 1) * 128],
                                identity=identb[:]), reads=[kdB, constB], writes=[bankB[bk]])
                        cp(DVE, kT_[:, :, pb_, :], pb[:, 0:256].rearrange("p (k n) -> p k n", k=2),
                           [bankB[bk]], [kTB])
                elif not first_prompt_tile:
                    cp(DVE, kT_[:, :, 0, :], kT_[:, :, 4, :], [kTB], [kTB])
                    cp(DVE, VA[:, 0], VA[:, 4], [VAB], [VAB])
                    if halo_keep:
                        ts(DVE, VA[:, 0].rearrange("p k n -> p (k n)"), VA[:, 0].rearrange("p k n -> p (k n)"),
                           keep[:, 0:1], None, ALU.mult, None, [VAB, constB], [VAB])
                wv, wB = wload(wkv_b, v8(256), "wkv_b")
                for s, sub in enumerate(subs):
                    pb_, ob_ = blocks(s)
                    bk = bank()
                    for c in range(8):
                        mm(psum[:, bk, 0:256], xT[:, c, s * 128:(s + 1) * 128], wv[:, c, :], c == 0, c == 7,
                           [wB] + xTB, [bankB[bk]])
                    act(VA[:, ob_, :, 0:64], psum[:, bk, 128:256].rearrange("p (k d) -> p k d", k=2), AF.Copy,
                        [bankB[bk]], [VAB])
                    if sample:
                        P.op(POOL, lambda e, ob_=ob_: e.memset(VA[32:64, ob_], 0.0), writes=[VAB])
                        P.op(POOL, lambda e, ob_=ob_: e.memset(VA[64:128, ob_], 0.0), writes=[VAB])
                    if sample or sub.get("last"):
                        cp(DVE, kvtok[:], psum[:, bk, 0:256], [bankB[bk], kvtokB], [kvtokB])
                        if sample:
                            i = sub["si"]
                            P.dma(ACT, lambda e, i=i: e.dma_start(out=sk_out[i], in_=kvtok[0:32, 0:128]),
                                  reads=[kvtokB], final=True)
                            P.dma(ACT, lambda e, i=i: e.dma_start(out=sv_out[i], in_=kvtok[0:32, 128:256]),
                                  reads=[kvtokB], final=True)
                        else:
                            P.dma(ACT, lambda e: e.dma_start(out=pk_out, in_=kvtok[:, 0:128]),
                                  reads=[kvtokB], final=True)
                            P.dma(ACT, lambda e: e.dma_start(out=pv_out, in_=kvtok[:, 128:256]),
                                  reads=[kvtokB], final=True)
                wv, wB = wload(wkd_b, v8(256), "wkd_b")
                for kv in range(2):
                    bk = feat_mm_group(wv, wB, kv * 128, xT, xTB, TT)
                    if sample:
                        for s in range(NS):
                            act(kT_[:, kv, 2 * s + 1, :], psum[:, bk, s * 128:(s + 1) * 128], AF.Copy,
                                [bankB[bk]], [kTB])
                    else:
                        act(kT_[:, kv, 1:5, :], psum[:, bk, :].rearrange("p (b n) -> p b n", b=4), AF.Copy,
                            [bankB[bk]], [kTB])
            if kv_only:
                return
            wq2 = [wload(wq_b[j, i], v8(512), "wq_b") for i in range(2)]

            def q_mm(m):
                wv, wB = wq2[m // 4]

                def fn(c, bk):
                    mm(psum[:, bk, 0:TT], wv[:, c, (m % 4) * 128:(m % 4) * 128 + 128], xT[:, c, 0:TT],
                       c == 0, c == 7, [wB, xTB[c]], [bankB[bk]])
                return fn
            nco = 6 if j == 1 else 0
            if nco:
                bks = couter([q_mm(m) for m in range(nco)])
                for m in range(nco):
                    act(qkT[:, m, 0:TT], psum[:, bks[m], 0:TT], AF.Copy, [bankB[bks[m]]], [qkB[m]], scale=0.125)
            for m in range(nco, 8):
                wv, wB = wq2[m // 4]
                bk = feat_mm_group(wv, wB, (m % 4) * 128, xT, xTB, TT)
                act(qkT[:, m, 0:TT], psum[:, bk, 0:TT], AF.Copy, [bankB[bk]], [qkB[m]], scale=0.125)
            def scores(s):
                blk = blocks(s)
                PTs = {}
                for kb in range(2):
                    for kv in range(2):
                        for p in range(2):
                            bk = bank()
                            mm(psum[:, bk, :], kT_[p * 64:(p + 1) * 64, kv, blk[kb], :],
                               qkT[p * 64:(p + 1) * 64, kv * 4:(kv + 1) * 4, s * 128:(s + 1) * 128],
                               True, True, [kTB] + qkB[kv * 4:(kv + 1) * 4], [bankB[bk]])
                            tmp, tmpB = R["f512"].next()
                            tt(DVE, tmp[:], psum[:, bk, :], biasT[:, kb * 4 + kv * 2 + p, :], ALU.add,
                               [bankB[bk], biasB], [tmpB])
                            pt, ptB = R["PT"].next()
                            act(pt[:], tmp[:], AF.Exp, [tmpB], [ptB])
                            PTs[(kb, kv, p)] = (pt, ptB)
                return PTs

            def pv_norm(s, PTs):
                blk = blocks(s)
                ont, onB = R["hn"].next()
                nr, nrB = R["nrm"].next()
                for grp in (tuple(range(0, 7)), tuple(range(7, 14)), (14, 15)):
                    bk = bank()
                    for gi, head in enumerate(grp):
                        kv, g_ = head // 8, head % 8
                        hh, p = g_ // 2, g_ % 2
                        for kb in range(2):
                            pt, ptB = PTs[(kb, kv, p)]
                            mm(psum[:, bk, gi * 65:(gi + 1) * 65], pt[:, hh * 128:(hh + 1) * 128],
                               VA[:, blk[kb], kv, 0:65], kb == 0, kb == 1, [ptB, VAB], [bankB[bk]])
                    ng = len(grp)
                    h0 = grp[0]
                    pv = psum[:, bk, 0:ng * 65].rearrange("p (h n) -> p h n", h=ng)
                    tt(DVE, nr[:, h0:h0 + ng].unsqueeze(2), pv[:, :, 64:65], esink[:, j, h0:h0 + ng].unsqueeze(2),
                       ALU.add, [bankB[bk], constB], [nrB])
                    P.op(DVE, lambda e, nr=nr, h0=h0, ng=ng: e.reciprocal(out=nr[:, 16 + h0:16 + h0 + ng],
                                                                          in_=nr[:, h0:h0 + ng]),
                         reads=[nrB], writes=[nrB])
                    if BCAST:
                        tt(DVE, ont[:, h0 * 64:(h0 + ng) * 64].rearrange("p (h n) -> p h n", h=ng), pv[:, :, 0:64],
                           nr[:, 16 + h0:16 + h0 + ng].unsqueeze(2).to_broadcast([128, ng, 64]), ALU.mult,
                           [bankB[bk], nrB], [onB])
                    else:
                        for gi, head in enumerate(grp):
                            if head % 2 == 0:
                                ts(DVE, ont[:, head * 64:(head + 1) * 64], psum[:, bk, gi * 65:gi * 65 + 64],
                                   nr[:, 16 + head:17 + head], None, ALU.mult, None, [bankB[bk], nrB], [onB])
                            else:
                                act(ont[:, head * 64:(head + 1) * 64], psum[:, bk, gi * 65:gi * 65 + 64],
                                    AF.Identity, [bankB[bk], nrB], [onB], scale=nr[:, 16 + head:17 + head])
                return ont, onB

            def attn_transposes(s, ont, onB):
                bk4 = bank()
                pb = psum[:, bk4, :].bitcast(BF16)
                for c in range(8):
                    P.op(PE, lambda e, pb=pb, c=c, ont=ont: e.transpose(
                        out=pb[:, c * 128:(c + 1) * 128], in_=ont[:, c * 128:(c + 1) * 128], identity=identb[:]),
                        reads=[onB, constB], writes=[bankB[bk4]])
                act(hT[:, :, s * 128:(s + 1) * 128], pb.rearrange("p (c n) -> p c n", c=8), AF.Copy,
                    [bankB[bk4]], [hTB[s]])

            PTn = scores(0)
            for s in range(NS):
                ont, onB = pv_norm(s, PTn)
                if s + 1 < NS:
                    PTn = scores(s + 1)
                attn_transposes(s, ont, onB)
            out_proj_ln(woutb_b[j], "woutb_b", (2 + j) * 2, TT)

        tiles = []
        for t in range(NT_P):
            subs = [dict(kind="p", st=0, row0=t * 512 + s * 128, yrow0=(t - NPRE) * 512 + s * 128,
                         discard=(t < NPRE)) for s in range(4)]
            if t == NT_P - 1:
                subs[-1]["last"] = True
            tiles.append(subs)
        if n_samp and not _os.environ.get("K_NOSAMP"):
            tiles.append([dict(kind="s", st=1 + i, si=i) for i in range(n_samp)])

        STOP = int(_os.environ.get("K_STOP", "99"))
        for ti, subs in enumerate(tiles):
            TT = 128 * len(subs)
            state_mode = subs[0]["kind"] == "p" and ti < NPRE - 1
            load_tile_x(subs, TT)
            spread = NPRE >= 5
            if ti == 0:
                conv_layer_a(1)
            mlstm_layer(0, subs, TT)
            if (ti == 1 and spread) or (ti == 0 and not spread):
                conv_ffn(1)
            ffn(0, TT)
            if (ti == 1 and spread) or (ti == 0 and not spread):
                conv_layer_b(0)
            mlstm_layer(1, subs, TT, state_only=state_mode)
            if (ti == 2 and spread) or (ti == 0 and not spread):
                conv_ffn(2)
            if (ti == 3 and spread) or (ti == 0 and not spread):
                conv_layer_b(1)
                conv_ffn(3)
            halo_tile = subs[0]["kind"] == "p" and NPRE > 0 and ti == NPRE - 1
            if halo_tile:
                ffn(1, TT)
                swa_layer(0, subs, TT, True, kv_only=True)
                release_ln()
            elif not state_mode:
                ffn(1, TT)
                for j in range(2):
                    swa_layer(j, subs, TT, ti == max(NPRE - 1, 0), halo_keep=(NPRE > 0 and ti == NPRE))
                    ffn(2 + j, TT)
                store_tile_y(subs, TT)
            if NPRE and ti == NPRE - 1:
                for l in range(2):
                    for jj in range(4):
                        ts(DVE, Cst[l][0][:, jj, :], Cst[l][0][:, jj, :], keep[:, 0:1], None, ALU.mult, None,
                           CstB[l][0] + [constB], CstB[l][0])
                    ts(DVE, mst[l][0][:], mst[l][0][:], keep[0:8, 0:1], None, ALU.mult, None,
                       [mstB[l][0], constB], [mstB[l][0]])
            last_prompt = (ti == NT_P - 1)
            if last_prompt or subs[0]["kind"] == "s":
                for l in range(2):
                    for sub in (subs if subs[0]["kind"] == "s" else subs[:1]):
                        si = sub["st"]
                        if sub["kind"] == "p":
                            dC, dn, dm = pC_out[l], pn_out[l], pm_out[l]
                        else:
                            dC, dn, dm = sC_out[l, sub["si"]], sn_out[l, sub["si"]], sm_out[l, sub["si"]]
                        P.dma(ACT, lambda e, dC=dC, l=l, si=si: e.dma_start(
                            out=dC.rearrange("j q e -> q j e"), in_=Cst[l][si][:, :, 0:128]),
                            reads=CstB[l][si], final=True)
                        P.dma(ACT, lambda e, dn=dn, l=l, si=si: e.dma_start(
                            out=dn.rearrange("j (q o) -> q j o", o=1), in_=Cst[l][si][:, :, 128:129],
                            allow_slow_non_contiguous=True), reads=CstB[l][si], final=True)
                        P.dma(ACT, lambda e, dm=dm, l=l, si=si: e.dma_start(
                            out=dm.rearrange("(h o) -> h o", o=1), in_=mst[l][si][:],
                            allow_slow_non_contiguous=True), reads=[mstB[l][si]], final=True)

        P.emit(block, sems, dma_sems)
    return nc


def make_in_maps(inputs, NT_P, n_cores=8, n_samp=2, NPRE=0):
    c = _consts()
    f = lambda a: np.ascontiguousarray(np.asarray(a, dtype=np.float32))
    xp = f(inputs["x_prompt"])
    xs = f(inputs["x_sample"])
    sC, sn, sm = f(inputs["state_C"]), f(inputs["state_n"]), f(inputs["state_m"])
    ck, cv = f(inputs["cache_k"]), f(inputs["cache_v"])
    shared = {k: f(inputs[k]) for k in ["w_in_a", "w_out_a", "w_kv", "w_q_b", "w_out_b", "w_gu", "w_down"]}
    shared["bgate"] = f(np.broadcast_to(inputs["b_gate_a"][:, None, :], (2, 128, 16)))
    shared["gnT"] = f(np.asarray(inputs["g_norm_a"]).reshape(2, 8, 128).transpose(0, 2, 1))
    shared["sinksrep"] = f(np.broadcast_to(inputs["sinks_b"][:, None, :], (2, 128, 16)))
    shared["lng"] = f(np.asarray(inputs["ln_g"]).reshape(4, 2, 8, 128).transpose(0, 1, 3, 2))
    shared["lnb"] = f(np.asarray(inputs["ln_b"]).reshape(4, 2, 8, 128).transpose(0, 1, 3, 2))
    for k, v in c.items():
        shared["c_" + k] = v
    maps = []
    nb = xp.shape[0]
    NOWN = (NT_P - NPRE) * 512
    for core in range(n_cores):
        m = dict(shared)
        if NPRE:
            b, half = core // 2, core % 2
            if half == 1:
                m["xin"] = f(xp[b, :NT_P * 512])
            else:
                m["xin"] = f(np.concatenate([np.zeros((NPRE * 512, D), np.float32), xp[b, :NOWN]], 0))
            m["keep"] = np.full((128, 1), float(half), np.float32)
        else:
            b = core % nb
            m["xin"] = f(xp[b, :NT_P * 512])
            m["keep"] = np.ones((128, 1), np.float32)
        sl = slice(core * n_samp, (core + 1) * n_samp)
        m["xs"] = f(xs[sl])
        m["stC"] = f(sC[:, sl].reshape(2, n_samp, 4, 128, 128))
        m["stn"] = f(sn[:, sl].reshape(2, n_samp, 4, 128))
        m["stm"] = f(sm[:, sl])
        m["ck"] = f(ck[sl].reshape(n_samp, 128, 128))
        m["cv"] = f(cv[sl].reshape(n_samp, 128, 128))
        maps.append(m)
    return maps


def assemble(results, NT_P, nb=4, n_cores=8, n_samp=2, NPRE=0):
    if NPRE:
        y = np.stack([np.concatenate([results[2 * b]["y"], results[2 * b + 1]["y"]], 0) for b in range(nb)], 0)
        pc = [2 * b + 1 for b in range(nb)]
    else:
        y = np.stack([results[b]["y"] for b in range(nb)], 0)
        pc = list(range(nb))
    ys = np.concatenate([results[c]["ys"] for c in range(n_cores)], 0)
    pC = np.stack([results[c]["pC"].reshape(2, 8, 64, 128) for c in pc], 1)
    pn = np.stack([results[c]["pn"].reshape(2, 8, 64) for c in pc], 1)
    pm = np.stack([results[c]["pm"] for c in pc], 1)
    pk = np.stack([results[c]["pk"].reshape(128, 2, 64) for c in pc], 0)
    pv = np.stack([results[c]["pv"].reshape(128, 2, 64) for c in pc], 0)
    sC = np.concatenate([results[c]["sC"].reshape(2, n_samp, 8, 64, 128) for c in range(n_cores)], 1)
    sn = np.concatenate([results[c]["sn"].reshape(2, n_samp, 8, 64) for c in range(n_cores)], 1)
    sm = np.concatenate([results[c]["sm"] for c in range(n_cores)], 1)
    sk = np.concatenate([results[c]["sk"].reshape(n_samp, 32, 2, 64) for c in range(n_cores)], 0)
    sv = np.concatenate([results[c]["sv"].reshape(n_samp, 32, 2, 64) for c in range(n_cores)], 0)
    outs = (y, ys, pC, pn, pm, pk, pv, sC, sn, sm, sk, sv)
    return tuple(np.ascontiguousarray(o, dtype=np.float32) for o in outs)


def kernel(**inputs):
    NT_P, NPRE = 16, 8
    nc = build(NT_P, NPRE=NPRE)
    maps = make_in_maps(inputs, NT_P, NPRE=NPRE)
    res = run_bass_kernel_spmd(nc, maps, core_ids=list(range(8)))
    return assemble(res.results, NT_P, NPRE=NPRE)
```

```python
import math
from contextlib import ExitStack
import numpy as np
import concourse.bass as bass
import concourse.mybir as mybir
from concourse.bass_utils import run_bass_kernel_spmd

F32 = mybir.dt.float32
BF16 = mybir.dt.bfloat16
AF = mybir.ActivationFunctionType
ALU = mybir.AluOpType
AX = mybir.AxisListType

PE, ACT, DVE, POOL, SP = "pe", "act", "dve", "pool", "sp"
ENGS = (PE, ACT, DVE, POOL, SP)
DMA_SEMS = {SP: 12, POOL: 10, ACT: 6}

D = 1024
KC = 8
DFF = 2816
NF = 22
ALPHA = 8.0 ** 0.25
LNENG = "dve"
import os as _os
YQ = _os.environ.get("K_YQ", "act")
BCAST = bool(int(_os.environ.get("K_BC", "1")))
UVENG = _os.environ.get("K_UVENG", "pool")
LN_EPS = 1e-5
HN_EPS = 1e-6
NEG = -30000.0


class Buf:
    __slots__ = ("w", "r", "name", "excl")

    def __init__(self, name="", excl=False):
        self.w = {}
        self.r = {}
        self.name = name
        self.excl = excl


class Ins:
    __slots__ = ("fn", "waits", "dma")

    def __init__(self, fn):
        self.fn = fn
        self.waits = []
        self.dma = None


class Prog:
    def __init__(self):
        self.streams = {e: [] for e in ENGS}
        self.waited = {e: {} for e in ENGS}
        self.needed = {e: set() for e in ENGS}
        self.dma_next = {q: 0 for q in DMA_SEMS}
        self.dma_val = {q: [0] * n for q, n in DMA_SEMS.items()}
        self.final_tokens = []

    def _deps(self, me, reads, writes):
        deps = {}
        for b in reads:
            for k, v in b.w.items():
                if deps.get(k, -1) < v:
                    deps[k] = v
            if b.excl:
                for k, v in b.r.items():
                    if k != me and deps.get(k, -1) < v:
                        deps[k] = v
        for b in writes:
            for k, v in b.w.items():
                if (k != me or me != PE) and deps.get(k, -1) < v:
                    deps[k] = v
            for k, v in b.r.items():
                if (k != me or me != PE) and deps.get(k, -1) < v:
                    deps[k] = v
        return deps

    def _commit(self, eng, ins, deps):
        wd = self.waited[eng]
        for k, v in deps.items():
            if wd.get(k, -1) < v:
                wd[k] = v
                ins.waits.append((k, v))
                if not isinstance(k, tuple):
                    self.needed[k].add(v)

    def op(self, eng, fn, reads=(), writes=()):
        ins = Ins(fn)
        n = len(self.streams[eng])
        self._commit(eng, ins, self._deps(eng, reads, writes))
        self.streams[eng].append(ins)
        for b in reads:
            b.r[eng] = n
        for b in writes:
            b.w[eng] = n
        return ins

    def dma(self, q, fn, reads=(), writes=(), final=False):
        ins = Ins(fn)
        si = self.dma_next[q]
        self.dma_next[q] = (si + 1) % DMA_SEMS[q]
        key = ("dma", q, si)
        deps = self._deps(key, reads, writes)
        pv = self.dma_val[q][si]
        if pv > 0 and deps.get(key, -1) < pv:
            deps[key] = pv
        self._commit(q, ins, deps)
        val = pv + 16
        self.dma_val[q][si] = val
        ins.dma = (key, val)
        self.streams[q].append(ins)
        for b in reads:
            b.r[key] = val
        for b in writes:
            b.w[key] = val
        if final:
            self.final_tokens.append((key, val))
        return ins

    def emit(self, block, sems, dma_sems):
        rank = {e: {s: i + 1 for i, s in enumerate(sorted(self.needed[e]))} for e in ENGS}
        handles = {PE: "tensor", ACT: "scalar", DVE: "vector", POOL: "gpsimd", SP: "sync"}
        prog = self

        def make(e):
            def body(eng):
                rk = rank[e]
                for n, ins in enumerate(prog.streams[e]):
                    for k, v in ins.waits:
                        if isinstance(k, tuple):
                            eng.wait_ge(dma_sems[k], v)
                        else:
                            eng.wait_ge(sems[k], rank[k][v])
                    inst = ins.fn(eng)
                    if ins.dma is not None:
                        inst.then_inc(dma_sems[ins.dma[0]], 16)
                    elif n in rk:
                        inst.then_inc(sems[e], 1)
                if e == SP:
                    for k, v in prog.final_tokens:
                        eng.wait_ge(dma_sems[k], v)
            return body
        for e in ENGS:
            getattr(block, handles[e])(make(e))


def _consts():
    c = {}
    c["ident"] = np.eye(128, dtype=np.float32)
    s = np.arange(128)
    c["tri"] = (s[:, None] <= s[None, :]).astype(np.float32)
    sel = np.zeros((8, 128), np.float32)
    for k in range(8):
        sel[k, (k % 2) * 64:(k % 2) * 64 + 64] = 1.0
    c["sel"] = sel
    pm = np.zeros((8, 4), np.float32)
    for k in range(8):
        pm[k, k // 2] = 1.0
    c["pairmask"] = pm
    c["ones8"] = np.ones((8, 128), np.float32)
    valid = (s < 32).astype(np.float32)
    c["valid2"] = np.stack([valid, (valid - 1.0) * 30000.0], 1).astype(np.float32)
    slopes = np.exp2(-8.0 * np.arange(1, 17, dtype=np.float64) / 16)
    bt = np.zeros((8, 128, 512), np.float32)
    k = np.arange(128)[:, None]
    q = np.arange(128)[None, :]
    for kb in range(2):
        kpos = k - 128 if kb == 0 else k
        dist = np.abs(q - kpos).astype(np.float64)
        cq = q // 64
        ck = (k // 64) - 2 if kb == 0 else (k // 64)
        vis = (ck >= cq - 2) & (ck <= cq)
        for kv in range(2):
            for p in range(2):
                for hh in range(4):
                    head = kv * 8 + 2 * hh + p
                    t = np.where(vis, -slopes[head] * dist, NEG)
                    bt[kb * 4 + kv * 2 + p][:, hh * 128:(hh + 1) * 128] = t.astype(np.float32)
    c["biasT"] = bt
    return c


CONST_SHAPES = {"ident": [128, 128], "tri": [128, 128], "sel": [8, 128], "pairmask": [8, 4],
                "ones8": [8, 128], "valid2": [128, 2], "biasT": [8, 128, 512]}


def build(NT_P, n_samp=2, NPRE=0):
    nc = bass.Bass("TRN2", target_bir_lowering=False)
    P = Prog()

    def din(name, shape, dt=F32):
        return nc.dram_tensor(name, list(shape), dt, kind="ExternalInput").ap()

    def dout(name, shape, dt=F32):
        return nc.dram_tensor(name, list(shape), dt, kind="ExternalOutput").ap()

    def dscr(name, shape, dt=BF16):
        return nc.dram_tensor(name, list(shape), dt, kind="Internal").ap()

    NTOK = NT_P * 512
    NOUT = (NT_P - NPRE) * 512
    keep_in = None
    xin = din("xin", [NTOK, D])
    xs_in = din("xs", [n_samp, 32, D])
    stC_in = din("stC", [2, n_samp, 4, 128, 128])
    stn_in = din("stn", [2, n_samp, 4, 128])
    stm_in = din("stm", [2, n_samp, 8])
    ck_in = din("ck", [n_samp, 128, 128])
    cv_in = din("cv", [n_samp, 128, 128])
    w_in_a = din("w_in_a", [2, D, 3088])
    w_out_a = din("w_out_a", [2, D, D])
    w_kv = din("w_kv", [D, 256])
    w_q_b = din("w_q_b", [2, D, D])
    w_out_b = din("w_out_b", [2, D, D])
    w_gu = din("w_gu", [4, D, 2 * DFF])
    w_down = din("w_down", [4, DFF, D])
    bgate_in = din("bgate", [2, 128, 16])
    gnT_in = din("gnT", [2, 128, 8])
    sinks_in = din("sinksrep", [2, 128, 16])
    lng_in = din("lng", [4, 2, 128, 8])
    lnb_in = din("lnb", [4, 2, 128, 8])
    cin = {k: din("c_" + k, shp) for k, shp in CONST_SHAPES.items()}

    y_out = dout("y", [NOUT, D])
    keep_in = din("keep", [128, 1])
    ys_out = dout("ys", [n_samp, 32, D])
    pC_out = dout("pC", [2, 4, 128, 128])
    pn_out = dout("pn", [2, 4, 128])
    pm_out = dout("pm", [2, 8])
    pk_out = dout("pk", [128, 128])
    pv_out = dout("pv", [128, 128])
    sC_out = dout("sC", [2, n_samp, 4, 128, 128])
    sn_out = dout("sn", [2, n_samp, 4, 128])
    sm_out = dout("sm", [2, n_samp, 8])
    sk_out = dout("sk", [n_samp, 32, 128])
    sv_out = dout("sv", [n_samp, 32, 128])

    wqk_b = dscr("wqk_b", [2, 2, 128, 8, 512])
    wtok_b = dscr("wtok_b", [2, 5, 128, 8, 512])
    wif_b = dscr("wif_b", [2, 128, 8, 16])
    wouta_b = dscr("wouta_b", [2, 2, 128, 8, 512])
    wkv_b = dscr("wkv_b", [128, 8, 256])
    wkd_b = dscr("wkd_b", [128, 8, 256])
    wq_b = dscr("wq_b", [2, 2, 128, 8, 512])
    woutb_b = dscr("woutb_b", [2, 2, 128, 8, 512])
    wgu_b = dscr("wgu_b", [4, 11, 128, 8, 512])
    wd_b = dscr("wd_b", [4, 6, 128, 8, 512])

    es = ExitStack()
    with es:
        def sb(name, shape, dt):
            return es.enter_context(nc.sbuf_tensor("sb_" + name, list(shape), dt))

        x32T = sb("x32T", [128, 8, 512], F32)
        xT = sb("xT", [128, 8, 512], BF16)
        NSLOT = 4
        wslots = [sb("wslot%d" % i, [128, 4096], BF16) for i in range(NSLOT)]
        wslotB = [Buf("wslot%d" % i) for i in range(NSLOT)]
        qkT = sb("qkT", [128, 8, 512], BF16)
        hT = sb("hT", [128, 8, 512], BF16)
        hffT = sb("hffT", [128, 1 if _os.environ.get("K_SHRINK") else NF, 512], BF16)
        ktok = [sb("ktok%d" % i, [128, 512], BF16) for i in range(4)]
        vraw = [sb("vraw%d" % i, [128, 1024], BF16) for i in range(4)]
        uv = [sb("uv%d" % i, [128, 8, 130], BF16) for i in range(4)]
        og = [sb("og%d" % i, [128, 1024], BF16) for i in range(4)]
        hn = [sb("hn%d" % i, [128, 1024], BF16) for i in range(4)]
        sq = sb("sq", [128, 8, 128], F32)
        f512 = [sb("f512_%d" % i, [128, 512], F32) for i in range(4)]
        b512 = [sb("b512_%d" % i, [128, 512], BF16) for i in range(4)]
        io = [sb("io%d" % i, [128, 1024], F32) for i in range(2)]
        PTST = sb("PTST", [128, 8, 512], BF16)
        PT = [PTST[:, i, :] for i in range(8)]
        ST = [PTST[:, 2 * i:2 * i + 2, :].rearrange("p a (h n) -> p (a h) n", h=4) for i in range(4)]
        kTd = sb("kTd", [128, 2, 5, 128], BF16)
        Vaug = sb("Vaug", [128, 5, 2, 66], BF16)
        kTd_s = sb("kTd_s", [128, 2, 4, 128], BF16)
        Vaug_s = sb("Vaug_s", [128, 4, 2, 66], BF16)
        kvtok = sb("kvtok", [128, 256], F32)
        biasT = sb("biasT", [128, 1 if _os.environ.get("K_SHRINK") else 8, 512], F32)
        NST = 1 + n_samp
        Cst = [[sb("Cst%d_%d" % (l, i), [128, 4, 129], F32) for i in range(NST)] for l in range(2)]
        mst = [[sb("mst%d_%d" % (l, i), [8, 1], F32) for i in range(NST)] for l in range(2)]
        C0b = [sb("C0b%d" % i, [128, 4, 130], BF16) for i in range(2)]
        NR = 4
        gat = [sb("gat%d" % i, [128, 16], F32) for i in range(NR)]
        e1 = [sb("e1_%d" % i, [128, 8], F32) for i in range(NR)]
        spt = [sb("sp%d" % i, [128, 8], F32) for i in range(NR)]
        an = [sb("an%d" % i, [128, 16], F32) for i in range(NR)]
        ul = [sb("ul%d" % i, [128, 16], F32) for i in range(NR)]
        tm16 = [sb("tm16_%d" % i, [128, 16], F32) for i in range(NR)]
        sm8 = [sb("sm8_%d" % i, [8, 16], F32) for i in range(NR)]
        dgG = [sb("dgG%d" % i, [8, 16], F32) for i in range(NR)]
        nrm = [sb("nrm%d" % i, [128, 48], F32) for i in range(NR)]
        dsbt = [sb("dsb%d" % i, [128, 4], F32) for i in range(NR)]
        nrm2 = [sb("nrm2_%d" % i, [128, 48], F32) for i in range(NR)]
        lnst = sb("lnst", [128, 8], F32)
        ident = sb("ident", [128, 128], F32)
        identb = sb("identb", [128, 128], BF16)
        tri = sb("tri", [128, 128], F32)
        trib = sb("trib", [128, 128], BF16)
        onesb = sb("onesb", [128, 128], BF16)
        sel = sb("sel", [8, 128], F32)
        pairmask = sb("pairmask", [8, 4], F32)
        ones8 = sb("ones8", [8, 128], F32)
        valid2 = sb("valid2", [128, 2], F32)
        bgate = sb("bgate", [128, 2, 16], F32)
        gnT = sb("gnT", [128, 2, 8], F32)
        esink = sb("esink", [128, 2, 16], F32)
        lng = sb("lng", [128, 8, 8], F32)
        lnb = sb("lnb", [128, 8, 8], F32)
        keep = sb("keep", [128, 1], F32)

        psum = es.enter_context(nc.psum_tensor("psum", [128, 8, 512], F32))
        sems = {e: es.enter_context(nc.semaphore("s_" + e)) for e in ENGS}
        dma_sems = {}
        for q, n in DMA_SEMS.items():
            for i in range(n):
                dma_sems[("dma", q, i)] = es.enter_context(nc.semaphore("d_%s%d" % (q, i)))
        block = es.enter_context(nc.Block())

        bankB = [Buf("bank%d" % i, excl=True) for i in range(8)]
        bank_state = {"next": 0, "pinned": set()}

        def bank():
            assert len(bank_state["pinned"]) < 8
            while True:
                b = bank_state["next"]
                bank_state["next"] = (b + 1) % 8
                if b not in bank_state["pinned"]:
                    return b

        def pin(b):
            bank_state["pinned"].add(b)

        def unpin(b):
            bank_state["pinned"].discard(b)

        class Ring:
            def __init__(self, tiles, name):
                self.tiles = tiles
                self.bufs = [Buf("%s%d" % (name, i)) for i in range(len(tiles))]
                self.i = 0

            def next(self):
                i = self.i
                self.i = (i + 1) % len(self.tiles)
                return self.tiles[i], self.bufs[i]

        R = {}
        for nm, tl in [("ktok", ktok), ("vraw", vraw), ("uv", uv), ("og", og), ("hn", hn),
                       ("f512", f512), ("b512", b512), ("io", io), ("C0b", C0b), ("gat", gat), ("e1", e1),
                       ("sp", spt), ("an", an), ("ul", ul), ("tm16", tm16), ("sm8", sm8), ("dgG", dgG),
                       ("nrm", nrm), ("PT", PT), ("dsb", dsbt), ("nrm2", nrm2)]:
            R[nm] = Ring(tl, nm)
        sqB = Buf("sq")

        class STRing:
            def __init__(self):
                self.i = 0

            def next(self):
                i = self.i
                self.i = (i + 1) % 4
                return ST[i], [R["PT"].bufs[2 * i], R["PT"].bufs[2 * i + 1]]
        STring = STRing()
        x32B = [Buf("x32_%d" % m) for m in range(8)]
        xTB = [Buf("xT_%d" % m) for m in range(8)]
        qkB = [Buf("qk_%d" % m) for m in range(8)]
        hTB = [Buf("hT_%d" % s) for s in range(4)]
        hffB = [Buf("hff_%d" % f) for f in range(NF)]
        kTdB = Buf("kTd")
        VaugB = Buf("Vaug")
        kTdsB = Buf("kTd_s")
        VaugsB = Buf("Vaug_s")
        kvtokB = Buf("kvtok")
        biasB = Buf("biasT")
        constB = Buf("const")
        CstB = [[[Buf("Cst%d" % q) for q in range(8)] for _ in range(NST)] for _ in range(2)]
        mstB = [[Buf("mst") for _ in range(NST)] for _ in range(2)]
        lnstB = Buf("lnst")
        wscrB = {}

        wstate = {"i": 0}

        def wload(src, view, name):
            i = wstate["i"]
            wstate["i"] = (i + 1) % NSLOT
            dst = view(wslots[i])
            P.dma(SP, lambda e, dst=dst, src=src: e.dma_start(out=dst, in_=src),
                  reads=[wscrB[name]],
                  writes=[wslotB[i]])
            return dst, wslotB[i]

        def v8(n):
            return lambda t: t[:, 0:8 * n].rearrange("p (c n) -> p c n", c=8)

        def mm(out, lhsT, rhs, start, stop, reads, writes):
            P.op(PE, lambda e: e.matmul(out, lhsT=lhsT, rhs=rhs, start=start, stop=stop),
                 reads=reads, writes=writes)

        def act(out, in_, func, reads, writes, **kw):
            P.op(ACT, lambda e: e.activation(out=out, in_=in_, func=func, **kw), reads=reads, writes=writes)

        def tt(eng, out, in0, in1, op, reads, writes):
            P.op(eng, lambda e: e.tensor_tensor(out=out, in0=in0, in1=in1, op=op), reads=reads, writes=writes)

        def ts(eng, out, in0, s1, s2, op0, op1, reads, writes):
            if op1 is None:
                P.op(eng, lambda e: e.tensor_scalar(out=out, in0=in0, scalar1=s1, scalar2=None, op0=op0),
                     reads=reads, writes=writes)
            else:
                P.op(eng, lambda e: e.tensor_scalar(out=out, in0=in0, scalar1=s1, scalar2=s2, op0=op0, op1=op1),
                     reads=reads, writes=writes)

        def stt(out, in0, scalar, in1, op0, op1, reads, writes):
            P.op(DVE, lambda e: e.scalar_tensor_tensor(out=out, in0=in0, scalar=scalar, in1=in1, op0=op0, op1=op1),
                 reads=reads, writes=writes)

        def cp(eng, out, in_, reads, writes):
            if eng == ACT:
                act(out, in_, AF.Copy, reads, writes)
            else:
                P.op(eng, lambda e: e.tensor_copy(out=out, in_=in_), reads=reads, writes=writes)

        def conv(dst, src, name):
            b = wscrB.setdefault(name, Buf(name))
            P.dma(POOL, lambda e: e.dma_start(out=dst, in_=src), writes=[b])

        def cblk(src2d):
            return src2d.rearrange("(c p) n -> p c n", p=128)

        def conv_layer_a(l):
            for i in range(2):
                conv(wqk_b[l, i], cblk(w_in_a[l][:, i * 512:(i + 1) * 512]), "wqk_b")
            conv(wif_b[l], cblk(w_in_a[l][:, 3072:3088]), "wif_b")
            for i in range(5):
                conv(wtok_b[l, i], cblk(w_in_a[l][:, 512 + i * 512:1024 + i * 512]), "wtok_b")
            for i in range(2):
                conv(wouta_b[l, i], cblk(w_out_a[l][:, i * 512:(i + 1) * 512]), "wouta_b")

        def conv_layer_b(j):
            if j == 0:
                conv(wkv_b, cblk(w_kv), "wkv_b")
                for kv in range(2):
                    for dup in range(2):
                        conv(wkd_b[:, :, kv * 128 + dup * 64:kv * 128 + dup * 64 + 64],
                             cblk(w_kv[:, kv * 64:(kv + 1) * 64]), "wkd_b")
            for i in range(2):
                conv(wq_b[j, i], cblk(w_q_b[j][:, i * 512:(i + 1) * 512]), "wq_b")
            for i in range(2):
                conv(woutb_b[j, i], cblk(w_out_b[j][:, i * 512:(i + 1) * 512]), "woutb_b")

        def conv_ffn(l):
            for b in range(11):
                conv(wgu_b[l, b][:, :, 0:256], cblk(w_gu[l][:, b * 256:(b + 1) * 256]), "wgu_b")
                conv(wgu_b[l, b][:, :, 256:512], cblk(w_gu[l][:, DFF + b * 256:DFF + (b + 1) * 256]), "wgu_b")
            for nh in range(2):
                for fb in range(3):
                    nfc = 8 if fb < 2 else 6
                    conv(wd_b[l, nh * 3 + fb][:, 0:nfc, :],
                         w_down[l][fb * 1024:fb * 1024 + nfc * 128, nh * 512:(nh + 1) * 512]
                         .rearrange("(c p) n -> p c n", p=128), "wd_b")

        def ld(q, dst, src, b):
            P.dma(q, lambda e: e.dma_start(out=dst, in_=src), writes=[b])

        ld(SP, ident[:], cin["ident"], constB)
        ld(SP, tri[:], cin["tri"], constB)
        ld(SP, sel[:], cin["sel"], constB)
        ld(SP, pairmask[:], cin["pairmask"], constB)
        ld(SP, ones8[:], cin["ones8"], constB)
        ld(SP, valid2[:], cin["valid2"], constB)
        if not _os.environ.get("K_SHRINK"):
            ld(SP, biasT[:], cin["biasT"].rearrange("g p n -> p g n"), biasB)
        ld(SP, bgate[:], bgate_in.rearrange("l p n -> p l n"), constB)
        ld(SP, gnT[:], gnT_in.rearrange("l p n -> p l n"), constB)
        ld(SP, esink[:], sinks_in.rearrange("l p n -> p l n"), constB)
        ld(SP, lng[:], lng_in.rearrange("l i p n -> p (l i) n"), constB)
        ld(SP, lnb[:], lnb_in.rearrange("l i p n -> p (l i) n"), constB)
        ld(SP, keep[:], keep_in, constB)
        cp(DVE, identb[:], ident[:], [constB], [constB])
        cp(DVE, trib[:], tri[:], [constB], [constB])
        P.op(DVE, lambda e: e.memset(onesb[:], 1.0 / 1024.0), writes=[constB])
        act(esink[:], esink[:], AF.Exp, [constB], [constB])
        for l in range(2):
            P.op(POOL, lambda e, l=l: e.memset(Cst[l][0][:], 0.0), writes=CstB[l][0])
            P.op(POOL, lambda e, l=l: e.memset(mst[l][0][:], 0.0), writes=[mstB[l][0]])
            for i in range(n_samp):
                P.dma(SP, lambda e, l=l, i=i: e.dma_start(out=Cst[l][1 + i][:, :, 0:128], in_=stC_in[l, i].rearrange("j q e -> q j e")), writes=CstB[l][1 + i])
                P.dma(SP, lambda e, l=l, i=i: e.dma_start(out=Cst[l][1 + i][:, :, 128:129],
                                                          in_=stn_in[l, i].rearrange("j (q o) -> q j o", o=1),
                                                          allow_slow_non_contiguous=True),
                      writes=CstB[l][1 + i])
                P.dma(SP, lambda e, l=l, i=i: e.dma_start(out=mst[l][1 + i][:],
                                                          in_=stm_in[l, i].rearrange("(h o) -> h o", o=1),
                                                          allow_slow_non_contiguous=True),
                      writes=[mstB[l][1 + i]])
        P.op(POOL, lambda e: e.memset(kTd[:], 0.0), writes=[kTdB])
        P.op(POOL, lambda e: e.memset(Vaug[:], 1.0), writes=[VaugB])
        P.op(POOL, lambda e: e.memset(Vaug[:, 0], 0.0), writes=[VaugB])
        P.op(POOL, lambda e: e.memset(Vaug_s[:], 1.0), writes=[VaugsB])
        for i in range(4):
            P.op(POOL, lambda e, i=i: e.memset(uv[i][:], 0.0), writes=[R["uv"].bufs[i]])

        conv_layer_a(0)
        conv_ffn(0)

        def feat_mm_group(wv, wB, m_lo, xsrc, xB, TT, kc=8):
            bk = bank()
            for c in range(kc):
                mm(psum[:, bk, 0:TT], wv[:, c, m_lo:m_lo + 128], xsrc[:, c, 0:TT], c == 0, c == kc - 1,
                   [wB] + xB, [bankB[bk]])
            return bk

        def layer_norm(li, TT, mix_banks_fn, release=None, LAG=1):
            release_ln()
            bm = bank()
            pin(bm)
            be = bank()
            pin(be)
            pend = []

            def flush(keep):
                while len(pend) > keep:
                    m, zb, zbB, zq, zqB = pend.pop(0)
                    mm(psum[:, bm, 0:TT], onesb[:], zb[:, 0:TT], m == 0, m == 7, [constB, zbB], [bankB[bm]])
                    mm(psum[:, be, 0:TT], onesb[:], zq[:, 0:TT], m == 0, m == 7, [constB, zqB], [bankB[be]])
            for m in range(8):
                bk = mix_banks_fn(m)
                stt(x32T[:, m, 0:TT], x32T[:, m, 0:TT], ALPHA, psum[:, bk, 0:TT], ALU.mult, ALU.add,
                    [x32B[m], bankB[bk]], [x32B[m]])
                if release is not None:
                    release(m)
                zb, zbB = R["b512"].next()
                act(zb[:, 0:TT], x32T[:, m, 0:TT], AF.Copy, [x32B[m]], [zbB])
                zq, zqB = R["b512"].next()
                act(zq[:, 0:TT], x32T[:, m, 0:TT], AF.Square, [x32B[m]], [zqB])
                pend.append((m, zb, zbB, zq, zqB))
                flush(LAG)
            flush(0)
            msq, msqB = R["f512"].next()
            act(msq[:, 0:TT], psum[:, bm, 0:TT], AF.Square, [bankB[bm]], [msqB])
            var, varB = R["f512"].next()
            tt(DVE, var[:, 0:TT], psum[:, be, 0:TT], msq[:, 0:TT], ALU.subtract, [bankB[be], msqB], [varB])
            ts(DVE, var[:, 0:TT], var[:, 0:TT], 0.0, LN_EPS, ALU.max, ALU.add, [varB], [varB])
            act(var[:, 0:TT], var[:, 0:TT], AF.Ln, [varB], [varB])
            act(psum[:, be, 0:TT], var[:, 0:TT], AF.Exp, [varB], [bankB[be]], scale=-0.5)
            for m in range(8):
                t1, t1B = R["f512"].next()
                tt(DVE, t1[:, 0:TT], x32T[:, m, 0:TT], psum[:, bm, 0:TT], ALU.subtract, [x32B[m], bankB[bm]], [t1B])
                tt(DVE, t1[:, 0:TT], t1[:, 0:TT], psum[:, be, 0:TT], ALU.mult, [t1B, bankB[be]], [t1B])
                act(xT[:, m, 0:TT], t1[:, 0:TT], AF.Identity, [t1B, constB], [xTB[m]],
                    scale=lng[:, li, m:m + 1], bias=lnb[:, li, m:m + 1])
                act(x32T[:, m, 0:TT], t1[:, 0:TT], AF.Identity, [t1B, constB], [x32B[m]],
                    scale=lng[:, li, m:m + 1], bias=lnb[:, li, m:m + 1])
            LNPIN.extend([bm, be])

        LNPIN = []

        def release_ln():
            while LNPIN:
                unpin(LNPIN.pop())

        def couter(mmfns):
            bks = [bank() for _ in mmfns]
            release_ln()
            for c in range(8):
                for fn, bk in zip(mmfns, bks):
                    fn(c, bk)
            return bks

        def out_proj_ln(wblocks, wname, li, TT):
            loaded = {}

            def get(m):
                i = m // 4
                if i not in loaded:
                    loaded[i] = wload(wblocks[i], v8(512), wname)
                wv, wB = loaded[i]
                return feat_mm_group(wv, wB, (m % 4) * 128, hT, hTB, TT)
            layer_norm(li, TT, get)

        def ffn(l, TT):
            def gu_evac(f, bg, bu):
                sg, sgB = R["f512"].next()
                act(sg[:, 0:TT], psum[:, bg, 0:TT], AF.Silu, [bankB[bg]], [sgB])
                tt(DVE, hffT[:, f, 0:TT], sg[:, 0:TT], psum[:, bu, 0:TT], ALU.mult, [sgB, bankB[bu]], [hffB[f]])

            blocks = {}

            def blk(b):
                if b not in blocks:
                    blocks[b] = wload(wgu_b[l, b], v8(512), "wgu_b")
                return blocks[b]

            def gu_mm(f, col0):
                b, ff = f // 2, f % 2
                wv, wB = blk(b)

                def fn(c, bk):
                    mm(psum[:, bk, 0:TT], wv[:, c, col0 + ff * 128:col0 + ff * 128 + 128], xT[:, c, 0:TT],
                       c == 0, c == 7, [wB, xTB[c]], [bankB[bk]])
                return fn
            blk(0)
            blk(1)
            bks = couter([gu_mm(f, col0) for f in range(3) for col0 in (0, 256)])
            for f in range(3):
                gu_evac(f, bks[2 * f], bks[2 * f + 1])
            for f in range(3, NF):
                wv, wB = blk(f // 2)
                ff = f % 2
                bg = feat_mm_group(wv, wB, ff * 128, xT, xTB, TT)
                bu = feat_mm_group(wv, wB, 256 + ff * 128, xT, xTB, TT)
                gu_evac(f, bg, bu)
            banks = {}

            def down_half(nh):
                bks = [bank() for _ in range(4)]
                for b in bks:
                    pin(b)
                for fb in range(3):
                    nfc = 8 if fb < 2 else 6
                    wv, wB = wload(wd_b[l, nh * 3 + fb][:, 0:nfc, :],
                                   lambda t, nfc=nfc: t[:, 0:nfc * 512].rearrange("p (c n) -> p c n", c=nfc), "wd_b")
                    for mi in range(4):
                        for fc in range(nfc):
                            f = fb * 8 + fc
                            mm(psum[:, bks[mi], 0:TT], wv[:, fc, mi * 128:(mi + 1) * 128], hffT[:, f, 0:TT],
                               f == 0, f == NF - 1, [wB, hffB[f]], [bankB[bks[mi]]])
                for mi in range(4):
                    banks[nh * 4 + mi] = bks[mi]

            def get(m):
                if m % 4 == 0:
                    down_half(m // 4)
                return banks[m]

            layer_norm(l * 2 + 1, TT, get, release=lambda m: unpin(banks[m]))

        def load_tile_x(subs, TT):
            release_ln()
            for s, sub in enumerate(subs):
                xt, xtB = R["io"].next()
                if sub["kind"] == "p":
                    r0 = sub["row0"]
                    P.dma(SP, lambda e, xt=xt, r0=r0: e.dma_start(out=xt[:], in_=xin[r0:r0 + 128, :]), writes=[xtB])
                else:
                    P.op(DVE, lambda e, xt=xt: e.memset(xt[:], 0.0), writes=[xtB])
                    i = sub["si"]
                    P.dma(SP, lambda e, xt=xt, i=i: e.dma_start(out=xt[0:32, :], in_=xs_in[i]), writes=[xtB])
                for g in range(2):
                    bk = bank()
                    for cc in range(4):
                        c = g * 4 + cc
                        P.op(PE, lambda e, bk=bk, cc=cc, c=c, xt=xt: e.transpose(
                            out=psum[:, bk, cc * 128:(cc + 1) * 128], in_=xt[:, c * 128:(c + 1) * 128],
                            identity=ident[:]), reads=[xtB, constB], writes=[bankB[bk]])
                    src = psum[:, bk, :].rearrange("p (c n) -> p c n", c=4)
                    if not _os.environ.get("K_NOACT"):
                        act(x32T[:, g * 4:(g + 1) * 4, s * 128:(s + 1) * 128], src, AF.Copy, [bankB[bk]],
                            x32B[g * 4:(g + 1) * 4])
                    if not _os.environ.get("K_NODVE"):
                        cp(_os.environ.get("K_E2", DVE), xT[:, g * 4:(g + 1) * 4, s * 128:(s + 1) * 128], src, [bankB[bk]], xTB[g * 4:(g + 1) * 4])

        def store_tile_y(subs, TT):
            release_ln()
            for s, sub in enumerate(subs):
                if sub.get("discard"):
                    continue
                yt = hffT[:, 4 * s:4 * s + 4, :].rearrange("p f n -> p (f n)").bitcast(F32)
                ytBs = hffB[4 * s:4 * s + 4]
                for g in range(2):
                    bk = bank()
                    for cc in range(4):
                        c = g * 4 + cc
                        P.op(PE, lambda e, bk=bk, cc=cc, c=c, s=s: e.transpose(
                            out=psum[:, bk, cc * 128:(cc + 1) * 128], in_=x32T[:, c, s * 128:(s + 1) * 128],
                            identity=ident[:]), reads=[x32B[c], constB], writes=[bankB[bk]])
                    if g == 0:
                        act(yt[:, 0:512], psum[:, bk, :], AF.Copy, [bankB[bk]], ytBs[0:2])
                    else:
                        cp(DVE, yt[:, 512:1024], psum[:, bk, :], [bankB[bk]], ytBs[2:4])
                if sub["kind"] == "p":
                    r0 = sub["yrow0"]
                    P.dma(YQ, lambda e, yt=yt, r0=r0: e.dma_start(out=y_out[r0:r0 + 128, :], in_=yt[:]),
                          reads=ytBs, final=True)
                else:
                    i = sub["si"]
                    P.dma(YQ, lambda e, yt=yt, i=i: e.dma_start(out=ys_out[i], in_=yt[0:32, :]),
                          reads=ytBs, final=True)

        def mlstm_layer(l, subs, TT, state_only=False):
            NS = len(subs)
            if state_only:
                release_ln()
            else:
                wq2 = [wload(wqk_b[l, i], v8(512), "wqk_b") for i in range(2)]

                def qk_mm(m):
                    wv, wB = wq2[m // 4]

                    def fn(c, bk):
                        mm(psum[:, bk, 0:TT], wv[:, c, (m % 4) * 128:(m % 4) * 128 + 128], xT[:, c, 0:TT],
                           c == 0, c == 7, [wB, xTB[c]], [bankB[bk]])
                    return fn

                def qk_evac(m, bk):
                    if m < 4:
                        act(qkT[:, m, 0:TT], psum[:, bk, 0:TT], AF.Copy, [bankB[bk]], [qkB[m]])
                    else:
                        act(qkT[:, m, 0:TT], psum[:, bk, 0:TT], AF.Copy, [bankB[bk]], [qkB[m]], scale=0.125)
                bks = couter([qk_mm(m) for m in range(6)])
                for m in range(6):
                    qk_evac(m, bks[m])
                for m in (6, 7):
                    wv, wB = wq2[1]
                    bk = feat_mm_group(wv, wB, (m % 4) * 128, xT, xTB, TT)
                    qk_evac(m, bk)
            wv, wB = wload(wif_b[l], v8(16), "wif_b")
            G = []
            G0 = []
            for s, sub in enumerate(subs):
                bk = bank()
                for c in range(8):
                    mm(psum[:, bk, 0:16], xT[:, c, s * 128:(s + 1) * 128], wv[:, c, :], c == 0, c == 7,
                       [wB] + xTB, [bankB[bk]])
                ga, gaB = R["gat"].next()
                tt(DVE, ga[:], psum[:, bk, 0:16], bgate[:, l, :], ALU.add, [bankB[bk], constB], [gaB])
                ee, eeB = R["e1"].next()
                act(ee[:], ga[:, 8:16], AF.Exp, [gaB], [eeB], scale=-1.0)
                sp_, spB = R["sp"].next()
                act(sp_[:], ee[:], AF.Ln, [eeB], [spB], bias=1.0)
                if sub["kind"] == "s":
                    ts(DVE, sp_[:], sp_[:], valid2[:, 0:1], None, ALU.mult, None, [spB, constB], [spB])
                    ts(DVE, ga[:, 0:8], ga[:, 0:8], valid2[:, 0:1], valid2[:, 1:2], ALU.mult, ALU.add,
                       [gaB, constB], [gaB])
                G0.append((ga, gaB, sp_, spB))
            KT, VR, OG = [None] * NS, [None] * NS, [None] * NS
            if not state_only:
                for s in range(NS):
                    bk = bank()
                    pbk = psum[:, bk, :].bitcast(BF16)
                    for jj in range(4):
                        P.op(PE, lambda e, pbk=pbk, jj=jj, s=s: e.transpose(
                            out=pbk[:, jj * 128:(jj + 1) * 128], in_=qkT[:, 4 + jj, s * 128:(s + 1) * 128],
                            identity=identb[:]), reads=[qkB[4 + jj], constB], writes=[bankB[bk]])
                    KT[s] = R["ktok"].next()
                    act(KT[s][0][:], pbk[:, 0:512], AF.Copy, [bankB[bk]], [KT[s][1]])
            for n in range(0 if state_only else 1, 3 if state_only else 5):
                wv, wB = wload(wtok_b[l, n], v8(512), "wtok_b")
                for s in range(NS):
                    bk = bank()
                    for c in range(8):
                        mm(psum[:, bk, :], xT[:, c, s * 128:(s + 1) * 128], wv[:, c, :], c == 0, c == 7,
                           [wB] + xTB, [bankB[bk]])
                    if n == 0:
                        KT[s] = R["ktok"].next()
                        act(KT[s][0][:], psum[:, bk, :], AF.Copy, [bankB[bk]], [KT[s][1]], scale=0.125)
                    elif n in (1, 2):
                        if n == 1:
                            VR[s] = R["vraw"].next()
                        cp(DVE, VR[s][0][:, (n - 1) * 512:n * 512], psum[:, bk, :], [bankB[bk]], [VR[s][1]])
                    else:
                        if n == 3:
                            OG[s] = R["og"].next()
                        act(OG[s][0][:, (n - 3) * 512:(n - 2) * 512], psum[:, bk, :], AF.Sigmoid, [bankB[bk]],
                            [OG[s][1]])
                    if NS > 2 and n == 0 and s >= 1:
                        pass
            for s, sub in enumerate(subs):
                ga, gaB, sp_, spB = G0[s]
                bk2 = bank()
                mm(psum[:, bk2, 0:8], tri[:], sp_[:], True, True, [constB, spB], [bankB[bk2]])
                a_, aB = R["an"].next()
                tt(DVE, a_[:, 0:8], ga[:, 0:8], psum[:, bk2, 0:8], ALU.add, [gaB, bankB[bk2]], [aB])
                cp(DVE, a_[:, 8:16], psum[:, bk2, 0:8], [bankB[bk2]], [aB])
                bk3 = bank()
                P.op(PE, lambda e, bk3=bk3, a_=a_: e.transpose(out=psum[0:8, bk3, 0:128], in_=a_[:, 0:8],
                                                               identity=ident[:]),
                     reads=[aB, constB], writes=[bankB[bk3]])
                P.op(PE, lambda e, bk3=bk3, a_=a_: e.transpose(out=psum[0:8, bk3, 128:256], in_=a_[:, 8:16],
                                                               identity=ident[:]),
                     reads=[aB, constB], writes=[bankB[bk3]])
                s8, s8B = R["sm8"].next()
                P.op(DVE, lambda e, s8=s8, bk3=bk3: e.tensor_reduce(out=s8[:, 0:1], in_=psum[0:8, bk3, 0:128],
                                                                    axis=AX.X, op=ALU.max),
                     reads=[bankB[bk3]], writes=[s8B])
                cp(DVE, s8[:, 4:5], psum[0:8, bk3, 255:256], [bankB[bk3]], [s8B])
                G.append((a_, aB, s8, s8B))
            CH = [None] * NS

            def phase_ab(s):
                sub = subs[s]
                si = sub["st"]
                M = mst[l][si]
                MB = mstB[l][si]
                a_, aB, s8, s8B = G[s]
                tt(DVE, s8[:, 1:2], s8[:, 0:1], M[:], ALU.max, [s8B, MB], [s8B])
                tt(DVE, s8[:, 2:3], M[:], s8[:, 1:2], ALU.subtract, [s8B, MB], [s8B])
                tt(DVE, s8[:, 3:4], M[:], s8[:, 1:2], ALU.subtract, [s8B, MB], [s8B])
                tt(DVE, M[:], s8[:, 1:2], s8[:, 4:5], ALU.subtract, [s8B], [MB])
                dg, dgB = R["dgG"].next()
                act(dg[:, 12:14], s8[:, 2:4], AF.Exp, [s8B], [dgB])
                ts(DVE, dg[:, 0:8], ident[0:8, 0:8], s8[:, 1:2], None, ALU.mult, None, [constB, s8B], [dgB])
                ts(DVE, dg[:, 8:12], pairmask[:], dg[:, 12:13], None, ALU.mult, None, [constB, dgB], [dgB])
                ch = {}
                st_banks = []
                if not state_only:
                    stt_, stBs = STring.next()
                    ch["ST"], ch["STB"] = stt_, stBs
                    for hg in range(2):
                        bk2 = bank()
                        for hh in range(4):
                            h = 2 * hh + hg
                            j, p = h // 2, h % 2
                            mm(psum[:, bk2, hh * 128:(hh + 1) * 128],
                               qkT[p * 64:(p + 1) * 64, 4 + j, s * 128:(s + 1) * 128],
                               qkT[p * 64:(p + 1) * 64, j, s * 128:(s + 1) * 128], True, True,
                               [qkB[4 + j], qkB[j]], [bankB[bk2]])
                        st_banks.append(bk2)
                bk = bank()
                mm(psum[:, bk, 0:8], ones8[:], dg[:, 0:8], True, True, [constB, dgB], [bankB[bk]])
                mm(psum[:, bk, 8:12], sel[:], dg[:, 8:12], True, True, [constB, dgB], [bankB[bk]])
                t16, t16B = R["tm16"].next()
                tt(DVE, t16[:, 0:8], a_[:, 0:8], psum[:, bk, 0:8], ALU.subtract, [aB, bankB[bk]], [t16B])
                tt(DVE, t16[:, 8:11], a_[:, 8:14:2], psum[:, bk, 0:6:2], ALU.subtract, [aB, bankB[bk]], [t16B])
                tt(DVE, t16[:, 11:14], a_[:, 9:15:2], psum[:, bk, 1:7:2], ALU.subtract, [aB, bankB[bk]], [t16B])
                tt(DVE, t16[:, 14:16], a_[:, 14:16], psum[:, bk, 6:8], ALU.subtract, [aB, bankB[bk]], [t16B])
                u_, uB = R["ul"].next()
                act(u_[:], t16[:], AF.Exp, [t16B], [uB])
                dsb, dsbB = R["dsb"].next()
                cp(DVE, dsb[:], psum[:, bk, 8:12], [bankB[bk]], [dsbB])
                for hg, bk2 in enumerate(st_banks):
                    if BCAST:
                        tt(DVE, stt_[:, hg:8:2, :], psum[:, bk2, :].rearrange("p (h n) -> p h n", h=4),
                           trib[:].unsqueeze(1).to_broadcast([128, 4, 128]), ALU.mult, [bankB[bk2], constB], stBs)
                    else:
                        for hh in range(4):
                            tt(DVE, stt_[:, 2 * hh + hg, :], psum[:, bk2, hh * 128:(hh + 1) * 128], trib[:],
                               ALU.mult, [bankB[bk2], constB], stBs)
                uvt, uvB = R["uv"].next()
                vr, vrB = VR[s]
                if BCAST:
                    tt(UVENG, uvt[:, :, 0:128], vr[:].rearrange("p (h n) -> p h n", h=8),
                       u_[:, 0:8].unsqueeze(2).to_broadcast([128, 8, 128]), ALU.mult, [vrB, uB], [uvB])
                else:
                    for h in range(8):
                        if h % 2 == 0:
                            act(uvt[:, h, 0:128], vr[:, h * 128:(h + 1) * 128], AF.Identity, [vrB, uB], [uvB],
                                scale=u_[:, h:h + 1])
                        else:
                            ts(DVE, uvt[:, h, 0:128], vr[:, h * 128:(h + 1) * 128], u_[:, h:h + 1], None, ALU.mult,
                               None, [vrB, uB], [uvB])
                cp(UVENG, uvt[:, :, 128:129], u_[:, 0:8].unsqueeze(2), [uB], [uvB])
                ch.update(dict(u=u_, uB=uB, dsb=dsb, dsbB=dsbB, uv=uvt, uvB=uvB))
                CH[s] = ch

            def transposes(s):
                hnt, hnB = CH[s]["hn"], CH[s]["hnB"]
                bk4 = bank()
                pb = psum[:, bk4, :].bitcast(BF16)
                for c in range(8):
                    P.op(PE, lambda e, pb=pb, c=c, hnt=hnt: e.transpose(
                        out=pb[:, c * 128:(c + 1) * 128], in_=hnt[:, c * 128:(c + 1) * 128], identity=identb[:]),
                        reads=[hnB, constB], writes=[bankB[bk4]])
                if BCAST:
                    tt(DVE, hT[:, :, s * 128:(s + 1) * 128], pb.rearrange("p (c n) -> p c n", c=8),
                       gnT[:, l, :].unsqueeze(2).to_broadcast([128, 8, 128]), ALU.mult,
                       [bankB[bk4], constB], [hTB[s]])
                else:
                    for c in range(8):
                        if c % 2 == 0:
                            act(hT[:, c, s * 128:(s + 1) * 128], pb[:, c * 128:(c + 1) * 128], AF.Identity,
                                [bankB[bk4], constB], [hTB[s]], scale=gnT[:, l, c:c + 1])
                        else:
                            ts(DVE, hT[:, c, s * 128:(s + 1) * 128], pb[:, c * 128:(c + 1) * 128],
                               gnT[:, l, c:c + 1], None, ALU.mult, None, [bankB[bk4], constB], [hTB[s]])

            def decay_cb(s):
                si = subs[s]["st"]
                C, CB = Cst[l][si], CstB[l][si]
                ch = CH[s]
                dsb, dsbB = ch["dsb"], ch["dsbB"]
                if BCAST:
                    tt(DVE, C[:], C[:], dsb[:].unsqueeze(2).to_broadcast([128, 4, 129]), ALU.mult, CB + [dsbB], CB)
                else:
                    for jj in range(4):
                        ts(DVE, C[:, jj, :], C[:, jj, :], dsb[:, jj:jj + 1], None, ALU.mult, None, CB + [dsbB], CB)
                ch["cb"], ch["cbB"] = None, None
                if not state_only:
                    cb, cbB = R["C0b"].next()
                    cp(ACT, cb[:, :, 0:129], C[:], CB, [cbB])
                    ch["cb"], ch["cbB"] = cb, cbB

            phase_ab(0)
            for s, sub in enumerate(subs):
                si = sub["st"]
                C = Cst[l][si]
                CB = CstB[l][si]
                ch = CH[s]
                u_, uB, dsb, dsbB, uvt, uvB = ch["u"], ch["uB"], ch["dsb"], ch["dsbB"], ch["uv"], ch["uvB"]
                if "cb" not in ch:
                    decay_cb(s)
                cb, cbB = ch["cb"], ch["cbB"]
                if s + 1 < NS:
                    phase_ab(s + 1)
                if not state_only:
                    stt_, stBs = ch["ST"], ch["STB"]
                    nr, nrB = R["nrm"].next()
                    hnt, hnB = R["hn"].next()
                    ch["hn"], ch["hnB"] = hnt, hnB
                    ogt, ogB = OG[s]
                    slot0 = 0
                    obanks = []
                    for grp in ((0, 2, 4), (1, 3, 5), (6,), (7,)):
                        bk3 = bank()
                        for gi, h in enumerate(grp):
                            j, p = h // 2, h % 2
                            o_ap = psum[:, bk3, gi * 129:(gi + 1) * 129]
                            mm(o_ap, qkT[p * 64:(p + 1) * 64, j, s * 128:(s + 1) * 128],
                               cb[p * 64:(p + 1) * 64, j, 0:129], True, False, [qkB[j], cbB], [bankB[bk3]])
                            mm(o_ap, stt_[:, h, :], uvt[:, h, 0:129], False, True, stBs + [uvB], [bankB[bk3]])
                        ng = len(grp)
                        h0 = slot0
                        slot0 += ng
                        obanks.append((bk3, grp, h0))
                        pv = psum[:, bk3, 0:ng * 129].rearrange("p (h n) -> p h n", h=ng)
                        act(nr[:, h0:h0 + ng].unsqueeze(2), pv[:, :, 128:129], AF.Abs, [bankB[bk3]], [nrB])
                        for gi, h in enumerate(grp):
                            P.op(ACT, lambda e, bk3=bk3, gi=gi, h0=h0, nr=nr: e.activation(
                                out=sq[:, h0 + gi, :], in_=psum[:, bk3, gi * 129:gi * 129 + 128], func=AF.Square,
                                accum_out=nr[:, 8 + h0 + gi:9 + h0 + gi]), reads=[bankB[bk3]], writes=[sqB, nrB])
                kt, ktB = KT[s]
                sbanks = []
                for j in range(4):
                    bk5 = bank()
                    mm(psum[:, bk5, 0:258], kt[:, j * 128:(j + 1) * 128],
                       uvt[:, 2 * j:2 * j + 2, 0:129], True, True, [ktB, uvB], [bankB[bk5]])
                    sbanks.append(bk5)
                for j, bk5 in enumerate(sbanks):
                    tt(DVE, C[0:64, j, :], C[0:64, j, :], psum[0:64, bk5, 0:129], ALU.add, [CB[2 * j], bankB[bk5]], [CB[2 * j]])
                    tt(DVE, C[64:128, j, :], C[64:128, j, :], psum[64:128, bk5, 129:258], ALU.add,
                       [CB[2 * j + 1], bankB[bk5]], [CB[2 * j + 1]])
                if s + 1 < NS:
                    decay_cb(s + 1)
                if not state_only:
                    nr2, nr2B = R["nrm2"].next()
                    tt(DVE, nr2[:, 16:24], nr[:, 0:8], u_[:, 8:16], ALU.max, [nrB, uB], [nr2B])
                    tt(DVE, nr2[:, 24:32], nr2[:, 16:24], nr2[:, 16:24], ALU.mult, [nr2B], [nr2B])
                    ts(DVE, nr2[:, 24:32], nr2[:, 24:32], HN_EPS, None, ALU.mult, None, [nr2B], [nr2B])
                    stt(nr2[:, 24:32], nr[:, 8:16], 1.0 / 128.0, nr2[:, 24:32], ALU.mult, ALU.add, [nrB, nr2B], [nr2B])
                    act(nr2[:, 32:40], nr2[:, 24:32], AF.Ln, [nr2B], [nr2B])
                    act(nr2[:, 40:48], nr2[:, 32:40], AF.Exp, [nr2B], [nr2B], scale=-0.5)
                    for bk3, grp, h0 in obanks:
                        for gi, h in enumerate(grp):
                            act(hnt[:, h * 128:(h + 1) * 128], psum[:, bk3, gi * 129:gi * 129 + 128], AF.Identity,
                                [bankB[bk3], nr2B], [hnB], scale=nr2[:, 40 + h0 + gi:41 + h0 + gi])
                    tt(POOL, hnt[:], hnt[:], ogt[:], ALU.mult, [hnB, ogB], [hnB])
                    if s >= 1:
                        transposes(s - 1)
            if not state_only:
                transposes(NS - 1)
            if state_only:
                return
            out_proj_ln(wouta_b[l], "wouta_b", l * 2, TT)

        def swa_layer(j, subs, TT, first_prompt_tile, halo_keep=False, kv_only=False):
            NS = len(subs)
            if j == 0:
                release_ln()
            sample = subs[0]["kind"] == "s"
            kT_ = kTd_s if sample else kTd
            kTB = kTdsB if sample else kTdB
            VA = Vaug_s if sample else Vaug
            VAB = VaugsB if sample else VaugB

            def blocks(s):
                return (2 * s, 2 * s + 1) if sample else (s, s + 1)
            if j == 0:
                if sample:
                    for s, sub in enumerate(subs):
                        i = sub["si"]
                        pb_, ob_ = blocks(s)
                        ct, ctB = R["io"].next()
                        P.dma(SP, lambda e, ct=ct, i=i: e.dma_start(out=ct[:, 0:128], in_=ck_in[i]), writes=[ctB])
                        P.dma(SP, lambda e, ct=ct, i=i: e.dma_start(out=ct[:, 128:256], in_=cv_in[i]), writes=[ctB])
                        kd, kdB = R["b512"].next()
                        for kv in range(2):
                            for dup in range(2):
                                cp(DVE, kd[:, kv * 128 + dup * 64:kv * 128 + dup * 64 + 64],
                                   ct[:, kv * 64:(kv + 1) * 64], [ctB], [kdB])
                        cp(DVE, VA[:, pb_, :, 0:64], ct[:, 128:256].rearrange("p (k d) -> p k d", k=2), [ctB], [VAB])
                        bk = bank()
                        pb = psum[:, bk, :].bitcast(BF16)
                        for kv in range(2):
                            P.op(PE, lambda e, pb=pb, kv=kv, kd=kd: e.transpose(
                                out=pb[:, kv * 128:(kv + 1) * 128], in_=kd[:, kv * 128:(kv + 1) * 128],
                                identity=identb[:]), reads=[kdB, constB], writes=[bankB[bk]])
                        cp(DVE, kT_[:, :, pb_, :], pb[:, 0:256].rearrange("p (k n) -> p k n", k=2),
                           [bankB[bk]], [kTB])
                elif not first_prompt_tile:
                    cp(DVE, kT_[:, :, 0, :], kT_[:, :, 4, :], [kTB], [kTB])
                    cp(DVE, VA[:, 0], VA[:, 4], [VAB], [VAB])
                    if halo_keep:
                        ts(DVE, VA[:, 0].rearrange("p k n -> p (k n)"), VA[:, 0].rearrange("p k n -> p (k n)"),
                           keep[:, 0:1], None, ALU.mult, None, [VAB, constB], [VAB])
                wv, wB = wload(wkv_b, v8(256), "wkv_b")
                for s, sub in enumerate(subs):
                    pb_, ob_ = blocks(s)
                    bk = bank()
                    for c in range(8):
                        mm(psum[:, bk, 0:256], xT[:, c, s * 128:(s + 1) * 128], wv[:, c, :], c == 0, c == 7,
                           [wB] + xTB, [bankB[bk]])
                    act(VA[:, ob_, :, 0:64], psum[:, bk, 128:256].rearrange("p (k d) -> p k d", k=2), AF.Copy,
                        [bankB[bk]], [VAB])
                    if sample:
                        P.op(POOL, lambda e, ob_=ob_: e.memset(VA[32:64, ob_], 0.0), writes=[VAB])
                        P.op(POOL, lambda e, ob_=ob_: e.memset(VA[64:128, ob_], 0.0), writes=[VAB])
                    if sample or sub.get("last"):
                        cp(DVE, kvtok[:], psum[:, bk, 0:256], [bankB[bk], kvtokB], [kvtokB])
                        if sample:
                            i = sub["si"]
                            P.dma(ACT, lambda e, i=i: e.dma_start(out=sk_out[i], in_=kvtok[0:32, 0:128]),
                                  reads=[kvtokB], final=True)
                            P.dma(ACT, lambda e, i=i: e.dma_start(out=sv_out[i], in_=kvtok[0:32, 128:256]),
                                  reads=[kvtokB], final=True)
                        else:
                            P.dma(ACT, lambda e: e.dma_start(out=pk_out, in_=kvtok[:, 0:128]),
                                  reads=[kvtokB], final=True)
                            P.dma(ACT, lambda e: e.dma_start(out=pv_out, in_=kvtok[:, 128:256]),
                                  reads=[kvtokB], final=True)
                wv, wB = wload(wkd_b, v8(256), "wkd_b")
                for kv in range(2):
                    bk = feat_mm_group(wv, wB, kv * 128, xT, xTB, TT)
                    if sample:
                        for s in range(NS):
                            act(kT_[:, kv, 2 * s + 1, :], psum[:, bk, s * 128:(s + 1) * 128], AF.Copy,
                                [bankB[bk]], [kTB])
                    else:
                        act(kT_[:, kv, 1:5, :], psum[:, bk, :].rearrange("p (b n) -> p b n", b=4), AF.Copy,
                            [bankB[bk]], [kTB])
            if kv_only:
                return
            wq2 = [wload(wq_b[j, i], v8(512), "wq_b") for i in range(2)]

            def q_mm(m):
                wv, wB = wq2[m // 4]

                def fn(c, bk):
                    mm(psum[:, bk, 0:TT], wv[:, c, (m % 4) * 128:(m % 4) * 128 + 128], xT[:, c, 0:TT],
                       c == 0, c == 7, [wB, xTB[c]], [bankB[bk]])
                return fn
            nco = 6 if j == 1 else 0
            if nco:
                bks = couter([q_mm(m) for m in range(nco)])
                for m in range(nco):
                    act(qkT[:, m, 0:TT], psum[:, bks[m], 0:TT], AF.Copy, [bankB[bks[m]]], [qkB[m]], scale=0.125)
            for m in range(nco, 8):
                wv, wB = wq2[m // 4]
                bk = feat_mm_group(wv, wB, (m % 4) * 128, xT, xTB, TT)
                act(qkT[:, m, 0:TT], psum[:, bk, 0:TT], AF.Copy, [bankB[bk]], [qkB[m]], scale=0.125)
            def scores(s):
                blk = blocks(s)
                PTs = {}
                for kb in range(2):
                    for kv in range(2):
                        for p in range(2):
                            bk = bank()
                            mm(psum[:, bk, :], kT_[p * 64:(p + 1) * 64, kv, blk[kb], :],
                               qkT[p * 64:(p + 1) * 64, kv * 4:(kv + 1) * 4, s * 128:(s + 1) * 128],
                               True, True, [kTB] + qkB[kv * 4:(kv + 1) * 4], [bankB[bk]])
                            tmp, tmpB = R["f512"].next()
                            tt(DVE, tmp[:], psum[:, bk, :], biasT[:, kb * 4 + kv * 2 + p, :], ALU.add,
                               [bankB[bk], biasB], [tmpB])
                            pt, ptB = R["PT"].next()
                            act(pt[:], tmp[:], AF.Exp, [tmpB], [ptB])
                            PTs[(kb, kv, p)] = (pt, ptB)
                return PTs

            def pv_norm(s, PTs):
                blk = blocks(s)
                ont, onB = R["hn"].next()
                nr, nrB = R["nrm"].next()
                for grp in (tuple(range(0, 7)), tuple(range(7, 14)), (14, 15)):
                    bk = bank()
                    for gi, head in enumerate(grp):
                        kv, g_ = head // 8, head % 8
                        hh, p = g_ // 2, g_ % 2
                        for kb in range(2):
                            pt, ptB = PTs[(kb, kv, p)]
                            mm(psum[:, bk, gi * 65:(gi + 1) * 65], pt[:, hh * 128:(hh + 1) * 128],
                               VA[:, blk[kb], kv, 0:65], kb == 0, kb == 1, [ptB, VAB], [bankB[bk]])
                    ng = len(grp)
                    h0 = grp[0]
                    pv = psum[:, bk, 0:ng * 65].rearrange("p (h n) -> p h n", h=ng)
                    tt(DVE, nr[:, h0:h0 + ng].unsqueeze(2), pv[:, :, 64:65], esink[:, j, h0:h0 + ng].unsqueeze(2),
                       ALU.add, [bankB[bk], constB], [nrB])
                    P.op(DVE, lambda e, nr=nr, h0=h0, ng=ng: e.reciprocal(out=nr[:, 16 + h0:16 + h0 + ng],
                                                                          in_=nr[:, h0:h0 + ng]),
                         reads=[nrB], writes=[nrB])
                    if BCAST:
                        tt(DVE, ont[:, h0 * 64:(h0 + ng) * 64].rearrange("p (h n) -> p h n", h=ng), pv[:, :, 0:64],
                           nr[:, 16 + h0:16 + h0 + ng].unsqueeze(2).to_broadcast([128, ng, 64]), ALU.mult,
                           [bankB[bk], nrB], [onB])
                    else:
                        for gi, head in enumerate(grp):
                            if head % 2 == 0:
                                ts(DVE, ont[:, head * 64:(head + 1) * 64], psum[:, bk, gi * 65:gi * 65 + 64],
                                   nr[:, 16 + head:17 + head], None, ALU.mult, None, [bankB[bk], nrB], [onB])
                            else:
                                act(ont[:, head * 64:(head + 1) * 64], psum[:, bk, gi * 65:gi * 65 + 64],
                                    AF.Identity, [bankB[bk], nrB], [onB], scale=nr[:, 16 + head:17 + head])
                return ont, onB

            def attn_transposes(s, ont, onB):
                bk4 = bank()
                pb = psum[:, bk4, :].bitcast(BF16)
                for c in range(8):
                    P.op(PE, lambda e, pb=pb, c=c, ont=ont: e.transpose(
                        out=pb[:, c * 128:(c + 1) * 128], in_=ont[:, c * 128:(c + 1) * 128], identity=identb[:]),
                        reads=[onB, constB], writes=[bankB[bk4]])
                act(hT[:, :, s * 128:(s + 1) * 128], pb.rearrange("p (c n) -> p c n", c=8), AF.Copy,
                    [bankB[bk4]], [hTB[s]])

            PTn = scores(0)
            for s in range(NS):
                ont, onB = pv_norm(s, PTn)
                if s + 1 < NS:
                    PTn = scores(s + 1)
                attn_transposes(s, ont, onB)
            out_proj_ln(woutb_b[j], "woutb_b", (2 + j) * 2, TT)

        tiles = []
        for t in range(NT_P):
            subs = [dict(kind="p", st=0, row0=t * 512 + s * 128, yrow0=(t - NPRE) * 512 + s * 128,
                         discard=(t < NPRE)) for s in range(4)]
            if t == NT_P - 1:
                subs[-1]["last"] = True
            tiles.append(subs)
        if n_samp and not _os.environ.get("K_NOSAMP"):
            tiles.append([dict(kind="s", st=1 + i, si=i) for i in range(n_samp)])

        STOP = int(_os.environ.get("K_STOP", "99"))
        for ti, subs in enumerate(tiles):
            TT = 128 * len(subs)
            state_mode = subs[0]["kind"] == "p" and ti < NPRE - 1
            load_tile_x(subs, TT)
            spread = NPRE >= 5
            if ti == 0:
                conv_layer_a(1)
            mlstm_layer(0, subs, TT)
            if (ti == 1 and spread) or (ti == 0 and not spread):
                conv_ffn(1)
            ffn(0, TT)
            if (ti == 1 and spread) or (ti == 0 and not spread):
                conv_layer_b(0)
            mlstm_layer(1, subs, TT, state_only=state_mode)
            if (ti == 2 and spread) or (ti == 0 and not spread):
                conv_ffn(2)
            if (ti == 3 and spread) or (ti == 0 and not spread):
                conv_layer_b(1)
                conv_ffn(3)
            halo_tile = subs[0]["kind"] == "p" and NPRE > 0 and ti == NPRE - 1
            if halo_tile:
                ffn(1, TT)
                swa_layer(0, subs, TT, True, kv_only=True)
                release_ln()
            elif not state_mode:
                ffn(1, TT)
                for j in range(2):
                    swa_layer(j, subs, TT, ti == max(NPRE - 1, 0), halo_keep=(NPRE > 0 and ti == NPRE))
                    ffn(2 + j, TT)
                store_tile_y(subs, TT)
            if NPRE and ti == NPRE - 1:
                for l in range(2):
                    for jj in range(4):
                        ts(DVE, Cst[l][0][:, jj, :], Cst[l][0][:, jj, :], keep[:, 0:1], None, ALU.mult, None,
                           CstB[l][0] + [constB], CstB[l][0])
                    ts(DVE, mst[l][0][:], mst[l][0][:], keep[0:8, 0:1], None, ALU.mult, None,
                       [mstB[l][0], constB], [mstB[l][0]])
            last_prompt = (ti == NT_P - 1)
            if last_prompt or subs[0]["kind"] == "s":
                for l in range(2):
                    for sub in (subs if subs[0]["kind"] == "s" else subs[:1]):
                        si = sub["st"]
                        if sub["kind"] == "p":
                            dC, dn, dm = pC_out[l], pn_out[l], pm_out[l]
                        else:
                            dC, dn, dm = sC_out[l, sub["si"]], sn_out[l, sub["si"]], sm_out[l, sub["si"]]
                        P.dma(ACT, lambda e, dC=dC, l=l, si=si: e.dma_start(
                            out=dC.rearrange("j q e -> q j e"), in_=Cst[l][si][:, :, 0:128]),
                            reads=CstB[l][si], final=True)
                        P.dma(ACT, lambda e, dn=dn, l=l, si=si: e.dma_start(
                            out=dn.rearrange("j (q o) -> q j o", o=1), in_=Cst[l][si][:, :, 128:129],
                            allow_slow_non_contiguous=True), reads=CstB[l][si], final=True)
                        P.dma(ACT, lambda e, dm=dm, l=l, si=si: e.dma_start(
                            out=dm.rearrange("(h o) -> h o", o=1), in_=mst[l][si][:],
                            allow_slow_non_contiguous=True), reads=[mstB[l][si]], final=True)

        P.emit(block, sems, dma_sems)
    return nc


def make_in_maps(inputs, NT_P, n_cores=8, n_samp=2, NPRE=0):
    c = _consts()
    f = lambda a: np.ascontiguousarray(np.asarray(a, dtype=np.float32))
    xp = f(inputs["x_prompt"])
    xs = f(inputs["x_sample"])
    sC, sn, sm = f(inputs["state_C"]), f(inputs["state_n"]), f(inputs["state_m"])
    ck, cv = f(inputs["cache_k"]), f(inputs["cache_v"])
    shared = {k: f(inputs[k]) for k in ["w_in_a", "w_out_a", "w_kv", "w_q_b", "w_out_b", "w_gu", "w_down"]}
    shared["bgate"] = f(np.broadcast_to(inputs["b_gate_a"][:, None, :], (2, 128, 16)))
    shared["gnT"] = f(np.asarray(inputs["g_norm_a"]).reshape(2, 8, 128).transpose(0, 2, 1))
    shared["sinksrep"] = f(np.broadcast_to(inputs["sinks_b"][:, None, :], (2, 128, 16)))
    shared["lng"] = f(np.asarray(inputs["ln_g"]).reshape(4, 2, 8, 128).transpose(0, 1, 3, 2))
    shared["lnb"] = f(np.asarray(inputs["ln_b"]).reshape(4, 2, 8, 128).transpose(0, 1, 3, 2))
    for k, v in c.items():
        shared["c_" + k] = v
    maps = []
    nb = xp.shape[0]
    NOWN = (NT_P - NPRE) * 512
    for core in range(n_cores):
        m = dict(shared)
        if NPRE:
            b, half = core // 2, core % 2
            if half == 1:
                m["xin"] = f(xp[b, :NT_P * 512])
            else:
                m["xin"] = f(np.concatenate([np.zeros((NPRE * 512, D), np.float32), xp[b, :NOWN]], 0))
            m["keep"] = np.full((128, 1), float(half), np.float32)
        else:
            b = core % nb
            m["xin"] = f(xp[b, :NT_P * 512])
            m["keep"] = np.ones((128, 1), np.float32)
        sl = slice(core * n_samp, (core + 1) * n_samp)
        m["xs"] = f(xs[sl])
        m["stC"] = f(sC[:, sl].reshape(2, n_samp, 4, 128, 128))
        m["stn"] = f(sn[:, sl].reshape(2, n_samp, 4, 128))
        m["stm"] = f(sm[:, sl])
        m["ck"] = f(ck[sl].reshape(n_samp, 128, 128))
        m["cv"] = f(cv[sl].reshape(n_samp, 128, 128))
        maps.append(m)
    return maps


def assemble(results, NT_P, nb=4, n_cores=8, n_samp=2, NPRE=0):
    if NPRE:
        y = np.stack([np.concatenate([results[2 * b]["y"], results[2 * b + 1]["y"]], 0) for b in range(nb)], 0)
        pc = [2 * b + 1 for b in range(nb)]
    else:
        y = np.stack([results[b]["y"] for b in range(nb)], 0)
        pc = list(range(nb))
    ys = np.concatenate([results[c]["ys"] for c in range(n_cores)], 0)
    pC = np.stack([results[c]["pC"].reshape(2, 8, 64, 128) for c in pc], 1)
    pn = np.stack([results[c]["pn"].reshape(2, 8, 64) for c in pc], 1)
    pm = np.stack([results[c]["pm"] for c in pc], 1)
    pk = np.stack([results[c]["pk"].reshape(128, 2, 64) for c in pc], 0)
    pv = np.stack([results[c]["pv"].reshape(128, 2, 64) for c in pc], 0)
    sC = np.concatenate([results[c]["sC"].reshape(2, n_samp, 8, 64, 128) for c in range(n_cores)], 1)
    sn = np.concatenate([results[c]["sn"].reshape(2, n_samp, 8, 64) for c in range(n_cores)], 1)
    sm = np.concatenate([results[c]["sm"] for c in range(n_cores)], 1)
    sk = np.concatenate([results[c]["sk"].reshape(n_samp, 32, 2, 64) for c in range(n_cores)], 0)
    sv = np.concatenate([results[c]["sv"].reshape(n_samp, 32, 2, 64) for c in range(n_cores)], 0)
    outs = (y, ys, pC, pn, pm, pk, pv, sC, sn, sm, sk, sv)
    return tuple(np.ascontiguousarray(o, dtype=np.float32) for o in outs)


def kernel(**inputs):
    NT_P, NPRE = 16, 8
    nc = build(NT_P, NPRE=NPRE)
    maps = make_in_maps(inputs, NT_P, NPRE=NPRE)
    res = run_bass_kernel_spmd(nc, maps, core_ids=list(range(8)))
    return assemble(res.results, NT_P, NPRE=NPRE)
```
